# Optimizing a Trainium2 kernel written in Bass

```python
import jax, jax.numpy as jnp
from jax import lax
import numpy as np

D_MODEL = 2048
BATCH = 2
SEQ = 4096
DEPTH = 1

MEM_LEN = 256
RMS_EPS = 1e-6

MLSTM_HEADS = 4
MLSTM_WIDTH = D_MODEL // 2
MLSTM_V_DIM = MLSTM_WIDTH // MLSTM_HEADS
MLSTM_QK_DIM = MLSTM_V_DIM // 2
MLSTM_QK_WIDTH = MLSTM_HEADS * MLSTM_QK_DIM
MLSTM_CHUNK = 64
GATE_SOFTCAP = 15.0

RWKV_WIDTH = D_MODEL // 2
RWKV_HEAD = 64
RWKV_HEADS = RWKV_WIDTH // RWKV_HEAD
RWKV_DECAY_RANK = 64
RWKV_A_RANK = 64
RWKV_GATE_RANK = 160
RWKV_GN_EPS = 64e-5

MLSTM_COLS = (MLSTM_QK_WIDTH, MLSTM_QK_WIDTH, MLSTM_WIDTH, MLSTM_WIDTH, MLSTM_HEADS, MLSTM_HEADS)
RWKV_COLS = (RWKV_WIDTH, RWKV_WIDTH, RWKV_WIDTH, RWKV_DECAY_RANK, RWKV_A_RANK, RWKV_GATE_RANK)
MLSTM_TOTAL = sum(MLSTM_COLS)
RWKV_TOTAL = sum(RWKV_COLS)
GATE_TOTAL = 2 * D_MODEL
IN_COLS = MLSTM_TOTAL + RWKV_TOTAL + GATE_TOTAL

XATTN_HEADS = 4
XATTN_HEAD_DIM = 128
XATTN_WIDTH = XATTN_HEADS * XATTN_HEAD_DIM

D_FF = 4 * D_MODEL
CONV_WIDTH = 3

kernel_name = "hybrid_mlstm_rwkv7_gated_xattn_convffn"


def rms_norm(x, g):
    xf = x.astype(jnp.float32)
    y = xf * lax.rsqrt(jnp.mean(xf * xf, axis=-1, keepdims=True) + RMS_EPS)
    return (y * g).astype(x.dtype)


def split_cols(t, sizes):
    idx = [int(i) for i in np.cumsum(sizes)[:-1]]
    return jnp.split(t, idx, axis=-1)


def shift_right(t, n):
    return jnp.pad(t, ((0, 0), (n, 0), (0, 0)))[:, : t.shape[1]]


def softcap(t, cap):
    return cap * jnp.tanh(t / cap)


def mlstm_chunkwise(q, k, v, ig, logf):
    B, S, H, dk = q.shape
    dv = v.shape[-1]
    L = MLSTM_CHUNK
    NC = S // L

    def to_chunks(t):
        return t.reshape(B, NC, L, H, t.shape[-1]).transpose(1, 0, 3, 2, 4)

    def gate_chunks(t):
        return t.reshape(B, NC, L, H).transpose(1, 0, 3, 2)

    causal = jnp.tril(jnp.ones((L, L), dtype=bool))

    def body(carry, inp):
        C, n, m = carry
        qc, kc, vc, ic, fc = inp
        b = jnp.cumsum(fc, axis=-1)
        dmat = jnp.where(causal, b[..., :, None] - b[..., None, :] + ic[..., None, :], -jnp.inf)
        inter = b + m[..., None]
        m_t = jnp.maximum(inter, jnp.max(dmat, axis=-1))
        dexp = jnp.exp(dmat - m_t[..., None])
        w_inter = jnp.exp(inter - m_t)
        s = jnp.einsum('bhtd,bhsd->bhts', qc, kc) * dexp
        num = w_inter[..., None] * jnp.einsum('bhvd,bhtd->bhtv', C, qc) + jnp.einsum('bhts,bhsv->bhtv', s, vc)
        den = w_inter * jnp.einsum('bhd,bhtd->bht', n, qc) + jnp.sum(s, axis=-1)
        h = num / jnp.maximum(jnp.abs(den), jnp.exp(-m_t))[..., None]
        bL = b[..., -1]
        gs = bL[..., None] - b + ic
        m_new = jnp.maximum(bL + m, jnp.max(gs, axis=-1))
        carry_w = jnp.exp(bL + m - m_new)
        ws = jnp.exp(gs - m_new[..., None])
        C = carry_w[..., None, None] * C + jnp.einsum('bhs,bhsv,bhsd->bhvd', ws, vc, kc)
        n = carry_w[..., None] * n + jnp.einsum('bhs,bhsd->bhd', ws, kc)
        return (C, n, m_new), h

    init = (jnp.zeros((B, H, dv, dk), jnp.float32), jnp.zeros((B, H, dk), jnp.float32), jnp.zeros((B, H), jnp.float32))
    _, h = lax.scan(body, init, (to_chunks(q), to_chunks(k), to_chunks(v), gate_chunks(ig), gate_chunks(logf)))
    return h.transpose(1, 0, 3, 2, 4).reshape(B, S, H, dv)


def rwkv7_scan(r, w, k, v, kk, kka):
    B, S, H, N = r.shape

    def step(state, inp):
        rt, wt, kt, vt, kkt, kat = inp
        state = (state * wt[:, :, None, :]
                 - jnp.einsum('bhvk,bhk->bhv', state, kkt)[..., None] * kat[:, :, None, :]
                 + vt[..., None] * kt[:, :, None, :])
        return state, jnp.einsum('bhvk,bhk->bhv', state, rt)

    seq_major = lambda t: t.transpose(1, 0, 2, 3)
    init = jnp.zeros((B, H, N, N), jnp.float32)
    _, y = lax.scan(step, init, tuple(seq_major(t) for t in (r, w, k, v, kk, kka)))
    return y.transpose(1, 0, 2, 3)


def hybrid_mixer(h, w_in, b_i, b_f, head_norm, mu, w0, w_up, a0, a_up, g_up, k_k, k_a, r_k, ln_g, ln_b,
                 w_branch_a, w_branch_b, w_out):
    B, S, _ = h.shape
    f32 = jnp.float32
    proj = h @ w_in
    mlstm_p, rwkv_p, gate_p = split_cols(proj, (MLSTM_TOTAL, RWKV_TOTAL, GATE_TOTAL))

    q, k, v, o, ir, fr = split_cols(mlstm_p, MLSTM_COLS)
    q = q.reshape(B, S, MLSTM_HEADS, MLSTM_QK_DIM).astype(f32)
    k = k.reshape(B, S, MLSTM_HEADS, MLSTM_QK_DIM).astype(f32) * (MLSTM_QK_DIM ** -0.5)
    v = v.reshape(B, S, MLSTM_HEADS, MLSTM_V_DIM).astype(f32)
    ig = softcap((ir + b_i).astype(f32), GATE_SOFTCAP)
    logf = jax.nn.log_sigmoid(softcap((fr + b_f).astype(f32), GATE_SOFTCAP))
    hm = mlstm_chunkwise(q, k, v, ig, logf)
    hm = hm * lax.rsqrt(jnp.mean(hm * hm, axis=-1, keepdims=True) + RMS_EPS) * head_norm.reshape(MLSTM_HEADS, MLSTM_V_DIM)
    ha = (jax.nn.sigmoid(o.astype(f32)) * hm.reshape(B, S, MLSTM_WIDTH)).astype(h.dtype)

    rp = rwkv_p + (shift_right(rwkv_p, 1) - rwkv_p) * mu
    r, kr, vr, wl, al, gl = split_cols(rp, RWKV_COLS)
    w_log = -jax.nn.softplus(-(w0 + jnp.tanh(wl) @ w_up)) - 0.5
    decay = jnp.exp(-jnp.exp(w_log.astype(f32)))
    a = jax.nn.sigmoid(a0 + al @ a_up)
    g = jax.nn.sigmoid(gl) @ g_up
    heads = lambda t: t.reshape(B, S, RWKV_HEADS, RWKV_HEAD).astype(f32)
    per_head = lambda p: p.reshape(RWKV_HEADS, RWKV_HEAD).astype(f32)
    kk = heads(kr * k_k)
    kk = kk / jnp.maximum(jnp.linalg.norm(kk, axis=-1, keepdims=True), 1e-12)
    a_h = heads(a)
    kr = heads(kr * (1.0 + (a - 1.0) * k_a))
    r_h, v_h = heads(r), heads(vr)
    y = rwkv7_scan(r_h, heads(decay), kr, v_h, kk, kk * a_h)
    mean = jnp.mean(y, axis=-1, keepdims=True)
    var = jnp.mean(jnp.square(y - mean), axis=-1, keepdims=True)
    y = (y - mean) * lax.rsqrt(var + RWKV_GN_EPS) * per_head(ln_g) + per_head(ln_b)
    y = y + jnp.sum(r_h * kr * r_k.astype(f32), axis=-1, keepdims=True) * v_h
    hb = (y.reshape(B, S, RWKV_WIDTH) * g).astype(h.dtype)

    g_a, g_b = split_cols(gate_p, (D_MODEL, D_MODEL))
    merged = jax.nn.sigmoid(g_a) * (ha @ w_branch_a) + jax.nn.sigmoid(g_b) * (hb @ w_branch_b)
    return merged @ w_out


def cross_attention(h, mem_n, wq, wkv, wo):
    B, S, _ = h.shape
    M = mem_n.shape[1]
    q = (h @ wq).reshape(B, S, XATTN_HEADS, XATTN_HEAD_DIM)
    k, v = split_cols(mem_n @ wkv, (XATTN_WIDTH, XATTN_WIDTH))
    k = k.reshape(B, M, XATTN_HEADS, XATTN_HEAD_DIM)
    v = v.reshape(B, M, XATTN_HEADS, XATTN_HEAD_DIM)
    scores = jnp.einsum('bshd,bmhd->bhsm', q, k).astype(jnp.float32) * (XATTN_HEAD_DIM ** -0.5)
    p = jax.nn.softmax(scores, axis=-1).astype(v.dtype)
    o = jnp.einsum('bhsm,bmhd->bshd', p, v).reshape(B, S, XATTN_WIDTH)
    return o @ wo


def conv_glu_ffn(h, w_up, conv_w, conv_b, w_down):
    u = h @ w_up
    uc = conv_b + sum(conv_w[j] * shift_right(u, CONV_WIDTH - 1 - j) for j in range(CONV_WIDTH))
    gate, up = split_cols(uc, (D_FF, D_FF))
    return (jax.nn.gelu(gate, approximate=True) * up) @ w_down


def setup_inputs(seed: int = 0) -> dict:
    key = jax.random.key(seed)
    ks = iter(jax.random.split(key, 40))
    nrm = lambda shape, scale: scale * jax.random.normal(next(ks), shape, jnp.float32)
    gain = lambda n: 1.0 + nrm((DEPTH, n), 0.05)
    Dp = DEPTH
    conv_w = nrm((Dp, CONV_WIDTH, 2 * D_FF), 0.2).at[:, CONV_WIDTH - 1].add(1.0)
    return {
        "x": nrm((BATCH, SEQ, D_MODEL), 1.0),
        "mem": nrm((BATCH, MEM_LEN, D_MODEL), 1.0),
        "mix_pre_norm": gain(D_MODEL),
        "w_in": nrm((Dp, D_MODEL, IN_COLS), D_MODEL ** -0.5),
        "mlstm_b_i": nrm((Dp, MLSTM_HEADS), 0.1),
        "mlstm_b_f": jnp.linspace(3.0, 6.0, MLSTM_HEADS)[None] + nrm((Dp, MLSTM_HEADS), 0.1),
        "mlstm_head_norm": gain(MLSTM_WIDTH),
        "rwkv_mu": jax.random.uniform(next(ks), (Dp, RWKV_TOTAL), jnp.float32),
        "rwkv_w0": (-6.5 + 5.0 * jnp.linspace(0.0, 1.0, RWKV_WIDTH) ** 0.85)[None] + nrm((Dp, RWKV_WIDTH), 0.1),
        "rwkv_w_up": nrm((Dp, RWKV_DECAY_RANK, RWKV_WIDTH), 0.5 * RWKV_DECAY_RANK ** -0.5),
        "rwkv_a0": nrm((Dp, RWKV_WIDTH), 0.1),
        "rwkv_a_up": nrm((Dp, RWKV_A_RANK, RWKV_WIDTH), RWKV_A_RANK ** -0.5),
        "rwkv_g_up": nrm((Dp, RWKV_GATE_RANK, RWKV_WIDTH), RWKV_GATE_RANK ** -0.5),
        "rwkv_k_k": 0.85 + nrm((Dp, RWKV_WIDTH), 0.05),
        "rwkv_k_a": 1.0 + nrm((Dp, RWKV_WIDTH), 0.05),
        "rwkv_r_k": nrm((Dp, RWKV_HEADS, RWKV_HEAD), 0.1),
        "rwkv_ln_g": gain(RWKV_WIDTH),
        "rwkv_ln_b": nrm((Dp, RWKV_WIDTH), 0.02),
        "w_branch_a": nrm((Dp, MLSTM_WIDTH, D_MODEL), MLSTM_WIDTH ** -0.5),
        "w_branch_b": nrm((Dp, RWKV_WIDTH, D_MODEL), RWKV_WIDTH ** -0.5),
        "w_mix_out": nrm((Dp, D_MODEL, D_MODEL), D_MODEL ** -0.5),
        "mix_post_norm": gain(D_MODEL),
        "xattn_pre_norm": gain(D_MODEL),
        "mem_norm": gain(D_MODEL),
        "xattn_wq": nrm((Dp, D_MODEL, XATTN_WIDTH), D_MODEL ** -0.5),
        "xattn_wkv": nrm((Dp, D_MODEL, 2 * XATTN_WIDTH), D_MODEL ** -0.5),
        "xattn_wo": nrm((Dp, XATTN_WIDTH, D_MODEL), XATTN_WIDTH ** -0.5),
        "xattn_post_norm": gain(D_MODEL),
        "ffn_pre_norm": gain(D_MODEL),
        "ffn_w_up": nrm((Dp, D_MODEL, 2 * D_FF), D_MODEL ** -0.5),
        "ffn_conv_w": conv_w,
        "ffn_conv_b": nrm((Dp, 2 * D_FF), 0.02),
        "ffn_w_down": nrm((Dp, D_FF, D_MODEL), D_FF ** -0.5),
        "ffn_post_norm": gain(D_MODEL),
    }


def reference(x, mem, mix_pre_norm, w_in, mlstm_b_i, mlstm_b_f, mlstm_head_norm, rwkv_mu, rwkv_w0, rwkv_w_up,
              rwkv_a0, rwkv_a_up, rwkv_g_up, rwkv_k_k, rwkv_k_a, rwkv_r_k, rwkv_ln_g, rwkv_ln_b, w_branch_a,
              w_branch_b, w_mix_out, mix_post_norm, xattn_pre_norm, mem_norm, xattn_wq, xattn_wkv, xattn_wo,
              xattn_post_norm, ffn_pre_norm, ffn_w_up, ffn_conv_w, ffn_conv_b, ffn_w_down, ffn_post_norm):
    for l in range(DEPTH):
        h = rms_norm(x, mix_pre_norm[l])
        y = hybrid_mixer(h, w_in[l], mlstm_b_i[l], mlstm_b_f[l], mlstm_head_norm[l], rwkv_mu[l], rwkv_w0[l],
                         rwkv_w_up[l], rwkv_a0[l], rwkv_a_up[l], rwkv_g_up[l], rwkv_k_k[l], rwkv_k_a[l],
                         rwkv_r_k[l], rwkv_ln_g[l], rwkv_ln_b[l], w_branch_a[l], w_branch_b[l], w_mix_out[l])
        x = x + rms_norm(y, mix_post_norm[l])
        h = rms_norm(x, xattn_pre_norm[l])
        m = rms_norm(mem, mem_norm[l])
        x = x + rms_norm(cross_attention(h, m, xattn_wq[l], xattn_wkv[l], xattn_wo[l]), xattn_post_norm[l])
        h = rms_norm(x, ffn_pre_norm[l])
        x = x + rms_norm(conv_glu_ffn(h, ffn_w_up[l], ffn_conv_w[l], ffn_conv_b[l], ffn_w_down[l]), ffn_post_norm[l])
    return x
```

```python
import numpy as np
import contextlib
import concourse.bass as bass
import concourse.mybir as mybir
from concourse.bass_utils import run_bass_kernel_spmd

F32 = mybir.dt.float32
BF16 = mybir.dt.bfloat16
AF = mybir.ActivationFunctionType
ALU = mybir.AluOpType
AX = mybir.AxisListType

D = 2048
KC = 16
SEQ = 4096
BATCH = 2
MEM = 256
IN_COLS = 10536
DFF = 8192
EPS = 1e-6
GN_EPS = 64e-5
O_Q, O_K, O_V, O_O, O_I, O_F = 0, 512, 1024, 2048, 3072, 3076
O_RW = 3080
O_RR, O_RK, O_RV, O_RWL, O_RAL, O_RGL = O_RW, O_RW + 1024, O_RW + 2048, O_RW + 3072, O_RW + 3136, O_RW + 3200
O_GA = O_RW + 3360
O_GB = O_GA + 2048


class V:
    __slots__ = ('ap', 'key')

    def __init__(self, ap, key):
        self.ap = ap
        self.key = key

    def __getitem__(self, idx):
        return V(self.ap[idx], self.key)

    def rearrange(self, *a, **k):
        return V(self.ap.rearrange(*a, **k), self.key)

    def bitcast(self, dt):
        return V(self.ap.bitcast(dt), self.key)

    def to_broadcast(self, shape):
        return V(self.ap.to_broadcast(shape), self.key)

    def k(self, sub):
        return V(self.ap, (self.key, sub))


def U(x):
    return x.ap if isinstance(x, V) else x


class Sched:
    ENGS = ['pe', 'act', 'dve', 'pool', 'sp']
    SAME_ENG_SYNC = {'act', 'dve', 'pool'}

    def __init__(self, nc):
        self.nc = nc
        self.ops = {e: [] for e in self.ENGS}
        self.last_real = {e: None for e in self.ENGS}
        self.last_w = {}
        self.reads = {}
        self.dma_cnt = {}
        self.stack = contextlib.ExitStack()

    @staticmethod
    def _key(x):
        if isinstance(x, V):
            return x.key
        if isinstance(x, (str, tuple)):
            return x
        return x.name

    def add(self, eng, fn, r=(), w=(), dma=None):
        deps = {}

        def dep(ev):
            if ev is None:
                return
            k = ev[:2]
            if deps.get(k, -1) < ev[2]:
                deps[k] = ev[2]
        rk = [self._key(x) for x in r]
        wk = [self._key(x) for x in w]
        for k in rk:
            dep(self.last_w.get(k))
        for k in wk:
            dep(self.last_w.get(k))
            for kk, v in self.reads.get(k, {}).items():
                dep(kk + (v,))
        idx = len(self.ops[eng])
        if dma is None:
            ev = ('op', eng, idx)
            self.last_real[eng] = idx
        else:
            self.dma_cnt[dma] = self.dma_cnt.get(dma, 0) + (1 if isinstance(dma, tuple) else 16)
            ev = ('dma', dma, self.dma_cnt[dma])
        for k in rk:
            d = self.reads.setdefault(k, {})
            if d.get(ev[:2], -1) < ev[2]:
                d[ev[:2]] = ev[2]
        for k in wk:
            self.last_w[k] = ev
            self.reads[k] = {}
        self.ops[eng].append(dict(fn=fn, deps=deps, dma=dma, signal=False, waits=[]))
        return ev

    def barrier(self):
        for e in self.ENGS:
            deps = {}
            for e2 in self.ENGS:
                if e2 != e and self.last_real[e2] is not None:
                    deps[('op', e2)] = self.last_real[e2]
            for s, v in self.dma_cnt.items():
                if isinstance(s, tuple) and e != 'pool':
                    continue
                deps[('dma', s)] = v
            self.ops[e].append(dict(fn=None, deps=deps, dma=None, signal=False, waits=[]))
        self.last_w = {}
        self.reads = {}

    def finish(self):
        self.barrier()
        for e in self.ENGS:
            seen = {}
            for i, op in enumerate(self.ops[e]):
                for k, v in op['deps'].items():
                    if k[0] == 'op':
                        if self.ops[k[1]][v]['dma'] is not None or self.ops[k[1]][v]['fn'] is None:
                            continue
                        if k[1] == e and e not in self.SAME_ENG_SYNC:
                            continue
                    if seen.get(k, -1) >= v:
                        continue
                    seen[k] = v
                    op['waits'].append((k, v))
                    if k[0] == 'op':
                        self.ops[k[1]][v]['signal'] = True
        self.sigval = {}
        for e in self.ENGS:
            c = 0
            for i, op in enumerate(self.ops[e]):
                if op['signal']:
                    c += 1
                    self.sigval[(e, i)] = c

    def emit(self):
        nc = self.nc
        st = self.stack
        esem = {e: st.enter_context(nc.semaphore(f"sem_{e}")) for e in self.ENGS}
        dsem = {s: st.enter_context(nc.semaphore("dsem_" + (s if isinstance(s, str) else "_".join(s)))) for s in self.dma_cnt}
        block = st.enter_context(nc.Block())
        names = {'pe': 'tensor', 'act': 'scalar', 'dve': 'vector', 'pool': 'gpsimd', 'sp': 'sync'}
        sched = self

        def make(e):
            def body(eng):
                for i, op in enumerate(sched.ops[e]):
                    for k, v in op['waits']:
                        if k[0] == 'op':
                            eng.wait_ge(esem[k[1]], sched.sigval[(k[1], v)])
                        else:
                            eng.wait_ge(dsem[k[1]], v)
                    if op['fn'] is None:
                        continue
                    ins = op['fn'](eng)
                    if isinstance(op['dma'], tuple):
                        ins.then_inc(dsem[op['dma']])
                    elif op['dma'] is not None:
                        ins.then_inc(dsem[op['dma']], 16)
                    elif op['signal']:
                        ins.then_inc(esem[e], 1)
            return body
        for e in self.ENGS:
            getattr(block, names[e])(make(e))

    def mm(self, out, lhsT, rhs, start=True, stop=True):
        o, l, rh = U(out), U(lhsT), U(rhs)
        return self.add('pe', lambda e: e.matmul(o, l, rh, start=start, stop=stop), r=[lhsT, rhs], w=[out])

    def tr(self, out, in_, ident):
        o, i, d = U(out), U(in_), U(ident)
        return self.add('pe', lambda e: e.transpose(o, i, d), r=[in_, ident], w=[out])

    def act(self, out, in_, func, bias=None, scale=None, accum_out=None):
        kw = {}
        rr = [in_]
        ww = [out]
        if bias is not None:
            kw['bias'] = U(bias)
            if isinstance(bias, V):
                rr.append(bias)
        if scale is not None:
            kw['scale'] = U(scale)
            if isinstance(scale, V):
                rr.append(scale)
        if accum_out is not None:
            kw['accum_out'] = U(accum_out)
            ww.append(accum_out)
        o, i = U(out), U(in_)
        return self.add('act', lambda e: e.activation(o, i, func, **kw), r=rr, w=ww)

    def tt(self, eng, out, a, b, op):
        o, aa, bb = U(out), U(a), U(b)
        return self.add(eng, lambda e: e.tensor_tensor(o, aa, bb, op), r=[a, b], w=[out])

    def ts(self, eng, out, a, s1, s2, op0, op1=None, accum_out=None):
        rr = [a] + [s for s in (s1, s2) if isinstance(s, V)]
        ww = [out] + ([accum_out] if accum_out is not None else [])
        kw = {}
        if accum_out is not None:
            kw['accum_out'] = U(accum_out)
        o, aa, u1, u2 = U(out), U(a), U(s1), U(s2)
        if op1 is None:
            fn = lambda e: e.tensor_scalar(o, aa, u1, None, op0, **kw)
        else:
            fn = lambda e: e.tensor_scalar(o, aa, u1, u2, op0, op1, **kw)
        return self.add(eng, fn, r=rr, w=ww)

    def stt(self, out, a, s, b, op0, op1, accum_out=None):
        rr = [a, b] + ([s] if isinstance(s, V) else [])
        ww = [out] + ([accum_out] if accum_out is not None else [])
        kw = {}
        if accum_out is not None:
            kw['accum_out'] = U(accum_out)
        o, aa, ss, bb = U(out), U(a), U(s), U(b)
        return self.add('dve', lambda e: e.scalar_tensor_tensor(o, aa, ss, bb, op0, op1, **kw), r=rr, w=ww)

    def copy(self, eng, out, in_):
        o, i = U(out), U(in_)
        if eng == 'act':
            fn = lambda e: e.copy(o, i)
        else:
            fn = lambda e: e.tensor_copy(o, i)
        return self.add(eng, fn, r=[in_], w=[out])

    def memset(self, eng, out, val):
        o = U(out)
        return self.add(eng, lambda e: e.memset(o, val), r=[], w=[out])

    def recip(self, out, in_):
        o, i = U(out), U(in_)
        return self.add('dve', lambda e: e.reciprocal(o, i), r=[in_], w=[out])

    def reduce(self, out, in_, op=ALU.add):
        o, i = U(out), U(in_)
        return self.add('dve', lambda e: e.tensor_reduce(o, i, AX.X, op), r=[in_], w=[out])

    def scan(self, out, d0, d1, init, op0, op1):
        o, a, b = U(out), U(d0), U(d1)
        return self.add('dve', lambda e: e.tensor_tensor_scan(o, a, b, init, op0, op1), r=[d0, d1], w=[out])

    def dma(self, q, out, in_, sem):
        o, i = U(out), U(in_)
        return self.add(q, lambda e: e.dma_start(out=o, in_=i), r=[in_], w=[out], dma=sem)


class Arena:
    def __init__(self, S, words):
        self.S = S
        self.words = words
        self.t = S.stack.enter_context(S.nc.sbuf_tensor("arena", [128, words], F32))
        self.off = 0
        self.n = 0
        self.peak = 0
        self.limit = words

    def alloc(self, shape, dtype, name=None):
        free = int(np.prod(shape[1:]))
        w = free if dtype == F32 else (free + 1) // 2
        assert self.off + w <= self.limit, f"arena overflow {self.off}+{w}>{self.limit} ({name})"
        ap = self.t[:, self.off:self.off + w]
        if dtype != F32:
            ap = ap.bitcast(dtype)[:, 0:free]
        self.off += w
        self.peak = max(self.peak, self.off)
        self.n += 1
        key = f"{name or 't'}_{self.n}"
        if len(shape) == 3:
            ap = ap.rearrange("p (a b) -> p a b", b=shape[2])
        elif len(shape) == 4:
            ap = ap.rearrange("p (a b c) -> p a b c", b=shape[2], c=shape[3])
        return V(ap, key)

    def alloc_top(self, shape, dtype, name):
        free = int(np.prod(shape[1:]))
        assert dtype == F32
        save = self.off
        self.off = self.words - free
        v = self.alloc(shape, dtype, name)
        self.top_off = self.words - free
        self.off = save
        return v

    def mark(self):
        return self.off

    def release(self, m):
        self.off = m


STOP = None


def build_program(NSEG, T, debug_taps=()):
    NT = T // 128
    BW = min(512, T)
    NB = T // BW
    TT = NSEG * T
    nc = bass.Bass("TRN2", target_bir_lowering=False)
    S = Sched(nc)

    def din(name, shape):
        return nc.dram_tensor(name, list(shape), F32, kind="ExternalInput").ap()
    xT = din("xT", [D, TT])
    memT = din("memT", [D, MEM])
    w_in = din("w_in", [D, IN_COLS])
    w_ba = din("w_ba", [1024, D])
    w_bb = din("w_bb", [1024, D])
    w_mo = din("w_mo", [D, D])
    w_q = din("w_q", [D, 512])
    w_kv = din("w_kv", [D, 1024])
    w_o = din("w_o", [512, D])
    w_fu = din("w_fu", [D, 2 * DFF])
    w_fd = din("w_fd", [DFF, D])
    rw_up = din("rw_up", [64, 1024])
    ra_up = din("ra_up", [64, 1024])
    rg_up = din("rg_up", [160, 1024])
    gcols_d = din("gcols", [128, 7 * 16])
    bif_d = din("bif", [128, 8])
    hncols_d = din("hncols", [128, 8])
    rvec_d = din("rvec", [128, 72])
    mulr_d = din("mulr", [128, 4])
    lngb_d = din("lngb", [128, 2048])
    convc_d = din("convc", [128, 512])
    cst_d = din("cst", [128, 6 * 128])
    mreset_d = din("mreset", [128, T])
    outT = nc.dram_tensor("outT", [D, TT], F32, kind="ExternalOutput").ap()
    xsp = [nc.dram_tensor(f"xspill{i}", [D, T], F32).ap() for i in range(NSEG)]
    cmask_d = din("cmask", [128, 4])
    csel_d = din("csel", [128, 4])
    xh_d = din("xh", [128, 16])
    SROWS = 1536
    summR = nc.dram_tensor("summR", [1024, 128], F32).ap()
    gathR = nc.dram_tensor("gathR", [4 * 1024, 128], F32).ap()
    summM = nc.dram_tensor("summM", [512, 258], F32).ap()
    gathM = nc.dram_tensor("gathM", [4 * 512, 258], F32).ap()
    hin = nc.dram_tensor("hin", [128, 32], F32).ap()
    NIDX = NSEG * 8
    scr_kT = nc.dram_tensor("scr_kT", [NSEG, 128, 4, T], BF16).ap()
    scr_vt = nc.dram_tensor("scr_vt", [NSEG, 128, NT, 4, 257], BF16).ap()
    scr_gs = nc.dram_tensor("scr_gs", [NSEG, 128, 16 * NT + 8], F32).ap()
    NHC_ = 2 * NT
    scr_rt = nc.dram_tensor("scr_rt", [NIDX, 128, T], BF16).ap()
    scr_G = nc.dram_tensor("scr_G", [NIDX, 128, NHC_, 256], BF16).ap()
    scr_WT = nc.dram_tensor("scr_WT", [NIDX, 128, NT, 128], BF16).ap()
    scr_Upr = nc.dram_tensor("scr_Upr", [NIDX, 128, NHC_, 64], F32).ap()
    scr_tok = nc.dram_tensor("scr_tok", [NIDX, 128, 3, NT, 128], BF16).ap()
    scr_gt = nc.dram_tensor("scr_gt", [NIDX, 128, NT, 128], F32).ap()
    scr_sm = nc.dram_tensor("scr_sm", [NIDX, 128, 3 * NT], F32).ap()
    hout = nc.dram_tensor("hout", [512, 32], F32).ap()
    taps = {}
    for nm, shp in debug_taps:
        taps[nm] = nc.dram_tensor("dbg_" + nm, list(shp), F32, kind="ExternalOutput").ap()

    A = Arena(S, 42 * 1024)
    PB = [V(S.stack.enter_context(nc.psum_tensor(f"PB{i}", [128, 512], F32))[:, :], f"PB{i}") for i in range(8)]
    pbi = [0]

    def bank():
        b = PB[pbi[0] % 8]
        pbi[0] += 1
        return b

    cst_bf = A.alloc([128, 6 * 128], BF16, "cstbf")
    cst_f = A.alloc([128, 6 * 128], F32, "cstf")
    ident = cst_bf[:, 0:128]
    ones_bf = cst_bf[:, 128:256]
    m_su, m_iu, m_sl = cst_bf[:, 256:384], cst_bf[:, 384:512], cst_bf[:, 512:640]
    ident_f = cst_f[:, 0:128]
    ones_f = cst_f[:, 128:256]
    tri_f = cst_f[:, 384:512]
    blk64_f = cst_f[:, 640:768]
    mask4 = A.alloc([128, 512], BF16, "mask4")
    mreset = A.alloc([128, T], F32, "mreset")
    gcols = A.alloc([128, 112], F32, "gcols")
    bif = A.alloc([128, 8], F32, "bif")
    hncols = A.alloc([128, 8], F32, "hncols")
    rvec = A.alloc([128, 72], F32, "rvec")
    omka = A.alloc([128, 8], F32, "omka")
    mulr = A.alloc([128, 4], F32, "mulr")
    lngb = A.alloc([128, 2048], F32, "lngb")
    convc = A.alloc([128, 512], F32, "convc")
    wup_bf = A.alloc([128, 1024], BF16, "wupbf")
    aup_bf = A.alloc([128, 1024], BF16, "aupbf")
    gup_bf = A.alloc([128, 2, 1024], BF16, "gupbf")
    Cst_f = A.alloc([128, 4, 257], F32, "Cst_f")
    Cst_b = A.alloc([128, 4, 257], BF16, "Cst_b")
    Sst_f = A.alloc([128, 8, 64], F32, "Sst_f")
    Sst_b = [A.alloc([128, 8, 128], BF16, f"Sst_b{i}") for i in range(2)]
    scur = [0] * 8
    hhalo = A.alloc([128, 16], BF16, "hhalo")
    hhalo0 = A.alloc([128, 16], BF16, "hhalo0")
    ebtot = A.alloc([128, 4], F32, "ebtot")
    cmask = A.alloc([128, 4], F32, "cmask")
    csel = A.alloc([128, 4], F32, "csel")
    xh = A.alloc([128, 16], F32, "xh")
    h3halo = A.alloc([128, 16, 2], BF16, "h3halo")
    uhalo = A.alloc([128, 128, 2], F32, "uhalo")
    eps_t = A.alloc([128, 2], F32, "eps")
    NWB = 4
    GW = 128
    WB = [A.alloc([128, 16 * GW], BF16, f"WB{i}") for i in range(NWB)]
    wbi = [0]
    xbufs = [A.alloc([128, T], F32, f"xbuf{i}") for i in range(2)]
    xbi = [0]

    for (dst, src) in ((cst_f, cst_d), (mreset, mreset_d), (gcols, gcols_d), (bif, bif_d), (hncols, hncols_d),
                       (rvec, rvec_d), (mulr, mulr_d), (lngb, lngb_d), (convc, convc_d),
                       (cmask, cmask_d), (csel, csel_d), (xh, xh_d)):
        S.dma('sp', dst, src, 'cst_' + dst.key)
    S.dma('pool', cst_bf, cst_d, 'cstb0')
    S.dma('pool', wup_bf[0:64, :], rw_up, 'cstb1')
    S.dma('pool', aup_bf[64:128, :], ra_up, 'cstb2')
    S.dma('pool', gup_bf[:, 0, :], rg_up[0:128, :], 'cstb3')
    S.dma('pool', gup_bf[0:32, 1, :], rg_up[128:160, :], 'cstb4')
    S.ts('dve', omka, rvec[:, 48:56], -1.0, 1.0, ALU.mult, ALU.add)
    for t_ in (Cst_f, Cst_b, Sst_f, Sst_b[0], Sst_b[1], hhalo, uhalo):
        S.memset('dve', t_, 0.0)
    S.memset('dve', eps_t[:, 0:1], EPS)
    S.memset('dve', eps_t[:, 1:2], GN_EPS)
    S.copy('dve', mask4[:, 0:128], m_su)
    S.copy('dve', mask4[:, 128:256], m_su)
    S.copy('dve', mask4[:, 256:384], m_iu)
    S.copy('dve', mask4[:, 384:512], m_iu)

    def tap(nm, v):
        if nm in taps:
            S.dma('pool', taps[nm], v, 'tap')

    def wload(src_ap, kch, ncols, prow=128):
        assert kch * ncols <= 16 * GW
        wb = WB[wbi[0] % NWB]
        wbi[0] += 1
        view = wb[:, 0:kch * ncols].rearrange("p (k n) -> p k n", n=ncols)
        S.dma('pool', view[0:prow], src_ap.rearrange("(kc p) n -> p kc n", p=prow), wb.key)
        return view

    def xload(src_ap):
        xb = xbufs[xbi[0] % 2]
        xbi[0] += 1
        S.dma('sp', xb[:, 0:src_ap.shape[1]], src_ap, xb.key)
        return xb

    def plan(ncol, kch, nsrc, nblk):
        chunks = max(1, 4 // (nsrc * nblk))
        GC = min(128 * chunks, ((ncol + 127) // 128) * 128)
        KP = max(1, min(kch, (16 * GW) // GC))
        return GC, KP

    def proj_fm(wsrc, col0, ncol, kch, rhs, blocks, sink):
        GC, KP = plan(ncol, kch, 1, len(blocks))
        loads = []
        for c in range(0, ncol, GC):
            g = min(GC, ncol - c)
            for k0 in range(0, kch, KP):
                loads.append((c, g, k0, min(KP, kch - k0)))
        def ld(l):
            c_, g_, k_, kp_ = l
            return wload(wsrc[k_ * 128:(k_ + kp_) * 128, col0 + c_: col0 + c_ + g_], kp_, g_)
        nxt = [ld(loads[0]), ld(loads[1]) if len(loads) > 1 else None]
        pss = None
        for li, (c, g, k0, kp) in enumerate(loads):
            wb = nxt[0]
            nxt = [nxt[1], ld(loads[li + 2]) if li + 2 < len(loads) else None]
            subs = list(range(0, g, 128))
            if k0 == 0:
                pss = {(sub, tag): bank() for sub in subs for (tag, width) in blocks}
            for sub in subs:
                m = min(128, g - sub)
                for (tag, width) in blocks:
                    ps = pss[(sub, tag)]
                    for kk in range(kp):
                        kc = k0 + kk
                        S.mm(ps[0:m, 0:width], wb[:, kk, sub:sub + m], rhs(kc, tag), start=(kc == 0), stop=(kc == kch - 1))
            if k0 + kp >= kch:
                for sub in subs:
                    m = min(128, g - sub)
                    for (tag, width) in blocks:
                        sink((c + sub) // 128, tag, pss[(sub, tag)][0:m, 0:width], m)

    def proj2(ws1, c1, k1, rhs1, ws2, c2, k2, rhs2, ncol, blocks, sink):
        GC, _ = plan(ncol, max(k1, k2), 2, len(blocks))
        srcs = ((ws1, c1, k1, rhs1), (ws2, c2, k2, rhs2))
        loads = []
        for c in range(0, ncol, GC):
            g = min(GC, ncol - c)
            for si, (ws, cb, kch, rh) in enumerate(srcs):
                KP = max(1, min(kch, (16 * GW) // GC))
                for k0 in range(0, kch, KP):
                    loads.append((c, g, si, k0, min(KP, kch - k0)))

        def ld(l):
            c, g, si, k0, kp = l
            ws, cb, kch, rh = srcs[si]
            return wload(ws[k0 * 128:(k0 + kp) * 128, cb + c: cb + c + g], kp, g)
        nxt = [ld(loads[0]), ld(loads[1]) if len(loads) > 1 else None]
        pss = {}
        for li, (c, g, si, k0, kp) in enumerate(loads):
            wb = nxt[0]
            nxt = [nxt[1], ld(loads[li + 2]) if li + 2 < len(loads) else None]
            ws, cb, kch, rh = srcs[si]
            subs = list(range(0, g, 128))
            if k0 == 0:
                for sub in subs:
                    for (tag, width) in blocks:
                        pss[(si, sub, tag)] = bank()
            for sub in subs:
                for (tag, width) in blocks:
                    ps = pss[(si, sub, tag)]
                    for kk in range(kp):
                        kc = k0 + kk
                        S.mm(ps[:, 0:width], wb[:, kk, sub:sub + 128], rh(kc, tag), start=(kc == 0), stop=(kc == kch - 1))
            if si == 1 and k0 + kp >= kch:
                for sub in subs:
                    for (tag, width) in blocks:
                        sink((c + sub) // 128, tag, pss[(0, sub, tag)][:, 0:width], pss[(1, sub, tag)][:, 0:width])

    def rms_stats(getx, bw, ncols):
        rstd = A.alloc([128, ncols], F32, "rstd")
        sqs = [A.alloc([128, ncols], BF16, f"sq{i}") for i in range(2)]
        nb = ncols // bw
        pss = [bank() for _ in range(nb)]
        for fc in range(KC):
            sq = sqs[fc % 2]
            S.act(sq, getx(fc), AF.Square)
            for b in range(nb):
                S.mm(pss[b][:, 0:bw], ones_bf, sq[:, b * bw:(b + 1) * bw], start=(fc == 0), stop=(fc == KC - 1))
        for b in range(nb):
            sl = rstd[:, b * bw:(b + 1) * bw]
            S.act(sl, pss[b][:, 0:bw], AF.Sqrt, bias=eps_t[:, 0:1], scale=1.0 / D)
            S.recip(sl, sl)
        return rstd

    TB = [(b, BW) for b in range(NB)]
    R = A.alloc_top([128, 16, T], F32, "R")
    S.memset('dve', ebtot, 1.0)
    S.memset('dve', h3halo, 0.0)
    hsq = A.alloc([128, 16], F32, "hsq")
    hss = A.alloc([128, 2], F32, "hss")
    S.tt('dve', hsq, xh, xh, ALU.mult)
    S.reduce(hss[:, 0:1], hsq)
    ps = bank()
    S.mm(ps[:, 0:1], ones_f, hss[:, 0:1])
    S.act(hss[:, 1:2], ps[:, 0:1], AF.Sqrt, bias=eps_t[:, 0:1], scale=1.0 / D)
    S.recip(hss[:, 1:2], hss[:, 1:2])
    S.tt('dve', hsq, xh, gcols[:, 0:16], ALU.mult)
    S.ts('dve', hhalo0, hsq, hss[:, 1:2], None, ALU.mult)
    base_mark = A.mark()

    def stop_at(nm):
        pass

    def post_norm_residual(Y, gidx, getres, dst_fn):
        m_ = A.mark()
        rs = rms_stats(lambda fc: Y[:, fc, :], BW, T)
        for fc in range(KC):
            S.stt(Y[:, fc, :], Y[:, fc, :], gcols[:, gidx * 16 + fc: gidx * 16 + fc + 1], rs, ALU.mult, ALU.mult)
            dst_fn(fc, getres(fc))
        S.barrier()
        A.release(m_)
    def seg_AB(seg, mode):
        t0 = seg * T
        A.release(ph_mark)
        A.limit = A.words
        seg_mark = A.mark()
        FULL = (mode == 'B')
        if seg == 0:
            S.copy('dve', hhalo, hhalo0)
        hT = A.alloc([128, 16, T + 1], BF16, "hT")
        haT = A.alloc([128, 8, T], BF16, "haT")
        hbT = A.alloc([128, 8, T], BF16, "hbT")
        m_h = A.mark()
        rstd = rms_stats(lambda fc: xload(xT[fc * 128:(fc + 1) * 128, t0:t0 + T])[:, 0:T], BW, T)
        for fc in range(KC):
            xb = xload(xT[fc * 128:(fc + 1) * 128, t0:t0 + T])
            S.stt(hT[:, fc, 1:T + 1], xb[:, 0:T], gcols[:, fc:fc + 1], rstd, ALU.mult, ALU.mult)
        S.copy('dve', hT[:, :, 0], hhalo)
        S.barrier()
        A.release(m_h)
        m_mix = A.mark()

        def hrhs(kc, b):
            if b == 'h':
                return hT[:, kc, 0:1]
            return hT[:, kc, 1 + b * BW: 1 + (b + 1) * BW]

        qT = A.alloc([128, 4, T], BF16, "qT")
        kT = A.alloc([128, 4, T], BF16, "kT")
        soT = A.alloc([128, 8, T], BF16, "soT")
        vtok = A.alloc([128, NT, 4, 257], BF16, "vtok")
        gates = A.alloc([128, NT, 8], F32, "gates")
        graw = A.alloc([128, NT, 8], F32, "graw")
        S.memset('dve', vtok[:, :, :, 256:257], 1.0)

        def sink_qk(cc, b, ps, m):
            if cc < 4:
                S.copy('act', qT[:, cc, b * BW:(b + 1) * BW], ps)
            else:
                S.act(kT[:, cc - 4, b * BW:(b + 1) * BW], ps, AF.Copy, scale=128 ** -0.5)
        if FULL:
            proj_fm(w_in, O_Q, 512, KC, hrhs, TB, sink_qk)
        else:
            proj_fm(w_in, O_K, 512, KC, hrhs, TB, lambda cc, b, ps, m: sink_qk(cc + 4, b, ps, m))

        def sink_o(cc, b, ps, m):
            S.act(soT[:, cc, b * BW:(b + 1) * BW], ps, AF.Sigmoid)
        if FULL:
            proj_fm(w_in, O_O, 1024, KC, hrhs, TB, sink_o)
        gsm = A.alloc([128, 4, NT, 4], F32, "gsm")
        gsm2 = A.alloc([128, 2, 4], F32, "gsm2")
        bcum, imb, expb, wst = gsm[:, 0], gsm[:, 1], gsm[:, 2], gsm[:, 3]
        bend, ebend = gsm2[:, 0], gsm2[:, 1]
        for cb in range(8):
            wb = wload(w_in[:, O_V + cb * 128: O_V + (cb + 1) * 128], KC, 128)
            for tt_ in range(NT):
                ps = bank()
                for kc in range(KC):
                    S.mm(ps[:, 0:128], hT[:, kc, 1 + tt_ * 128: 1 + (tt_ + 1) * 128], wb[:, kc, 0:128],
                         start=(kc == 0), stop=(kc == KC - 1))
                S.copy('act' if tt_ % 2 else 'dve', vtok[:, tt_, cb // 2, (cb % 2) * 128:(cb % 2) * 128 + 128], ps[:, 0:128])
        if not FULL:
            wb = wload(w_in[:, O_I: O_I + 8], KC, 8)
            for tt_ in range(NT):
                ps = bank()
                for kc in range(KC):
                    S.mm(ps[:, 0:8], hT[:, kc, 1 + tt_ * 128: 1 + (tt_ + 1) * 128], wb[:, kc, 0:8],
                         start=(kc == 0), stop=(kc == KC - 1))
                S.tt('dve', graw[:, tt_, :], ps[:, 0:8], bif, ALU.add)
            S.act(graw, graw, AF.Tanh, scale=1.0 / 15.0)
            S.ts('dve', gates[:, :, 0:4], graw[:, :, 0:4], 15.0, None, ALU.mult)
            S.act(graw[:, :, 4:8], graw[:, :, 4:8], AF.Exp, scale=-15.0)
            S.act(graw[:, :, 4:8], graw[:, :, 4:8], AF.Ln, bias=1.0)
            S.ts('dve', gates[:, :, 4:8], graw[:, :, 4:8], -1.0, None, ALU.mult)
            for tt_ in range(NT):
                ps = bank()
                for j in range(tt_ + 1):
                    S.mm(ps[:, 0:4], (tri_f if j == tt_ else ones_f), gates[:, j, 4:8], start=(j == 0), stop=(j == tt_))
                S.copy('dve', bcum[:, tt_, :], ps[:, 0:4])
            ps = bank()
            for j in range(NT):
                S.mm(ps[:, 0:4], ones_f, gates[:, j, 4:8], start=(j == 0), stop=(j == NT - 1))
            S.copy('dve', bend, ps[:, 0:4])
            S.tt('dve', imb, gates[:, :, 0:4], bcum, ALU.subtract)
            S.act(expb, bcum, AF.Exp)
            for hh in range(4):
                S.act(wst[:, :, hh], imb[:, :, hh], AF.Exp, bias=bend[:, hh:hh + 1])
            S.act(ebend, bend, AF.Exp)
            S.dma('sp', scr_kT[seg], kT, 'st_kT')
            S.dma('sp', scr_gs[seg][:, 0:16 * NT], gsm.rearrange("p a n h -> p (a n h)"), 'st_gs')
            S.dma('sp', scr_gs[seg][:, 16 * NT:16 * NT + 8], gsm2.rearrange("p a h -> p (a h)"), 'st_gs2')
        else:
            S.dma('sp', kT, scr_kT[seg], 'ld_kT')
            S.dma('sp', gsm.rearrange("p a n h -> p (a n h)"), scr_gs[seg][:, 0:16 * NT], 'ld_gs')
            S.dma('sp', gsm2.rearrange("p a h -> p (a h)"), scr_gs[seg][:, 16 * NT:16 * NT + 8], 'ld_gs2')
        diag = [A.alloc([128, 128], F32, f"diag{i}") for i in range(2)]
        Dm = [A.alloc([128, 128], F32, f"Dm{i}") for i in range(2)]
        Pm = [A.alloc([128, 128], BF16, f"Pm{i}") for i in range(3)]
        hmn = [A.alloc([128, 256], BF16, f"hmn{i}") for i in range(2)]
        sc = [A.alloc([128, 8], F32, f"sc{i}") for i in range(2)]
        numts = [A.alloc([128, 257], F32, f"numt{i}") for i in range(2)]
        tmpis = [A.alloc([128, 257], F32, f"tmpi{i}") for i in range(2)]
        junk = A.alloc([128, 256], F32, "junk")
        kw_t = [A.alloc([128, 128], BF16, f"kw{i}") for i in range(2)]
        cnt = 0
        for hh in range(4):
            for jt in (range(NT) if FULL else []):
                dg = diag[cnt % 2]
                S.ts('dve', dg, ident_f, bcum[:, jt, hh:hh + 1], None, ALU.mult)
                pbrow = bank()
                S.mm(pbrow[:, 0:128], ones_f, dg)
                pnum = bank()
                for js in range(jt + 1):
                    pst = bank()
                    S.mm(pst[:, 0:128], kT[:, hh, js * 128:(js + 1) * 128], qT[:, hh, jt * 128:(jt + 1) * 128])
                    dm = Dm[js % 2]
                    S.act(dm, pbrow[:, 0:128], AF.Exp, bias=imb[:, js, hh:hh + 1])
                    if js == jt:
                        S.tt('pool', dm, dm, m_iu, ALU.mult)
                    pm = Pm[js % 3]
                    S.tt('dve', pm, pst[:, 0:128], dm, ALU.mult)
                    S.mm(pnum[:, 0:257], pm, vtok[:, js, hh, :], start=(js == 0), stop=(js == jt))
                pint = bank()
                S.mm(pint[:, 0:257], qT[:, hh, jt * 128:(jt + 1) * 128], Cst_b[:, hh, :])
                tmpi = tmpis[cnt % 2]
                S.ts('dve', tmpi, pint[:, 0:257], expb[:, jt, hh:hh + 1], None, ALU.mult)
                s_ = sc[cnt % 2]
                numt = numts[cnt % 2]
                S.tt('dve', numt, pnum[:, 0:257], tmpi, ALU.add)
                S.ts('dve', s_[:, 0:1], numt[:, 256:257], -1.0, None, ALU.mult)
                S.tt('dve', s_[:, 0:1], s_[:, 0:1], numt[:, 256:257], ALU.max)
                S.ts('dve', s_[:, 0:1], s_[:, 0:1], 1.0, None, ALU.max)
                S.recip(s_[:, 0:1], s_[:, 0:1])
                S.stt(junk, numt[:, 0:256], 1.0, numt[:, 0:256], ALU.mult, ALU.mult, accum_out=s_[:, 1:2])
                S.tt('dve', s_[:, 2:3], s_[:, 0:1], s_[:, 0:1], ALU.mult)
                S.tt('dve', s_[:, 2:3], s_[:, 2:3], s_[:, 1:2], ALU.mult)
                S.act(s_[:, 3:4], s_[:, 2:3], AF.Sqrt, bias=eps_t[:, 0:1], scale=1.0 / 256.0)
                S.recip(s_[:, 3:4], s_[:, 3:4])
                S.tt('dve', s_[:, 4:5], s_[:, 3:4], s_[:, 0:1], ALU.mult)
                hm = hmn[cnt % 2]
                S.ts('dve', hm, numt[:, 0:256], s_[:, 4:5], None, ALU.mult)
                ptr = bank()
                ptb = ptr.bitcast(BF16)
                for vc in range(2):
                    S.tr(ptb[:, vc * 128:(vc + 1) * 128], hm[:, vc * 128:(vc + 1) * 128], ident)
                for vc in range(2):
                    fcx = hh * 2 + vc
                    S.stt(haT[:, fcx, jt * 128:(jt + 1) * 128], ptb[:, vc * 128:(vc + 1) * 128],
                          hncols[:, fcx:fcx + 1], soT[:, fcx, jt * 128:(jt + 1) * 128], ALU.mult, ALU.mult)
                cnt += 1
            pc = bank()
            for js in range(NT):
                ptr = bank()
                ptb = ptr.bitcast(BF16)
                S.tr(ptb[:, 0:128], kT[:, hh, js * 128:(js + 1) * 128], ident)
                kw_ = kw_t[js % 2]
                S.ts('dve', kw_, ptb[:, 0:128], wst[:, js, hh:hh + 1], None, ALU.mult)
                S.mm(pc[:, 0:257], kw_, vtok[:, js, hh, :], start=(js == 0), stop=(js == NT - 1))
            S.stt(Cst_f[:, hh, :], Cst_f[:, hh, :], ebend[:, hh:hh + 1], pc[:, 0:257], ALU.mult, ALU.add)
            S.copy('act', Cst_b[:, hh, :], Cst_f[:, hh, :])
        if not FULL:
            S.tt('dve', ebtot, ebtot, ebend, ALU.mult)
        tap("haT", haT.rearrange("p a t -> p (a t)"))
        stop_at("mlstm")
        S.barrier()
        A.release(m_mix)

        HB = TB + [('h', 1)]

        def colof(b):
            return (0, 1) if b == 'h' else (1 + b * BW, BW)
        if not FULL:
            twl = A.alloc([128, T], BF16, "twl")
            sg = A.alloc([128, 2, T], BF16, "sg")
            m_lr = A.mark()
            lrraw = A.alloc([128, 3, T + 1], F32, "lrraw")

            def sink_lr(cc, b, ps, m):
                c0, w_ = colof(b)
                S.copy('act', lrraw[0:m, cc, c0:c0 + w_], ps)
            proj_fm(w_in, O_RWL, 288, KC, hrhs, HB, sink_lr)
            lrt = A.alloc([128, T], F32, "lrt")
            S.tt('dve', lrt, lrraw[:, 0, 0:T], lrraw[:, 0, 1:T + 1], ALU.subtract)
            S.stt(lrt, lrt, mulr[:, 0:1], lrraw[:, 0, 1:T + 1], ALU.mult, ALU.add)
            S.act(twl[0:64, :], lrt[0:64, :], AF.Tanh)
            S.copy('dve', twl[64:128, :], lrt[64:128, :])
            S.tt('dve', lrt, lrraw[:, 1, 0:T], lrraw[:, 1, 1:T + 1], ALU.subtract)
            S.stt(lrt, lrt, mulr[:, 1:2], lrraw[:, 1, 1:T + 1], ALU.mult, ALU.add)
            S.act(sg[:, 0, :], lrt, AF.Sigmoid)
            S.tt('dve', lrt[0:32, :], lrraw[0:32, 2, 0:T], lrraw[0:32, 2, 1:T + 1], ALU.subtract)
            S.stt(lrt[0:32, :], lrt[0:32, :], mulr[0:32, 2:3], lrraw[0:32, 2, 1:T + 1], ALU.mult, ALU.add)
            S.act(sg[0:32, 1, :], lrt[0:32, :], AF.Sigmoid)
            S.barrier()
            A.release(m_lr)
        stop_at('rw_lr')
        m_rw = A.mark()
        for hp in range(8):
            A.release(m_rw)
            cs_ = slice(hp * 128, (hp + 1) * 128)
            bon = A.alloc([128, NT, 2], F32, "bon")
            gL = A.alloc([128, NT], F32, "gL")
            rt = A.alloc([128, T], BF16, "rt")
            at = A.alloc([128, T], BF16, "at")
            bt = A.alloc([128, T], BF16, "bt")
            kt = A.alloc([128, T], BF16, "kt")
            bh = A.alloc([128, T], BF16, "bh")
            kh = A.alloc([128, T], BF16, "kh")
            vb = A.alloc([128, T], BF16, "vb")
            NHC = 2 * NT
            sidx = seg * 8 + hp
            if not FULL:
                m_tmp = A.mark()
                praw = A.alloc([128, T + 1], F32, "praw")

                def sk(cc, b, ps, m):
                    c0, w_ = colof(b)
                    S.copy('act', praw[:, c0:c0 + w_], ps)
                rr_ = A.alloc([128, T], F32, "r")
                kr_ = A.alloc([128, T], F32, "kr")
                vr_ = A.alloc([128, T], F32, "vr")
                tmp = A.alloc([128, T], F32, "tmp")
                ka = A.alloc([128, T], F32, "ka")
                tmp2 = ka
                for i, (off, dst) in enumerate(((O_RR, rr_), (O_RK, kr_), (O_RV, vr_))):
                    proj_fm(w_in, off + hp * 128, 128, KC, hrhs, HB, sk)
                    S.tt('dve', tmp, praw[:, 0:T], praw[:, 1:T + 1], ALU.subtract)
                    S.stt(dst, tmp, rvec[:, i * 8 + hp: i * 8 + hp + 1], praw[:, 1:T + 1], ALU.mult, ALU.add)
                lw = A.alloc([128, T], F32, "lw")
                a_ = A.alloc([128, T], F32, "a")
                for b in range(NB):
                    bs = slice(b * BW, (b + 1) * BW)
                    ps = bank()
                    S.mm(ps[:, 0:BW], wup_bf[0:64, cs_], twl[0:64, bs])
                    S.act(lw[:, bs], ps[:, 0:BW], AF.Sigmoid, bias=rvec[:, 24 + hp:25 + hp])
                    ps = bank()
                    S.mm(ps[:, 0:BW], aup_bf[64:128, cs_], twl[64:128, bs])
                    S.act(a_[:, bs], ps[:, 0:BW], AF.Sigmoid, bias=rvec[:, 32 + hp:33 + hp])
                S.ts('dve', lw, lw, -0.6065306597126334, None, ALU.mult)
                kap = A.alloc([128, T], F32, "kap")
                S.ts('dve', kap, kr_, rvec[:, 40 + hp:41 + hp], None, ALU.mult)
                S.tt('dve', tmp, kap, kap, ALU.mult)
                for b in range(NB):
                    bs = slice(b * BW, (b + 1) * BW)
                    ps = bank()
                    S.mm(ps[:, 0:BW], blk64_f, tmp[:, bs])
                    S.act(tmp2[:, bs], ps[:, 0:BW], AF.Sqrt)
                S.ts('dve', tmp2, tmp2, 1e-12, None, ALU.max)
                S.recip(tmp2, tmp2)
                S.tt('dve', kap, kap, tmp2, ALU.mult)
                S.ts('dve', tmp, a_, rvec[:, 48 + hp:49 + hp], omka[:, hp:hp + 1], ALU.mult, ALU.add)
                S.tt('dve', kr_, kr_, tmp, ALU.mult)
                S.tt('dve', tmp, rr_, kr_, ALU.mult)
                S.ts('dve', tmp, tmp, rvec[:, 56 + hp:57 + hp], None, ALU.mult)
                for n in range(NT):
                    ps = bank()
                    S.mm(ps[:, 0:2], tmp[:, n * 128:(n + 1) * 128], blk64_f[:, 0:128:64])
                    S.copy('dve', bon[:, n, :], ps[:, 0:2])
                S.tt('dve', ka, kap, a_, ALU.mult)
                cs = A.alloc([128, T], F32, "cs")
                S.scan(cs, mreset, lw, 0.0, ALU.mult, ALU.add)
                cs3 = cs.rearrange("p (n t) -> p n t", t=128)
                S.act(gL, cs3[:, :, 127], AF.Exp)
                S.copy('pool', vb, vr_)
                S.act(tmp, cs, AF.Exp)
                S.tt('dve', rt, rr_, tmp, ALU.mult)
                S.tt('dve', tmp, cs, lw, ALU.subtract)
                S.act(tmp, tmp, AF.Exp)
                S.stt(at, kap, -1.0, tmp, ALU.mult, ALU.mult)
                S.act(tmp, cs, AF.Exp, scale=-1.0)
                S.tt('dve', bt, ka, tmp, ALU.mult)
                S.tt('dve', kt, kr_, tmp, ALU.mult)
                tmp3 = tmp.rearrange("p (n t) -> p n t", t=128)
                S.tt('dve', tmp3, cs3[:, :, 127:128].to_broadcast([128, NT, 128]), cs3, ALU.subtract)
                S.act(tmp, tmp, AF.Exp)
                S.tt('dve', bh, ka, tmp, ALU.mult)
                S.tt('dve', kh, kr_, tmp, ALU.mult)
                stop_at('rw_proj')
                S.barrier()
                A.release(m_tmp)
                toks = []
                for X in (at, bh, kh, vb):
                    Xt = A.alloc([128, NT, 128], BF16, "tok")
                    ptr = bank()
                    ptb = ptr.bitcast(BF16)
                    for n in range(NT):
                        S.tr(ptb[:, n * 128:(n + 1) * 128], X[:, n * 128:(n + 1) * 128], ident)
                    S.copy('act', Xt.rearrange("p n c -> p (n c)"), ptb[:, 0:NT * 128])
                    toks.append(Xt)
                a_tok, bh_tok, kh_tok, v_tok = toks
                g_tok = A.alloc([128, NT, 128], F32, "gtok")
                for n in range(NT):
                    ps = bank()
                    S.mm(ps[:, 0:128], sg[:, 0, n * 128:(n + 1) * 128], gup_bf[:, 0, cs_], start=True, stop=False)
                    S.mm(ps[:, 0:128], sg[0:32, 1, n * 128:(n + 1) * 128], gup_bf[0:32, 1, cs_], start=False, stop=True)
                    S.copy('act', g_tok[:, n, :], ps[:, 0:128])
                NHC = 2 * NT
                G4 = A.alloc([128, NHC, 512], BF16, "G4")
                PP = [A.alloc([128, NHC, 256], BF16, f"PP{i}") for i in range(2)]
                TTm = [A.alloc([128, NHC, 128], BF16, f"TT{i}") for i in range(2)]
                RH2 = A.alloc([128, NHC, 64], BF16, "RH2")
                WT = A.alloc([128, NT, 128], BF16, "WT")
                Upr = A.alloc([128, NHC, 64], F32, "Upr")
                for h in range(2):
                    hs = slice(h * 64, (h + 1) * 64)
                    for n in range(NT):
                        hc = h * NT + n
                        ch = slice(n * 128, (n + 1) * 128)
                        pg = bank()
                        S.mm(pg[:, 0:128], bt[hs, ch], at[hs, ch])
                        S.mm(pg[:, 128:256], kt[hs, ch], at[hs, ch])
                        S.mm(pg[:, 256:384], bt[hs, ch], rt[hs, ch])
                        S.mm(pg[:, 384:512], kt[hs, ch], rt[hs, ch])
                        S.tt('dve', G4[:, hc, :], pg[:, 0:512], mask4, ALU.mult)
                        pa = bank()
                        S.mm(pa[:, 0:128], at[hs, ch], bt[hs, ch])
                        S.tt('dve', PP[0][:, hc, 0:128], pa[:, 0:128], m_sl, ALU.mult)
                        S.copy('pool', PP[0][:, hc, 128:256], G4[:, hc, 0:128])
                        S.tt('pool', TTm[0][:, hc, :], G4[:, hc, 0:128], ident, ALU.add)
                        p2 = bank()
                        S.mm(p2[:, 0:64], G4[:, hc, 128:256], v_tok[:, n, hs])
                        S.copy('act', RH2[:, hc, :], p2[:, 0:64])
                for lev in range(1, 7):
                    src, dst = PP[(lev - 1) % 2], PP[lev % 2]
                    tsrc, tdst = TTm[(lev - 1) % 2], TTm[lev % 2]
                    for hc in range(NHC):
                        pq = bank()
                        S.mm(pq[:, 0:128], src[:, hc, 128:256], src[:, hc, 0:128])
                        if lev < 6:
                            S.mm(pq[:, 128:256], src[:, hc, 0:128], src[:, hc, 128:256])
                            S.copy('act', dst[:, hc, :], pq[:, 0:256])
                        else:
                            S.copy('act', dst[:, hc, 0:128], pq[:, 0:128])
                        pt_ = bank()
                        S.mm(pt_[:, 0:128], dst[:, hc, 0:128], tsrc[:, hc, :])
                        S.tt('dve', tdst[:, hc, :], pt_[:, 0:128], tsrc[:, hc, :], ALU.add)
                TTf = TTm[0]
                for h in range(2):
                    hs = slice(h * 64, (h + 1) * 64)
                    for n in range(NT):
                        hc = h * NT + n
                        pw = bank()
                        S.mm(pw[hs, 0:128], a_tok[:, n, hs], TTf[:, hc, :])
                        S.copy('act', WT[hs, n, :], pw[hs, 0:128])
                        pu = bank()
                        S.mm(pu[:, 0:64], TTf[:, hc, :], RH2[:, hc, :])
                        S.copy('dve', Upr[:, hc, :], pu[:, 0:64])
                S.dma('sp', scr_rt[sidx], rt, 'st_rt')
                S.dma('sp', scr_G[sidx], G4[:, :, 256:512], 'st_G')
                S.dma('sp', scr_WT[sidx], WT, 'st_WT')
                S.dma('sp', scr_Upr[sidx], Upr, 'st_Upr')
                S.dma('sp', scr_tok[sidx][:, 0], bh_tok, 'st_t0')
                S.dma('sp', scr_tok[sidx][:, 1], kh_tok, 'st_t1')
                S.dma('sp', scr_tok[sidx][:, 2], v_tok, 'st_t2')
                S.dma('sp', scr_gt[sidx], g_tok, 'st_gt')
                S.dma('sp', scr_sm[sidx][:, 0:NT], gL, 'st_gl')
                S.dma('sp', scr_sm[sidx][:, NT:3 * NT], bon.rearrange("p n h -> p (n h)"), 'st_bon')
            else:
                bh_tok = A.alloc([128, NT, 128], BF16, "tok")
                kh_tok = A.alloc([128, NT, 128], BF16, "tok")
                v_tok = A.alloc([128, NT, 128], BF16, "tok")
                g_tok = A.alloc([128, NT, 128], F32, "gtok")
                G4 = A.alloc([128, NHC, 512], BF16, "G4")
                WT = A.alloc([128, NT, 128], BF16, "WT")
                Upr = A.alloc([128, NHC, 64], F32, "Upr")
                S.dma('sp', rt, scr_rt[sidx], 'ld_rt')
                S.dma('sp', G4[:, :, 256:512], scr_G[sidx], 'ld_G')
                S.dma('sp', WT, scr_WT[sidx], 'ld_WT')
                S.dma('sp', Upr, scr_Upr[sidx], 'ld_Upr')
                S.dma('sp', bh_tok, scr_tok[sidx][:, 0], 'ld_t0')
                S.dma('sp', kh_tok, scr_tok[sidx][:, 1], 'ld_t1')
                S.dma('sp', v_tok, scr_tok[sidx][:, 2], 'ld_t2')
                S.dma('sp', g_tok, scr_gt[sidx], 'ld_gt')
                S.dma('sp', gL, scr_sm[sidx][:, 0:NT], 'ld_gl')
                S.dma('sp', bon.rearrange("p n h -> p (n h)"), scr_sm[sidx][:, NT:3 * NT], 'ld_bon')
            stop_at('rw_gram')
            W_ = 64 if FULL else 128
            if FULL:
                Sf_, Sb_, cur_ = Sst_f, Sst_b, scur
            else:
                Sf_, Sb_, cur_ = SfA, SbA, scurA
            y_tok = A.alloc([128, NT, 128], F32, "ytok")
            UTb = [A.alloc([128, 2, W_], BF16, f"UTb{i}") for i in range(2)]
            for n in range(NT):
                ch = slice(n * 128, (n + 1) * 128)
                Sold = Sb_[cur_[hp]]
                Snew = Sb_[1 - cur_[hp]]
                ut = UTb[n % 2]
                pu = bank()
                S.mm(pu[:, 0:2 * W_], WT[:, n, :], Sold[:, hp, :])
                for h in range(2):
                    S.tt('dve', ut[:, h, 0:64], pu[:, h * W_:h * W_ + 64], Upr[:, h * NT + n, :], ALU.add)
                    if not FULL:
                        S.copy('act', ut[:, h, 64:128], pu[:, h * W_ + 64:(h + 1) * W_])
                ps_ = bank()
                if FULL:
                    py = bank()
                    S.mm(py[:, 0:128], rt[:, ch], Sold[:, hp, :], start=True, stop=False)
                for h in range(2):
                    hs = slice(h * 64, (h + 1) * 64)
                    hc = h * NT + n
                    S.mm(ps_[hs, 0:W_], bh_tok[:, n, hs], ut[:, h, :], start=True, stop=False)
                    S.mm(ps_[hs, 0:64], kh_tok[:, n, hs], v_tok[:, n, hs], start=False, stop=True)
                    if FULL:
                        S.mm(py[:, h * 64:(h + 1) * 64], G4[:, hc, 256:384], ut[:, h, :], start=False, stop=False)
                        S.mm(py[:, h * 64:(h + 1) * 64], G4[:, hc, 384:512], v_tok[:, n, hs], start=False, stop=True)
                S.stt(Sf_[:, hp, :], Sf_[:, hp, :], gL[:, n:n + 1], ps_[:, 0:W_], ALU.mult, ALU.add)
                S.copy('act', Snew[0:64, hp, 0:W_], Sf_[0:64, hp, :])
                S.copy('act', Snew[64:128, hp, W_:2 * W_], Sf_[64:128, hp, :])
                cur_[hp] = 1 - cur_[hp]
                if FULL:
                    S.copy('act', y_tok[:, n, :], py[:, 0:128])
            if not FULL:
                S.barrier()
                continue
            stop_at('rw_seq')
            y4 = y_tok.rearrange("p n (h v) -> p (n h) v", v=64)
            st1 = A.alloc([128, NHC], F32, "st1")
            st2 = A.alloc([128, NHC], F32, "st2")
            yc = A.alloc([128, NHC, 64], F32, "yc")
            ysq = A.alloc([128, NHC, 64], F32, "ysq")
            S.reduce(st1, y4)
            S.ts('dve', st1, st1, 1.0 / 64.0, None, ALU.mult)
            S.tt('dve', yc, y4, st1.rearrange("p (a o) -> p a o", o=1).to_broadcast([128, NHC, 64]), ALU.subtract)
            S.tt('pool', ysq, yc, yc, ALU.mult)
            S.reduce(st2, ysq)
            S.act(st2, st2, AF.Sqrt, bias=eps_t[:, 1:2], scale=1.0 / 64.0)
            S.recip(st2, st2)
            S.tt('dve', yc, yc, st2.rearrange("p (a o) -> p a o", o=1).to_broadcast([128, NHC, 64]), ALU.mult)
            yc3 = yc.rearrange("p (n h) v -> p n (h v)", h=2)
            lg = lngb[:, hp * 128:(hp + 1) * 128].rearrange("p (o c) -> p o c", o=1).to_broadcast([128, NT, 128])
            lb = lngb[:, 1024 + hp * 128:1024 + (hp + 1) * 128].rearrange("p (o c) -> p o c", o=1).to_broadcast([128, NT, 128])
            S.tt('dve', yc3, yc3, lg, ALU.mult)
            S.tt('dve', yc3, yc3, lb, ALU.add)
            bv = ysq
            S.tt('dve', bv, v_tok.rearrange("p n (h v) -> p (n h) v", v=64),
                 bon.rearrange("p n (h o) -> p (n h) o", o=1).to_broadcast([128, NHC, 64]), ALU.mult)
            S.tt('dve', yc, yc, bv, ALU.add)
            hb_tok = A.alloc([128, NT, 128], BF16, "hbtok")
            S.tt('dve', hb_tok, yc3, g_tok, ALU.mult)
            ptr = bank()
            ptb = ptr.bitcast(BF16)
            for n in range(NT):
                S.tr(ptb[:, n * 128:(n + 1) * 128], hb_tok[:, n, :], ident)
            S.copy('act', hbT[:, hp, :], ptb[:, 0:T])
            S.barrier()
        tap("hbT", hbT.rearrange("p a t -> p (a t)"))
        stop_at("rwkv")
        S.copy('dve', hhalo, hT[:, :, T])
        A.release(m_mix)
        if not FULL:
            S.barrier()
            return

        A.limit = A.top_off
        mergedT = A.alloc([128, 16, T], BF16, "mergedT")
        sgt = [A.alloc([128, BW], F32, f"sgt{i}") for i in range(2)]
        sgi = [0]

        def hrhs2(kc, b):
            return hT[:, kc, 1 + b * BW: 1 + (b + 1) * BW]

        def sink_ma(cc, b, p1, p2):
            t_ = sgt[sgi[0] % 2]
            sgi[0] += 1
            S.act(t_, p1, AF.Sigmoid)
            S.tt('dve', mergedT[:, cc, b * BW:(b + 1) * BW], t_, p2, ALU.mult)
        proj2(w_in, O_GA, KC, hrhs2, w_ba, 0, 8, lambda kc, b: haT[:, kc, b * BW:(b + 1) * BW], 2048, TB, sink_ma)

        def sink_mb(cc, b, p1, p2):
            t_ = sgt[sgi[0] % 2]
            sgi[0] += 1
            S.act(t_, p1, AF.Sigmoid)
            S.tt('dve', t_, t_, p2, ALU.mult)
            S.tt('pool', mergedT[:, cc, b * BW:(b + 1) * BW], mergedT[:, cc, b * BW:(b + 1) * BW], t_, ALU.add)
        proj2(w_in, O_GB, KC, hrhs2, w_bb, 0, 8, lambda kc, b: hbT[:, kc, b * BW:(b + 1) * BW], 2048, TB, sink_mb)

        def sink_R(cc, b, ps, m):
            S.copy('act' if (cc + b) % 2 else 'dve', R[:, cc, b * BW:(b + 1) * BW], ps)
        proj_fm(w_mo, 0, 2048, KC, lambda kc, b: mergedT[:, kc, b * BW:(b + 1) * BW], TB, sink_R)
        S.barrier()
        A.release(seg_mark)

        def post_norm_residual(Y, gidx, getres, dst_fn):
            m_ = A.mark()
            rs = rms_stats(lambda fc: Y[:, fc, :], BW, T)
            for fc in range(KC):
                S.stt(Y[:, fc, :], Y[:, fc, :], gcols[:, gidx * 16 + fc: gidx * 16 + fc + 1], rs, ALU.mult, ALU.mult)
                dst_fn(fc, getres(fc))
            S.barrier()
            A.release(m_)

        def dst_R(fc, res):
            S.tt('dve', R[:, fc, :], R[:, fc, :], res, ALU.add)
        post_norm_residual(R, 1, lambda fc: xload(xT[fc * 128:(fc + 1) * 128, t0:t0 + T])[:, 0:T], dst_R)
        tap("x1", R.rearrange("p a t -> p (a t)"))
        stop_at("merge")

        def pre_norm(gidx):
            h_ = A.alloc([128, 16, T], BF16, "hTn")
            m_ = A.mark()
            rs = rms_stats(lambda fc: R[:, fc, :], BW, T)
            for fc in range(KC):
                S.stt(h_[:, fc, :], R[:, fc, :], gcols[:, gidx * 16 + fc: gidx * 16 + fc + 1], rs, ALU.mult, ALU.mult)
            S.barrier()
            A.release(m_)
            return h_
        oT = A.alloc([128, 4, T], BF16, "oT")
        m_xa = A.mark()
        hT2 = pre_norm(2)
        mnT = A.alloc([128, 16, MEM], BF16, "mnT")
        m_ = A.mark()
        rsm = rms_stats(lambda fc: xload(memT[fc * 128:(fc + 1) * 128, :])[:, 0:MEM], MEM, MEM)
        for fc in range(KC):
            xb = xload(memT[fc * 128:(fc + 1) * 128, :])
            S.stt(mnT[:, fc, :], xb[:, 0:MEM], gcols[:, 3 * 16 + fc: 3 * 16 + fc + 1], rsm, ALU.mult, ALU.mult)
        S.barrier()
        A.release(m_)
        kmT = A.alloc([128, 4, MEM], BF16, "kmT")
        vm = A.alloc([128, 2, 512], BF16, "vm")
        qT2 = A.alloc([128, 4, T], BF16, "qT2")

        def sink_km(cc, b, ps, m):
            S.copy('act', kmT[:, cc, :], ps)
        proj_fm(w_kv, 0, 512, KC, lambda kc, b: mnT[:, kc, :], [(0, MEM)], sink_km)
        for cb in range(4):
            wb = wload(w_kv[:, 512 + cb * 128: 512 + (cb + 1) * 128], KC, 128)
            for mt in range(2):
                ps = bank()
                for kc in range(KC):
                    S.mm(ps[:, 0:128], mnT[:, kc, mt * 128:(mt + 1) * 128], wb[:, kc, 0:128],
                         start=(kc == 0), stop=(kc == KC - 1))
                S.copy('dve', vm[:, mt, cb * 128:(cb + 1) * 128], ps[:, 0:128])

        def sink_q2(cc, b, ps, m):
            S.act(qT2[:, cc, b * BW:(b + 1) * BW], ps, AF.Copy, scale=128 ** -0.5)
        proj_fm(w_q, 0, 512, KC, lambda kc, b: hT2[:, kc, b * BW:(b + 1) * BW], TB, sink_q2)
        pex = [A.alloc([128, MEM], F32, f"pex{i}") for i in range(2)]
        pnb = [A.alloc([128, MEM], BF16, f"pnb{i}") for i in range(2)]
        pTt = [A.alloc([128, 2, 128], BF16, f"pTt{i}") for i in range(2)]
        sm = [A.alloc([128, 4], F32, f"sm{i}") for i in range(2)]
        c2 = 0
        for h in range(4):
            for tt_ in range(NT):
                ts_ = slice(tt_ * 128, (tt_ + 1) * 128)
                psc = bank()
                S.mm(psc[:, 0:MEM], qT2[:, h, ts_], kmT[:, h, :])
                s_ = sm[c2 % 2]
                S.add('dve', (lambda o, i: (lambda e: e.tensor_reduce(o, i, AX.X, ALU.max)))(U(s_[:, 0:1]), U(psc[:, 0:MEM])),
                      r=[psc], w=[s_])
                S.ts('dve', s_[:, 1:2], s_[:, 0:1], -1.0, None, ALU.mult)
                pe_ = pex[c2 % 2]
                S.act(pe_, psc[:, 0:MEM], AF.Exp, bias=s_[:, 1:2], accum_out=s_[:, 2:3])
                S.recip(s_[:, 3:4], s_[:, 2:3])
                pn_ = pnb[c2 % 2]
                S.ts('dve', pn_, pe_, s_[:, 3:4], None, ALU.mult)
                ptr = bank()
                ptb = ptr.bitcast(BF16)
                for mt in range(2):
                    S.tr(ptb[:, mt * 128:(mt + 1) * 128], pn_[:, mt * 128:(mt + 1) * 128], ident)
                pt2 = pTt[c2 % 2]
                S.copy('act', pt2.rearrange("p a t -> p (a t)"), ptb[:, 0:256])
                po = bank()
                for mt in range(2):
                    S.mm(po[:, 0:128], vm[:, mt, h * 128:(h + 1) * 128], pt2[:, mt, :], start=(mt == 0), stop=(mt == 1))
                S.copy('dve', oT[:, h, ts_], po[:, 0:128])
                c2 += 1
        S.barrier()
        A.release(m_xa)
        Y2 = A.alloc([128, 16, T], F32, "Y2")

        def sink_Y2(cc, b, ps, m):
            S.copy('act' if (cc + b) % 2 else 'dve', Y2[:, cc, b * BW:(b + 1) * BW], ps)
        proj_fm(w_o, 0, 2048, 4, lambda kc, b: oT[:, kc, b * BW:(b + 1) * BW], TB, sink_Y2)

        def dst_R2(fc, res):
            S.tt('dve', R[:, fc, :], R[:, fc, :], res, ALU.add)
        post_norm_residual(Y2, 4, lambda fc: Y2[:, fc, :], dst_R2)
        tap("x2", R.rearrange("p a t -> p (a t)"))
        stop_at("xattn")
        S.barrier()
        A.release(seg_mark)
        S.dma('sp', xsp[seg].rearrange("(fc p) t -> p fc t", p=128), R, f'spill{seg}')
        if seg == NSEG - 1:
            sq2 = A.alloc([128, 16, 2], BF16, "sq2")
            S.act(sq2, R[:, :, T - 2:T], AF.Square)
            ps = bank()
            for fc in range(KC):
                S.mm(ps[:, 0:2], ones_bf, sq2[:, fc, :], start=(fc == 0), stop=(fc == KC - 1))
            rs2 = A.alloc([128, 2], F32, "rs2")
            S.act(rs2, ps[:, 0:2], AF.Sqrt, bias=eps_t[:, 0:1], scale=1.0 / D)
            S.recip(rs2, rs2)
            h3h = A.alloc([128, 16, 2], F32, "h3h")
            S.tt('dve', h3h, R[:, :, T - 2:T],
                 gcols[:, 80:96].rearrange("p (a o) -> p a o", o=1).to_broadcast([128, 16, 2]), ALU.mult)
            S.tt('dve', h3h, h3h, rs2.rearrange("p (o t) -> p o t", o=1).to_broadcast([128, 16, 2]), ALU.mult)
            S.dma('sp', hin, h3h.rearrange("p a t -> p (a t)"), 'hin')
        S.barrier()


    def seg_C(seg):
        t0 = seg * T
        A.release(ph_mark)
        A.limit = A.top_off
        hT3 = A.alloc([128, 16, T], BF16, "hT3")
        m_ = A.mark()
        rs3 = rms_stats(lambda fc: xload(xsp[seg][fc * 128:(fc + 1) * 128, :])[:, 0:T], BW, T)
        for fc in range(KC):
            xb = xload(xsp[seg][fc * 128:(fc + 1) * 128, :])
            S.stt(hT3[:, fc, :], xb[:, 0:T], gcols[:, 80 + fc:81 + fc], rs3, ALU.mult, ALU.mult)
        S.barrier()
        A.release(m_)
        ACC = R
        FB = ([('h', 2)] if seg == 0 else []) + TB

        def h3rhs(kc, b):
            if b == 'h':
                return h3halo[:, kc, :]
            return hT3[:, kc, b * BW:(b + 1) * BW]
        GF = 8
        actT = A.alloc([128, GF, T], BF16, "actT")
        ug = [A.alloc([128, T + 2], F32, f"ug{i}") for i in range(2)]
        uu = [A.alloc([128, T + 2], F32, f"uu{i}") for i in range(2)]
        cgs = [A.alloc([128, BW], F32, f"cg{i}") for i in range(2)]
        cus = [A.alloc([128, BW], F32, f"cu{i}") for i in range(2)]
        pls = [A.alloc([128, BW], F32, f"pl{i}") for i in range(2)]
        ci = [0]
        for g in range(DFF // 128 // GF):
            def sink_f(cc, b, pg_, pu_, g=g):
                f = g * GF + cc
                ug_, uu_ = ug[f % 2], uu[f % 2]
                if b == 'h':
                    S.copy('act', ug_[:, 0:2], pg_)
                    S.copy('act', uu_[:, 0:2], pu_)
                    return
                bs2 = slice(2 + b * BW, 2 + (b + 1) * BW)
                if b == 0 and seg > 0:
                    S.copy('pool', ug_[:, 0:2], uhalo[:, f, :])
                    S.copy('pool', uu_[:, 0:2], uhalo[:, 64 + f, :])
                S.copy('act', ug_[:, bs2], pg_)
                S.copy('act', uu_[:, bs2], pu_)
                k_ = ci[0] % 2
                ci[0] += 1
                cg, cu, pl = cgs[k_], cus[k_], pls[k_]
                for (src, dst, chn) in ((ug_, cg, f), (uu_, cu, 64 + f)):
                    S.ts('dve', dst, src[:, 2 + b * BW: 2 + (b + 1) * BW], convc[:, 256 + chn:257 + chn],
                         convc[:, 384 + chn:385 + chn], ALU.mult, ALU.add)
                    S.stt(dst, src[:, 1 + b * BW: 1 + (b + 1) * BW], convc[:, 128 + chn:129 + chn], dst, ALU.mult, ALU.add)
                    S.stt(dst, src[:, b * BW: (b + 1) * BW], convc[:, chn:chn + 1], dst, ALU.mult, ALU.add)
                S.tt('pool', pl, cg, cg, ALU.mult)
                S.ts('pool', pl, pl, 0.044715, 1.0, ALU.mult, ALU.add)
                S.tt('pool', pl, pl, cg, ALU.mult)
                S.act(pl, pl, AF.Sigmoid, scale=1.5957691216057308)
                S.tt('dve', cg, cg, pl, ALU.mult)
                S.tt('dve', actT[:, cc, b * BW:(b + 1) * BW], cg, cu, ALU.mult)
                if b == NB - 1:
                    S.copy('pool', uhalo[:, f, :], ug_[:, T:T + 2])
                    S.copy('pool', uhalo[:, 64 + f, :], uu_[:, T:T + 2])
            proj2(w_fu, g * GF * 128, KC, h3rhs,
                  w_fu, DFF + g * GF * 128, KC, h3rhs, GF * 128, FB, sink_f)

            def sink_acc(cc, b, ps, m, g=g):
                dst = ACC[:, cc, b * BW:(b + 1) * BW]
                if g == 0:
                    S.copy('act', dst, ps)
                else:
                    S.tt('dve', dst, dst, ps, ALU.add)
            proj_fm(w_fd[g * GF * 128:(g + 1) * GF * 128, :], 0, 2048, GF,
                    lambda kc, b: actT[:, kc, b * BW:(b + 1) * BW], TB, sink_acc)

        def dst_out(fc, res):
            S.tt('dve', ACC[:, fc, :], ACC[:, fc, :], res, ALU.add)
            S.dma('sp', outT[fc * 128:(fc + 1) * 128, t0:t0 + T], ACC[:, fc, :], 'out')
        post_norm_residual(ACC, 6, lambda fc: xload(xsp[seg][fc * 128:(fc + 1) * 128, :])[:, 0:T], dst_out)
        S.barrier()


    A.release(base_mark)
    SfA = A.alloc([128, 8, 128], F32, "SfA")
    SbA = [A.alloc([128, 8, 256], BF16, f"SbA{i}") for i in range(2)]
    scurA = [0] * 8
    S.memset('dve', SfA, 0.0)
    S.memset('dve', SbA[0], 0.0)
    S.memset('dve', SbA[1], 0.0)
    for hp in range(8):
        S.copy('dve', SfA[0:64, hp, 64:128], ident_f[0:64, 0:64])
        S.copy('dve', SfA[64:128, hp, 64:128], ident_f[64:128, 64:128])
        S.copy('dve', SbA[0][0:64, hp, 64:128], ident_f[0:64, 0:64])
        S.copy('dve', SbA[0][64:128, hp, 192:256], ident_f[64:128, 64:128])
    ph_mark = A.mark()
    for seg in range(NSEG):
        seg_AB(seg, 'A')
    A.release(ph_mark)
    S.dma('sp', summR.rearrange("(hp p) c -> p hp c", p=128), SfA, 'ex0')
    Cex = A.alloc([128, 4, 258], F32, "Cex")
    S.copy('dve', Cex[:, :, 0:257], Cst_f)
    S.copy('dve', Cex[:, :, 257], ebtot)
    S.dma('sp', summM.rearrange("(h p) c -> p h c", p=128), Cex, 'ex1')
    S.add('pool', lambda e: e.collective_compute("AllGather", ALU.bypass, replica_groups=[[0, 1, 2, 3], [4, 5, 6, 7]],
                                                 ins=[summR], outs=[gathR]),
          r=[summR], w=[gathR], dma=('cc', 'cc1'))
    S.add('pool', lambda e: e.collective_compute("AllGather", ALU.bypass, replica_groups=[[0, 1, 2, 3], [4, 5, 6, 7]],
                                                 ins=[summM], outs=[gathM]),
          r=[summM], w=[gathM], dma=('cc', 'cc1b'))
    ccd = A.alloc([128, 2], F32, "ccd")
    S.add('pool', lambda e: e.memset(U(ccd), 0.0), r=[gathR, gathM], w=[gathR, gathM, ccd])
    GR = A.alloc([128, 8, 128], F32, "GR")
    GM = A.alloc([128, 4, 258], F32, "GM")
    XA = A.alloc([128, 128], F32, "XA")
    XAT = A.alloc([128, 128], BF16, "XAT")
    Sp = A.alloc([128, 64], F32, "Sp")
    Cp = A.alloc([128, 257], F32, "Cp")
    S.memset('dve', Cst_f, 0.0)
    S.memset('dve', XA, 0.0)
    for r in range(3):
        S.dma('sp', GR, gathR[r * 1024:(r + 1) * 1024, :].rearrange("(hp p) c -> p hp c", p=128), 'gr')
        S.dma('sp', GM, gathM[r * 512:(r + 1) * 512, :].rearrange("(h p) c -> p h c", p=128), 'gm')
        for hp in range(8):
            S.copy('dve', XA[0:64, 0:64], GR[0:64, hp, 64:128])
            S.copy('dve', XA[64:128, 64:128], GR[64:128, hp, 64:128])
            ptr = bank()
            S.tr(ptr[:, 0:128], XA, ident_f)
            S.copy('act', XAT, ptr[:, 0:128])
            pf = bank()
            S.mm(pf[:, 0:128], XAT, Sst_b[scur[hp]][:, hp, :])
            for h in range(2):
                hs = slice(h * 64, (h + 1) * 64)
                S.tt('dve', Sp[hs, :], pf[hs, h * 64:(h + 1) * 64], GR[hs, hp, 0:64], ALU.add)
            S.tt('dve', Sp, Sp, Sst_f[:, hp, :], ALU.subtract)
            S.stt(Sst_f[:, hp, :], Sp, cmask[:, r:r + 1], Sst_f[:, hp, :], ALU.mult, ALU.add)
            S.copy('act', Sst_b[scur[hp]][0:64, hp, 0:64], Sst_f[0:64, hp, :])
            S.copy('act', Sst_b[scur[hp]][64:128, hp, 64:128], Sst_f[64:128, hp, :])
        for hh in range(4):
            S.stt(Cp, Cst_f[:, hh, :], GM[:, hh, 257:258], GM[:, hh, 0:257], ALU.mult, ALU.add)
            S.tt('dve', Cp, Cp, Cst_f[:, hh, :], ALU.subtract)
            S.stt(Cst_f[:, hh, :], Cp, cmask[:, r:r + 1], Cst_f[:, hh, :], ALU.mult, ALU.add)
    S.copy('act', Cst_b, Cst_f)
    S.barrier()

    A.release(base_mark)
    ph_mark = A.mark()
    for seg in range(NSEG):
        seg_AB(seg, 'B')
    A.release(ph_mark)
    S.add('pool', lambda e: e.collective_compute("AllGather", ALU.bypass, replica_groups=[[0, 1, 2, 3], [4, 5, 6, 7]],
                                                 ins=[hin], outs=[hout]),
          r=[hin], w=[hout], dma=('cc', 'cc2'))
    ccd2 = A.alloc([128, 2], F32, "ccd2")
    S.add('pool', lambda e: e.memset(U(ccd2), 0.0), r=[hout], w=[hout, ccd2])
    HG = A.alloc([128, 4, 32], F32, "HG")
    hacc = A.alloc([128, 32], F32, "hacc")
    S.dma('sp', HG, hout.rearrange("(r p) c -> p r c", p=128), 'hg')
    S.ts('dve', hacc, HG[:, 0, :], csel[:, 0:1], None, ALU.mult)
    for r in range(1, 4):
        S.stt(hacc, HG[:, r, :], csel[:, r:r + 1], hacc, ALU.mult, ALU.add)
    S.copy('dve', h3halo.rearrange("p a t -> p (a t)"), hacc)
    S.barrier()

    for seg in range(NSEG):
        seg_C(seg)
    S.finish()
    S.emit()
    S.stack.close()
    print("arena peak words", A.peak, "ops", {e: len(v) for e, v in S.ops.items()})
    return nc


_CACHE = {}


def _consts(T):
    r = np.arange(128)
    ident = np.eye(128, dtype=np.float32)
    ones = np.ones((128, 128), np.float32)
    m_su = (r[None, :] > r[:, None]).astype(np.float32)
    m_iu = (r[None, :] >= r[:, None]).astype(np.float32)
    m_sl = (r[None, :] < r[:, None]).astype(np.float32)
    blk = ((r[None, :] // 64) == (r[:, None] // 64)).astype(np.float32)
    cst = np.concatenate([ident, ones, m_su, m_iu, m_sl, blk], axis=1)
    mreset = np.ones((128, T), np.float32)
    mreset[:, ::128] = 0.0
    return np.ascontiguousarray(cst), mreset


def colmaj(v, nch):
    return np.ascontiguousarray(np.asarray(v, np.float32).reshape(nch, 128).T)


def prepare_shared(inp, T):
    f = lambda k: np.ascontiguousarray(np.asarray(inp[k], np.float32)[0])
    cst, mreset = _consts(T)
    gnames = ["mix_pre_norm", "mix_post_norm", "xattn_pre_norm", "mem_norm", "xattn_post_norm", "ffn_pre_norm", "ffn_post_norm"]
    gcols = np.concatenate([colmaj(f(n), 16) for n in gnames], axis=1)
    bif = np.ascontiguousarray(np.broadcast_to(np.concatenate([f("mlstm_b_i"), f("mlstm_b_f")])[None, :], (128, 8)))
    hncols = colmaj(f("mlstm_head_norm"), 8)
    mu = f("rwkv_mu")
    rvec = np.concatenate([colmaj(mu[0:1024], 8), colmaj(mu[1024:2048], 8), colmaj(mu[2048:3072], 8),
                           colmaj(f("rwkv_w0"), 8), colmaj(f("rwkv_a0"), 8), colmaj(f("rwkv_k_k"), 8),
                           colmaj(f("rwkv_k_a"), 8), colmaj(f("rwkv_r_k").reshape(-1), 8),
                           np.zeros((128, 8), np.float32)], axis=1)
    mulr = np.zeros((128, 4), np.float32)
    mulr[0:64, 0] = mu[3072:3136]
    mulr[64:128, 0] = mu[3136:3200]
    mulr[:, 1] = mu[3200:3328]
    mulr[0:32, 2] = mu[3328:3360]
    lngb = np.ascontiguousarray(np.broadcast_to(np.concatenate([f("rwkv_ln_g"), f("rwkv_ln_b")])[None, :], (128, 2048)))
    cw = f("ffn_conv_w")
    convc = np.concatenate([colmaj(cw[0], 128), colmaj(cw[1], 128), colmaj(cw[2], 128), colmaj(f("ffn_conv_b"), 128)], axis=1)
    return {
        "w_in": f("w_in"), "w_ba": f("w_branch_a"), "w_bb": f("w_branch_b"), "w_mo": f("w_mix_out"),
        "w_q": f("xattn_wq"), "w_kv": f("xattn_wkv"), "w_o": f("xattn_wo"), "w_fu": f("ffn_w_up"),
        "w_fd": f("ffn_w_down"), "rw_up": f("rwkv_w_up"), "ra_up": f("rwkv_a_up"), "rg_up": f("rwkv_g_up"),
        "gcols": gcols, "bif": bif, "hncols": hncols, "rvec": np.ascontiguousarray(rvec), "mulr": mulr,
        "lngb": lngb, "convc": np.ascontiguousarray(convc), "cst": cst, "mreset": mreset,
    }


T_SEG = 512
NSEG_CORE = 2
TOK_CORE = T_SEG * NSEG_CORE


def run(inputs, debug_taps=()):
    x = np.asarray(inputs["x"], np.float32)
    mem = np.asarray(inputs["mem"], np.float32)
    B, SQ, _ = x.shape
    G = SQ // TOK_CORE
    assert B * G == 8 and G == 4
    nc = build_program(NSEG_CORE, T_SEG, debug_taps)
    shared = prepare_shared(inputs, T_SEG)
    in_maps = []
    for c in range(8):
        b, g = c // G, c % G
        m = dict(shared)
        m["xT"] = np.ascontiguousarray(x[b, g * TOK_CORE:(g + 1) * TOK_CORE].T)
        m["memT"] = np.ascontiguousarray(mem[b].T)
        prev = x[b, g * TOK_CORE - 1] if g > 0 else np.zeros((D,), np.float32)
        m["xh"] = colmaj(prev, 16)
        cm = np.zeros((128, 4), np.float32)
        cm[:, :g] = 1.0
        cs = np.zeros((128, 4), np.float32)
        if g > 0:
            cs[:, g - 1] = 1.0
        m["cmask"] = cm
        m["csel"] = cs
        in_maps.append(m)
    res = run_bass_kernel_spmd(nc, in_maps, core_ids=list(range(8)))
    return res


def kernel(**inputs):
    x = np.asarray(inputs["x"], np.float32)
    B, SQ, _ = x.shape
    G = SQ // TOK_CORE
    res = run(inputs)
    out = np.empty((B, SQ, D), np.float32)
    for c in range(8):
        b, g = c // G, c % G
        out[b, g * TOK_CORE:(g + 1) * TOK_CORE] = res.results[c]["outT"].T
    return out
```

```python
import numpy as np
import contextlib
import concourse.bass as bass
import concourse.mybir as mybir
from concourse.bass_utils import run_bass_kernel_spmd

F32 = mybir.dt.float32
BF16 = mybir.dt.bfloat16
AF = mybir.ActivationFunctionType
ALU = mybir.AluOpType
AX = mybir.AxisListType

D = 2048
KC = 16
SEQ = 4096
BATCH = 2
MEM = 256
IN_COLS = 10536
DFF = 8192
EPS = 1e-6
GN_EPS = 64e-5
O_Q, O_K, O_V, O_O, O_I, O_F = 0, 512, 1024, 2048, 3072, 3076
O_RW = 3080
O_RR, O_RK, O_RV, O_RWL, O_RAL, O_RGL = O_RW, O_RW + 1024, O_RW + 2048, O_RW + 3072, O_RW + 3136, O_RW + 3200
O_GA = O_RW + 3360
O_GB = O_GA + 2048


class V:
    __slots__ = ('ap', 'key')

    def __init__(self, ap, key):
        self.ap = ap
        self.key = key

    def __getitem__(self, idx):
        return V(self.ap[idx], self.key)

    def rearrange(self, *a, **k):
        return V(self.ap.rearrange(*a, **k), self.key)

    def bitcast(self, dt):
        return V(self.ap.bitcast(dt), self.key)

    def to_broadcast(self, shape):
        return V(self.ap.to_broadcast(shape), self.key)

    def k(self, sub):
        return V(self.ap, (self.key, sub))


def U(x):
    return x.ap if isinstance(x, V) else x


class Sched:
    ENGS = ['pe', 'act', 'dve', 'pool', 'sp']
    SAME_ENG_SYNC = {'act', 'dve', 'pool'}

    def __init__(self, nc):
        self.nc = nc
        self.ops = {e: [] for e in self.ENGS}
        self.last_real = {e: None for e in self.ENGS}
        self.last_w = {}
        self.reads = {}
        self.dma_cnt = {}
        self.stack = contextlib.ExitStack()

    @staticmethod
    def _key(x):
        if isinstance(x, V):
            return x.key
        if isinstance(x, (str, tuple)):
            return x
        return x.name

    def add(self, eng, fn, r=(), w=(), dma=None):
        deps = {}

        def dep(ev):
            if ev is None:
                return
            k = ev[:2]
            if deps.get(k, -1) < ev[2]:
                deps[k] = ev[2]
        rk = [self._key(x) for x in r]
        wk = [self._key(x) for x in w]
        for k in rk:
            dep(self.last_w.get(k))
        for k in wk:
            dep(self.last_w.get(k))
            for kk, v in self.reads.get(k, {}).items():
                dep(kk + (v,))
        idx = len(self.ops[eng])
        if dma is None:
            ev = ('op', eng, idx)
            self.last_real[eng] = idx
        else:
            self.dma_cnt[dma] = self.dma_cnt.get(dma, 0) + (1 if isinstance(dma, tuple) else 16)
            ev = ('dma', dma, self.dma_cnt[dma])
        for k in rk:
            d = self.reads.setdefault(k, {})
            if d.get(ev[:2], -1) < ev[2]:
                d[ev[:2]] = ev[2]
        for k in wk:
            self.last_w[k] = ev
            self.reads[k] = {}
        self.ops[eng].append(dict(fn=fn, deps=deps, dma=dma, signal=False, waits=[]))
        return ev

    def barrier(self):
        for e in self.ENGS:
            deps = {}
            for e2 in self.ENGS:
                if e2 != e and self.last_real[e2] is not None:
                    deps[('op', e2)] = self.last_real[e2]
            for s, v in self.dma_cnt.items():
                if isinstance(s, tuple) and e != 'pool':
                    continue
                deps[('dma', s)] = v
            self.ops[e].append(dict(fn=None, deps=deps, dma=None, signal=False, waits=[]))
        self.last_w = {}
        self.reads = {}

    def finish(self):
        self.barrier()
        for e in self.ENGS:
            seen = {}
            for i, op in enumerate(self.ops[e]):
                for k, v in op['deps'].items():
                    if k[0] == 'op':
                        if self.ops[k[1]][v]['dma'] is not None or self.ops[k[1]][v]['fn'] is None:
                            continue
                        if k[1] == e and e not in self.SAME_ENG_SYNC:
                            continue
                    if seen.get(k, -1) >= v:
                        continue
                    seen[k] = v
                    op['waits'].append((k, v))
                    if k[0] == 'op':
                        self.ops[k[1]][v]['signal'] = True
        self.sigval = {}
        for e in self.ENGS:
            c = 0
            for i, op in enumerate(self.ops[e]):
                if op['signal']:
                    c += 1
                    self.sigval[(e, i)] = c

    def emit(self):
        nc = self.nc
        st = self.stack
        esem = {e: st.enter_context(nc.semaphore(f"sem_{e}")) for e in self.ENGS}
        dsem = {s: st.enter_context(nc.semaphore("dsem_" + (s if isinstance(s, str) else "_".join(s)))) for s in self.dma_cnt}
        block = st.enter_context(nc.Block())
        names = {'pe': 'tensor', 'act': 'scalar', 'dve': 'vector', 'pool': 'gpsimd', 'sp': 'sync'}
        sched = self

        def make(e):
            def body(eng):
                for i, op in enumerate(sched.ops[e]):
                    for k, v in op['waits']:
                        if k[0] == 'op':
                            eng.wait_ge(esem[k[1]], sched.sigval[(k[1], v)])
                        else:
                            eng.wait_ge(dsem[k[1]], v)
                    if op['fn'] is None:
                        continue
                    ins = op['fn'](eng)
                    if isinstance(op['dma'], tuple):
                        ins.then_inc(dsem[op['dma']])
                    elif op['dma'] is not None:
                        ins.then_inc(dsem[op['dma']], 16)
                    elif op['signal']:
                        ins.then_inc(esem[e], 1)
            return body
        for e in self.ENGS:
            getattr(block, names[e])(make(e))

    def mm(self, out, lhsT, rhs, start=True, stop=True):
        o, l, rh = U(out), U(lhsT), U(rhs)
        return self.add('pe', lambda e: e.matmul(o, l, rh, start=start, stop=stop), r=[lhsT, rhs], w=[out])

    def tr(self, out, in_, ident):
        o, i, d = U(out), U(in_), U(ident)
        return self.add('pe', lambda e: e.transpose(o, i, d), r=[in_, ident], w=[out])

    def act(self, out, in_, func, bias=None, scale=None, accum_out=None):
        kw = {}
        rr = [in_]
        ww = [out]
        if bias is not None:
            kw['bias'] = U(bias)
            if isinstance(bias, V):
                rr.append(bias)
        if scale is not None:
            kw['scale'] = U(scale)
            if isinstance(scale, V):
                rr.append(scale)
        if accum_out is not None:
            kw['accum_out'] = U(accum_out)
            ww.append(accum_out)
        o, i = U(out), U(in_)
        return self.add('act', lambda e: e.activation(o, i, func, **kw), r=rr, w=ww)

    def tt(self, eng, out, a, b, op):
        o, aa, bb = U(out), U(a), U(b)
        return self.add(eng, lambda e: e.tensor_tensor(o, aa, bb, op), r=[a, b], w=[out])

    def ts(self, eng, out, a, s1, s2, op0, op1=None, accum_out=None):
        rr = [a] + [s for s in (s1, s2) if isinstance(s, V)]
        ww = [out] + ([accum_out] if accum_out is not None else [])
        kw = {}
        if accum_out is not None:
            kw['accum_out'] = U(accum_out)
        o, aa, u1, u2 = U(out), U(a), U(s1), U(s2)
        if op1 is None:
            fn = lambda e: e.tensor_scalar(o, aa, u1, None, op0, **kw)
        else:
            fn = lambda e: e.tensor_scalar(o, aa, u1, u2, op0, op1, **kw)
        return self.add(eng, fn, r=rr, w=ww)

    def stt(self, out, a, s, b, op0, op1, accum_out=None):
        rr = [a, b] + ([s] if isinstance(s, V) else [])
        ww = [out] + ([accum_out] if accum_out is not None else [])
        kw = {}
        if accum_out is not None:
            kw['accum_out'] = U(accum_out)
        o, aa, ss, bb = U(out), U(a), U(s), U(b)
        return self.add('dve', lambda e: e.scalar_tensor_tensor(o, aa, ss, bb, op0, op1, **kw), r=rr, w=ww)

    def copy(self, eng, out, in_):
        o, i = U(out), U(in_)
        if eng == 'act':
            fn = lambda e: e.copy(o, i)
        else:
            fn = lambda e: e.tensor_copy(o, i)
        return self.add(eng, fn, r=[in_], w=[out])

    def memset(self, eng, out, val):
        o = U(out)
        return self.add(eng, lambda e: e.memset(o, val), r=[], w=[out])

    def recip(self, out, in_):
        o, i = U(out), U(in_)
        return self.add('dve', lambda e: e.reciprocal(o, i), r=[in_], w=[out])

    def reduce(self, out, in_, op=ALU.add):
        o, i = U(out), U(in_)
        return self.add('dve', lambda e: e.tensor_reduce(o, i, AX.X, op), r=[in_], w=[out])

    def scan(self, out, d0, d1, init, op0, op1):
        o, a, b = U(out), U(d0), U(d1)
        return self.add('dve', lambda e: e.tensor_tensor_scan(o, a, b, init, op0, op1), r=[d0, d1], w=[out])

    def dma(self, q, out, in_, sem):
        o, i = U(out), U(in_)
        return self.add(q, lambda e: e.dma_start(out=o, in_=i), r=[in_], w=[out], dma=sem)


class Arena:
    def __init__(self, S, words):
        self.S = S
        self.words = words
        self.t = S.stack.enter_context(S.nc.sbuf_tensor("arena", [128, words], F32))
        self.off = 0
        self.n = 0
        self.peak = 0
        self.limit = words

    def alloc(self, shape, dtype, name=None):
        free = int(np.prod(shape[1:]))
        w = free if dtype == F32 else (free + 1) // 2
        assert self.off + w <= self.limit, f"arena overflow {self.off}+{w}>{self.limit} ({name})"
        ap = self.t[:, self.off:self.off + w]
        if dtype != F32:
            ap = ap.bitcast(dtype)[:, 0:free]
        self.off += w
        self.peak = max(self.peak, self.off)
        self.n += 1
        key = f"{name or 't'}@{self.off - w}"
        if len(shape) == 3:
            ap = ap.rearrange("p (a b) -> p a b", b=shape[2])
        elif len(shape) == 4:
            ap = ap.rearrange("p (a b c) -> p a b c", b=shape[2], c=shape[3])
        return V(ap, key)

    def alloc_top(self, shape, dtype, name):
        free = int(np.prod(shape[1:]))
        assert dtype == F32
        save = self.off
        self.off = self.words - free
        v = self.alloc(shape, dtype, name)
        self.top_off = self.words - free
        self.off = save
        return v

    def mark(self):
        return self.off

    def release(self, m):
        self.off = m


STOP = None


def build_program(NSEG, T, debug_taps=()):
    NT = T // 128
    BW = min(512, T)
    NB = T // BW
    TT = NSEG * T
    nc = bass.Bass("TRN2", target_bir_lowering=False)
    S = Sched(nc)

    def din(name, shape):
        return nc.dram_tensor(name, list(shape), F32, kind="ExternalInput").ap()
    xT = din("xT", [D, TT])
    memT = din("memT", [D, MEM])
    w_in = din("w_in", [D, IN_COLS])
    w_ba = din("w_ba", [1024, D])
    w_bb = din("w_bb", [1024, D])
    w_mo = din("w_mo", [D, D])
    w_q = din("w_q", [D, 512])
    w_kv = din("w_kv", [D, 1024])
    w_o = din("w_o", [512, D])
    w_fu = din("w_fu", [D, 2 * DFF])
    w_fd = din("w_fd", [DFF, D])
    rw_up = din("rw_up", [64, 1024])
    ra_up = din("ra_up", [64, 1024])
    rg_up = din("rg_up", [160, 1024])
    gcols_d = din("gcols", [128, 7 * 16])
    bif_d = din("bif", [128, 8])
    hncols_d = din("hncols", [128, 8])
    rvec_d = din("rvec", [128, 72])
    mulr_d = din("mulr", [128, 4])
    lngb_d = din("lngb", [128, 2048])
    convc_d = din("convc", [128, 512])
    cst_d = din("cst", [128, 6 * 128])
    mreset_d = din("mreset", [128, T])
    outT = nc.dram_tensor("outT", [D, TT], F32, kind="ExternalOutput").ap()
    xsp = [nc.dram_tensor(f"xspill{i}", [D, T], F32).ap() for i in range(NSEG)]
    cmask_d = din("cmask", [128, 4])
    csel_d = din("csel", [128, 4])
    xh_d = din("xh", [128, 16])
    SROWS = 1536
    summR = nc.dram_tensor("summR", [1024, 128], F32).ap()
    gathR = nc.dram_tensor("gathR", [4 * 1024, 128], F32).ap()
    summM = nc.dram_tensor("summM", [512, 258], F32).ap()
    gathM = nc.dram_tensor("gathM", [4 * 512, 258], F32).ap()
    hin = nc.dram_tensor("hin", [128, 32], F32).ap()
    NIDX = NSEG * 8
    scr_kT = nc.dram_tensor("scr_kT", [NSEG, 128, 4, T], BF16).ap()
    scr_vt = nc.dram_tensor("scr_vt", [NSEG, 128, NT * 4 * 257], BF16).ap()
    scr_gs = nc.dram_tensor("scr_gs", [NSEG, 128, 16 * NT + 8], F32).ap()
    NHC_ = 2 * NT
    scr_rt = nc.dram_tensor("scr_rt", [NIDX, 128, T], BF16).ap()
    scr_G = nc.dram_tensor("scr_G", [NIDX, 128, NHC_, 256], BF16).ap()
    scr_WT = nc.dram_tensor("scr_WT", [NIDX, 128, NT, 128], BF16).ap()
    scr_Upr = nc.dram_tensor("scr_Upr", [NIDX, 128, NHC_, 64], F32).ap()
    scr_tok = nc.dram_tensor("scr_tok", [NIDX, 128, 3, NT, 128], BF16).ap()
    scr_gt = nc.dram_tensor("scr_gt", [NIDX, 128, NT, 128], F32).ap()
    scr_sm = nc.dram_tensor("scr_sm", [NIDX, 128, 3 * NT], F32).ap()
    hout = nc.dram_tensor("hout", [512, 32], F32).ap()
    taps = {}
    for nm, shp in debug_taps:
        taps[nm] = nc.dram_tensor("dbg_" + nm, list(shp), F32, kind="ExternalOutput").ap()

    A = Arena(S, 42 * 1024)
    PB = [V(S.stack.enter_context(nc.psum_tensor(f"PB{i}", [128, 512], F32))[:, :], f"PB{i}") for i in range(8)]
    pbi = [0]

    def bank():
        b = PB[pbi[0] % 8]
        pbi[0] += 1
        return b

    cst_bf = A.alloc([128, 6 * 128], BF16, "cstbf")
    cst_f = A.alloc([128, 6 * 128], F32, "cstf")
    ident = cst_bf[:, 0:128]
    ones_bf = cst_bf[:, 128:256]
    m_su, m_iu, m_sl = cst_bf[:, 256:384], cst_bf[:, 384:512], cst_bf[:, 512:640]
    ident_f = cst_f[:, 0:128]
    ones_f = cst_f[:, 128:256]
    tri_f = cst_f[:, 384:512]
    blk64_f = cst_f[:, 640:768]
    mask4 = A.alloc([128, 512], BF16, "mask4")
    mreset = A.alloc([128, T], F32, "mreset")
    gcols = A.alloc([128, 112], F32, "gcols")
    bif = A.alloc([128, 8], F32, "bif")
    hncols = A.alloc([128, 8], F32, "hncols")
    rvec = A.alloc([128, 72], F32, "rvec")
    omka = A.alloc([128, 8], F32, "omka")
    mulr = A.alloc([128, 4], F32, "mulr")
    lngb = A.alloc([128, 2048], F32, "lngb")
    convc = A.alloc([128, 512], F32, "convc")
    wup_bf = A.alloc([128, 1024], BF16, "wupbf")
    aup_bf = A.alloc([128, 1024], BF16, "aupbf")
    gup_bf = A.alloc([128, 2, 1024], BF16, "gupbf")
    Cst_f = A.alloc([128, 4, 257], F32, "Cst_f")
    Cst_b = A.alloc([128, 4, 257], BF16, "Cst_b")
    Sst_f = A.alloc([128, 8, 64], F32, "Sst_f")
    Sst_b = [A.alloc([128, 8, 128], BF16, f"Sst_b{i}") for i in range(2)]
    scur = [0] * 8
    hhalo = A.alloc([128, 16], BF16, "hhalo")
    hhalo0 = A.alloc([128, 16], BF16, "hhalo0")
    ebtot = A.alloc([128, 4], F32, "ebtot")
    cmask = A.alloc([128, 4], F32, "cmask")
    csel = A.alloc([128, 4], F32, "csel")
    xh = A.alloc([128, 16], F32, "xh")
    h3halo = A.alloc([128, 16, 2], BF16, "h3halo")
    uhalo = A.alloc([128, 128, 2], F32, "uhalo")
    eps_t = A.alloc([128, 2], F32, "eps")
    NWB = 4
    GW = 128
    WB = [A.alloc([128, 16 * GW], BF16, f"WB{i}") for i in range(NWB)]
    wbi = [0]
    xbufs = [A.alloc([128, T], F32, f"xbuf{i}") for i in range(2)]
    xbi = [0]

    for (dst, src) in ((cst_f, cst_d), (mreset, mreset_d), (gcols, gcols_d), (bif, bif_d), (hncols, hncols_d),
                       (rvec, rvec_d), (mulr, mulr_d), (lngb, lngb_d), (convc, convc_d),
                       (cmask, cmask_d), (csel, csel_d), (xh, xh_d)):
        S.dma('sp', dst, src, 'cst_' + dst.key)
    S.dma('pool', cst_bf, cst_d, 'cstb0')
    S.dma('pool', wup_bf[0:64, :], rw_up, 'cstb1')
    S.dma('pool', aup_bf[64:128, :], ra_up, 'cstb2')
    S.dma('pool', gup_bf[:, 0, :], rg_up[0:128, :], 'cstb3')
    S.dma('pool', gup_bf[0:32, 1, :], rg_up[128:160, :], 'cstb4')
    S.ts('dve', omka, rvec[:, 48:56], -1.0, 1.0, ALU.mult, ALU.add)
    for t_ in (Cst_f, Cst_b, Sst_f, Sst_b[0], Sst_b[1], hhalo, uhalo):
        S.memset('dve', t_, 0.0)
    S.memset('dve', eps_t[:, 0:1], EPS)
    S.memset('dve', eps_t[:, 1:2], GN_EPS)
    S.copy('dve', mask4[:, 0:128], m_su)
    S.copy('dve', mask4[:, 128:256], m_su)
    S.copy('dve', mask4[:, 256:384], m_iu)
    S.copy('dve', mask4[:, 384:512], m_iu)

    def tap(nm, v):
        if nm in taps:
            S.dma('pool', taps[nm], v, 'tap')

    def wload(src_ap, kch, ncols, prow=128):
        assert kch * ncols <= 16 * GW
        wb = WB[wbi[0] % NWB]
        wbi[0] += 1
        view = wb[:, 0:kch * ncols].rearrange("p (k n) -> p k n", n=ncols)
        S.dma('pool', view[0:prow], src_ap.rearrange("(kc p) n -> p kc n", p=prow), wb.key)
        return view

    def xload(src_ap):
        xb = xbufs[xbi[0] % 2]
        xbi[0] += 1
        S.dma('sp', xb[:, 0:src_ap.shape[1]], src_ap, xb.key)
        return xb

    def plan(ncol, kch, nsrc, nblk):
        chunks = max(1, 4 // (nsrc * nblk))
        GC = min(128 * chunks, ((ncol + 127) // 128) * 128)
        KP = max(1, min(kch, (16 * GW) // GC))
        return GC, KP

    def proj_fm(wsrc, col0, ncol, kch, rhs, blocks, sink):
        GC, KP = plan(ncol, kch, 1, len(blocks))
        loads = []
        for c in range(0, ncol, GC):
            g = min(GC, ncol - c)
            for k0 in range(0, kch, KP):
                loads.append((c, g, k0, min(KP, kch - k0)))
        def ld(l):
            c_, g_, k_, kp_ = l
            return wload(wsrc[k_ * 128:(k_ + kp_) * 128, col0 + c_: col0 + c_ + g_], kp_, g_)
        nxt = [ld(loads[0]), ld(loads[1]) if len(loads) > 1 else None]
        pss = None
        for li, (c, g, k0, kp) in enumerate(loads):
            wb = nxt[0]
            nxt = [nxt[1], ld(loads[li + 2]) if li + 2 < len(loads) else None]
            subs = list(range(0, g, 128))
            if k0 == 0:
                pss = {(sub, tag): bank() for sub in subs for (tag, width) in blocks}
            for sub in subs:
                m = min(128, g - sub)
                for (tag, width) in blocks:
                    ps = pss[(sub, tag)]
                    for kk in range(kp):
                        kc = k0 + kk
                        S.mm(ps[0:m, 0:width], wb[:, kk, sub:sub + m], rhs(kc, tag), start=(kc == 0), stop=(kc == kch - 1))
            if k0 + kp >= kch:
                for sub in subs:
                    m = min(128, g - sub)
                    for (tag, width) in blocks:
                        sink((c + sub) // 128, tag, pss[(sub, tag)][0:m, 0:width], m)

    def proj2(ws1, c1, k1, rhs1, ws2, c2, k2, rhs2, ncol, blocks, sink):
        GC, _ = plan(ncol, max(k1, k2), 2, len(blocks))
        srcs = ((ws1, c1, k1, rhs1), (ws2, c2, k2, rhs2))
        loads = []
        for c in range(0, ncol, GC):
            g = min(GC, ncol - c)
            for si, (ws, cb, kch, rh) in enumerate(srcs):
                KP = max(1, min(kch, (16 * GW) // GC))
                for k0 in range(0, kch, KP):
                    loads.append((c, g, si, k0, min(KP, kch - k0)))

        def ld(l):
            c, g, si, k0, kp = l
            ws, cb, kch, rh = srcs[si]
            return wload(ws[k0 * 128:(k0 + kp) * 128, cb + c: cb + c + g], kp, g)
        nxt = [ld(loads[0]), ld(loads[1]) if len(loads) > 1 else None]
        pss = {}
        for li, (c, g, si, k0, kp) in enumerate(loads):
            wb = nxt[0]
            nxt = [nxt[1], ld(loads[li + 2]) if li + 2 < len(loads) else None]
            ws, cb, kch, rh = srcs[si]
            subs = list(range(0, g, 128))
            if k0 == 0:
                for sub in subs:
                    for (tag, width) in blocks:
                        pss[(si, sub, tag)] = bank()
            for sub in subs:
                for (tag, width) in blocks:
                    ps = pss[(si, sub, tag)]
                    for kk in range(kp):
                        kc = k0 + kk
                        S.mm(ps[:, 0:width], wb[:, kk, sub:sub + 128], rh(kc, tag), start=(kc == 0), stop=(kc == kch - 1))
            if si == 1 and k0 + kp >= kch:
                for sub in subs:
                    for (tag, width) in blocks:
                        sink((c + sub) // 128, tag, pss[(0, sub, tag)][:, 0:width], pss[(1, sub, tag)][:, 0:width])

    def rms_stats(getx, bw, ncols):
        rstd = A.alloc([128, ncols], F32, "rstd")
        sqs = [A.alloc([128, ncols], BF16, f"sq{i}") for i in range(2)]
        nb = ncols // bw
        pss = [bank() for _ in range(nb)]
        for fc in range(KC):
            sq = sqs[fc % 2]
            S.act(sq, getx(fc), AF.Square)
            for b in range(nb):
                S.mm(pss[b][:, 0:bw], ones_bf, sq[:, b * bw:(b + 1) * bw], start=(fc == 0), stop=(fc == KC - 1))
        for b in range(nb):
            sl = rstd[:, b * bw:(b + 1) * bw]
            S.act(sl, pss[b][:, 0:bw], AF.Sqrt, bias=eps_t[:, 0:1], scale=1.0 / D)
            S.recip(sl, sl)
        return rstd

    TB = [(b, BW) for b in range(NB)]
    R = A.alloc_top([128, 16, T], F32, "R")
    S.memset('dve', ebtot, 1.0)
    S.memset('dve', h3halo, 0.0)
    hsq = A.alloc([128, 16], F32, "hsq")
    hss = A.alloc([128, 2], F32, "hss")
    S.tt('dve', hsq, xh, xh, ALU.mult)
    S.reduce(hss[:, 0:1], hsq)
    ps = bank()
    S.mm(ps[:, 0:1], ones_f, hss[:, 0:1])
    S.act(hss[:, 1:2], ps[:, 0:1], AF.Sqrt, bias=eps_t[:, 0:1], scale=1.0 / D)
    S.recip(hss[:, 1:2], hss[:, 1:2])
    S.tt('dve', hsq, xh, gcols[:, 0:16], ALU.mult)
    S.ts('dve', hhalo0, hsq, hss[:, 1:2], None, ALU.mult)
    base_mark = A.mark()

    def stop_at(nm):
        pass

    def post_norm_residual(Y, gidx, getres, dst_fn):
        m_ = A.mark()
        rs = rms_stats(lambda fc: Y[:, fc, :], BW, T)
        for fc in range(KC):
            S.stt(Y[:, fc, :], Y[:, fc, :], gcols[:, gidx * 16 + fc: gidx * 16 + fc + 1], rs, ALU.mult, ALU.mult)
            dst_fn(fc, getres(fc))
        S.barrier()
        A.release(m_)
    def seg_AB(seg, mode):
        t0 = seg * T
        A.release(ph_mark)
        A.limit = A.words
        seg_mark = A.mark()
        FULL = (mode == 'B')
        if seg == 0:
            S.copy('dve', hhalo, hhalo0)
        hT = A.alloc([128, 16, T + 1], BF16, "hT")
        haT = A.alloc([128, 8, T], BF16, "haT")
        hbT = A.alloc([128, 8, T], BF16, "hbT")
        m_h = A.mark()
        rstd = rms_stats(lambda fc: xload(xT[fc * 128:(fc + 1) * 128, t0:t0 + T])[:, 0:T], BW, T)
        for fc in range(KC):
            xb = xload(xT[fc * 128:(fc + 1) * 128, t0:t0 + T])
            S.stt(hT[:, fc, 1:T + 1], xb[:, 0:T], gcols[:, fc:fc + 1], rstd, ALU.mult, ALU.mult)
        S.copy('dve', hT[:, :, 0], hhalo)
        S.barrier()
        A.release(m_h)
        m_mix = A.mark()

        def hrhs(kc, b):
            if b == 'h':
                return hT[:, kc, 0:1]
            return hT[:, kc, 1 + b * BW: 1 + (b + 1) * BW]

        qT = A.alloc([128, 4, T], BF16, "qT")
        kT = A.alloc([128, 4, T], BF16, "kT")
        soT = A.alloc([128, 8, T], BF16, "soT")
        vtok = A.alloc([128, NT, 4, 257], BF16, "vtok")
        gates = A.alloc([128, NT, 8], F32, "gates")
        graw = A.alloc([128, NT, 8], F32, "graw")
        S.memset('dve', vtok[:, :, :, 256:257], 1.0)

        def sink_qk(cc, b, ps, m):
            if cc < 4:
                S.copy('act', qT[:, cc, b * BW:(b + 1) * BW], ps)
            else:
                S.act(kT[:, cc - 4, b * BW:(b + 1) * BW], ps, AF.Copy, scale=128 ** -0.5)
        if FULL:
            proj_fm(w_in, O_Q, 512, KC, hrhs, TB, sink_qk)
        else:
            proj_fm(w_in, O_K, 512, KC, hrhs, TB, lambda cc, b, ps, m: sink_qk(cc + 4, b, ps, m))

        def sink_o(cc, b, ps, m):
            S.act(soT[:, cc, b * BW:(b + 1) * BW], ps, AF.Sigmoid)
        if FULL:
            proj_fm(w_in, O_O, 1024, KC, hrhs, TB, sink_o)
        gsm = A.alloc([128, 4, NT, 4], F32, "gsm")
        gsm2 = A.alloc([128, 2, 4], F32, "gsm2")
        bcum, imb, expb, wst = gsm[:, 0], gsm[:, 1], gsm[:, 2], gsm[:, 3]
        bend, ebend = gsm2[:, 0], gsm2[:, 1]
        if not FULL:
            for cb in range(8):
                wb = wload(w_in[:, O_V + cb * 128: O_V + (cb + 1) * 128], KC, 128)
                for tt_ in range(NT):
                    ps = bank()
                    for kc in range(KC):
                        S.mm(ps[:, 0:128], hT[:, kc, 1 + tt_ * 128: 1 + (tt_ + 1) * 128], wb[:, kc, 0:128],
                             start=(kc == 0), stop=(kc == KC - 1))
                    S.copy('act' if tt_ % 2 else 'dve', vtok[:, tt_, cb // 2, (cb % 2) * 128:(cb % 2) * 128 + 128], ps[:, 0:128])
            wb = wload(w_in[:, O_I: O_I + 8], KC, 8)
            for tt_ in range(NT):
                ps = bank()
                for kc in range(KC):
                    S.mm(ps[:, 0:8], hT[:, kc, 1 + tt_ * 128: 1 + (tt_ + 1) * 128], wb[:, kc, 0:8],
                         start=(kc == 0), stop=(kc == KC - 1))
                S.tt('dve', graw[:, tt_, :], ps[:, 0:8], bif, ALU.add)
            S.act(graw, graw, AF.Tanh, scale=1.0 / 15.0)
            S.ts('dve', gates[:, :, 0:4], graw[:, :, 0:4], 15.0, None, ALU.mult)
            S.act(graw[:, :, 4:8], graw[:, :, 4:8], AF.Exp, scale=-15.0)
            S.act(graw[:, :, 4:8], graw[:, :, 4:8], AF.Ln, bias=1.0)
            S.ts('dve', gates[:, :, 4:8], graw[:, :, 4:8], -1.0, None, ALU.mult)
            for tt_ in range(NT):
                ps = bank()
                for j in range(tt_ + 1):
                    S.mm(ps[:, 0:4], (tri_f if j == tt_ else ones_f), gates[:, j, 4:8], start=(j == 0), stop=(j == tt_))
                S.copy('dve', bcum[:, tt_, :], ps[:, 0:4])
            ps = bank()
            for j in range(NT):
                S.mm(ps[:, 0:4], ones_f, gates[:, j, 4:8], start=(j == 0), stop=(j == NT - 1))
            S.copy('dve', bend, ps[:, 0:4])
            S.tt('dve', imb, gates[:, :, 0:4], bcum, ALU.subtract)
            S.act(expb, bcum, AF.Exp)
            for hh in range(4):
                S.act(wst[:, :, hh], imb[:, :, hh], AF.Exp, bias=bend[:, hh:hh + 1])
            S.act(ebend, bend, AF.Exp)
            S.dma('sp', scr_kT[seg], kT, 'st_kT')
            S.dma('sp', scr_vt[seg], vtok.rearrange("p n h v -> p (n h v)"), 'st_vt')
            S.dma('sp', scr_gs[seg][:, 0:16 * NT], gsm.rearrange("p a n h -> p (a n h)"), 'st_gs')
            S.dma('sp', scr_gs[seg][:, 16 * NT:16 * NT + 8], gsm2.rearrange("p a h -> p (a h)"), 'st_gs2')
        else:
            S.dma('sp', kT, scr_kT[seg], 'ld_kT')
            S.dma('sp', vtok.rearrange("p n h v -> p (n h v)"), scr_vt[seg], 'ld_vt')
            S.dma('sp', gsm.rearrange("p a n h -> p (a n h)"), scr_gs[seg][:, 0:16 * NT], 'ld_gs')
            S.dma('sp', gsm2.rearrange("p a h -> p (a h)"), scr_gs[seg][:, 16 * NT:16 * NT + 8], 'ld_gs2')
        diag = [A.alloc([128, 128], F32, f"diag{i}") for i in range(2)]
        Dm = [A.alloc([128, 128], F32, f"Dm{i}") for i in range(2)]
        Pm = [A.alloc([128, 128], BF16, f"Pm{i}") for i in range(3)]
        hmn = [A.alloc([128, 256], BF16, f"hmn{i}") for i in range(2)]
        sc = [A.alloc([128, 8], F32, f"sc{i}") for i in range(2)]
        numts = [A.alloc([128, 257], F32, f"numt{i}") for i in range(2)]
        tmpis = [A.alloc([128, 257], F32, f"tmpi{i}") for i in range(2)]
        junk = A.alloc([128, 256], F32, "junk")
        kw_t = [A.alloc([128, 128], BF16, f"kw{i}") for i in range(2)]
        cnt = 0
        for hh in range(4):
            for jt in (range(NT) if FULL else []):
                dg = diag[cnt % 2]
                S.ts('dve', dg, ident_f, bcum[:, jt, hh:hh + 1], None, ALU.mult)
                pbrow = bank()
                S.mm(pbrow[:, 0:128], ones_f, dg)
                pnum = bank()
                for js in range(jt + 1):
                    pst = bank()
                    S.mm(pst[:, 0:128], kT[:, hh, js * 128:(js + 1) * 128], qT[:, hh, jt * 128:(jt + 1) * 128])
                    dm = Dm[js % 2]
                    S.act(dm, pbrow[:, 0:128], AF.Exp, bias=imb[:, js, hh:hh + 1])
                    if js == jt:
                        S.tt('pool', dm, dm, m_iu, ALU.mult)
                    pm = Pm[js % 3]
                    S.tt('dve', pm, pst[:, 0:128], dm, ALU.mult)
                    S.mm(pnum[:, 0:257], pm, vtok[:, js, hh, :], start=(js == 0), stop=(js == jt))
                pint = bank()
                S.mm(pint[:, 0:257], qT[:, hh, jt * 128:(jt + 1) * 128], Cst_b[:, hh, :])
                tmpi = tmpis[cnt % 2]
                S.ts('dve', tmpi, pint[:, 0:257], expb[:, jt, hh:hh + 1], None, ALU.mult)
                s_ = sc[cnt % 2]
                numt = numts[cnt % 2]
                S.tt('dve', numt, pnum[:, 0:257], tmpi, ALU.add)
                S.ts('dve', s_[:, 0:1], numt[:, 256:257], -1.0, None, ALU.mult)
                S.tt('dve', s_[:, 0:1], s_[:, 0:1], numt[:, 256:257], ALU.max)
                S.ts('dve', s_[:, 0:1], s_[:, 0:1], 1.0, None, ALU.max)
                S.recip(s_[:, 0:1], s_[:, 0:1])
                S.stt(junk, numt[:, 0:256], 1.0, numt[:, 0:256], ALU.mult, ALU.mult, accum_out=s_[:, 1:2])
                S.tt('dve', s_[:, 2:3], s_[:, 0:1], s_[:, 0:1], ALU.mult)
                S.tt('dve', s_[:, 2:3], s_[:, 2:3], s_[:, 1:2], ALU.mult)
                S.act(s_[:, 3:4], s_[:, 2:3], AF.Sqrt, bias=eps_t[:, 0:1], scale=1.0 / 256.0)
                S.recip(s_[:, 3:4], s_[:, 3:4])
                S.tt('dve', s_[:, 4:5], s_[:, 3:4], s_[:, 0:1], ALU.mult)
                hm = hmn[cnt % 2]
                S.ts('dve', hm, numt[:, 0:256], s_[:, 4:5], None, ALU.mult)
                ptr = bank()
                ptb = ptr.bitcast(BF16)
                for vc in range(2):
                    S.tr(ptb[:, vc * 128:(vc + 1) * 128], hm[:, vc * 128:(vc + 1) * 128], ident)
                for vc in range(2):
                    fcx = hh * 2 + vc
                    S.stt(haT[:, fcx, jt * 128:(jt + 1) * 128], ptb[:, vc * 128:(vc + 1) * 128],
                          hncols[:, fcx:fcx + 1], soT[:, fcx, jt * 128:(jt + 1) * 128], ALU.mult, ALU.mult)
                cnt += 1
            pc = bank()
            for js in range(NT):
                ptr = bank()
                ptb = ptr.bitcast(BF16)
                S.tr(ptb[:, 0:128], kT[:, hh, js * 128:(js + 1) * 128], ident)
                kw_ = kw_t[js % 2]
                S.ts('dve', kw_, ptb[:, 0:128], wst[:, js, hh:hh + 1], None, ALU.mult)
                S.mm(pc[:, 0:257], kw_, vtok[:, js, hh, :], start=(js == 0), stop=(js == NT - 1))
            S.stt(Cst_f[:, hh, :], Cst_f[:, hh, :], ebend[:, hh:hh + 1], pc[:, 0:257], ALU.mult, ALU.add)
            S.copy('act', Cst_b[:, hh, :], Cst_f[:, hh, :])
        if not FULL:
            S.tt('dve', ebtot, ebtot, ebend, ALU.mult)
        tap("haT", haT.rearrange("p a t -> p (a t)"))
        stop_at("mlstm")
        S.barrier()
        A.release(m_mix)

        HB = TB + [('h', 1)]

        def colof(b):
            return (0, 1) if b == 'h' else (1 + b * BW, BW)
        if not FULL:
            twl = A.alloc([128, T], BF16, "twl")
            sg = A.alloc([128, 2, T], BF16, "sg")
            m_lr = A.mark()
            lrraw = A.alloc([128, 3, T + 1], F32, "lrraw")

            def sink_lr(cc, b, ps, m):
                c0, w_ = colof(b)
                S.copy('act', lrraw[0:m, cc, c0:c0 + w_], ps)
            proj_fm(w_in, O_RWL, 288, KC, hrhs, HB, sink_lr)
            lrt = A.alloc([128, T], F32, "lrt")
            S.tt('dve', lrt, lrraw[:, 0, 0:T], lrraw[:, 0, 1:T + 1], ALU.subtract)
            S.stt(lrt, lrt, mulr[:, 0:1], lrraw[:, 0, 1:T + 1], ALU.mult, ALU.add)
            S.act(twl[0:64, :], lrt[0:64, :], AF.Tanh)
            S.copy('dve', twl[64:128, :], lrt[64:128, :])
            S.tt('dve', lrt, lrraw[:, 1, 0:T], lrraw[:, 1, 1:T + 1], ALU.subtract)
            S.stt(lrt, lrt, mulr[:, 1:2], lrraw[:, 1, 1:T + 1], ALU.mult, ALU.add)
            S.act(sg[:, 0, :], lrt, AF.Sigmoid)
            S.tt('dve', lrt[0:32, :], lrraw[0:32, 2, 0:T], lrraw[0:32, 2, 1:T + 1], ALU.subtract)
            S.stt(lrt[0:32, :], lrt[0:32, :], mulr[0:32, 2:3], lrraw[0:32, 2, 1:T + 1], ALU.mult, ALU.add)
            S.act(sg[0:32, 1, :], lrt[0:32, :], AF.Sigmoid)
            S.barrier()
            A.release(m_lr)
        stop_at('rw_lr')
        m_rw = A.mark()
        for hp in range(8):
            A.release(m_rw)
            cs_ = slice(hp * 128, (hp + 1) * 128)
            bon = A.alloc([128, NT, 2], F32, "bon")
            gL = A.alloc([128, NT], F32, "gL")
            rt = A.alloc([128, T], BF16, "rt")
            at = A.alloc([128, T], BF16, "at")
            bt = A.alloc([128, T], BF16, "bt")
            kt = A.alloc([128, T], BF16, "kt")
            bh = A.alloc([128, T], BF16, "bh")
            kh = A.alloc([128, T], BF16, "kh")
            vb = A.alloc([128, T], BF16, "vb")
            NHC = 2 * NT
            sidx = seg * 8 + hp
            if not FULL:
                m_tmp = A.mark()
                praw = A.alloc([128, T + 1], F32, "praw")

                def sk(cc, b, ps, m):
                    c0, w_ = colof(b)
                    S.copy('act', praw[:, c0:c0 + w_], ps)
                rr_ = A.alloc([128, T], F32, "r")
                kr_ = A.alloc([128, T], F32, "kr")
                vr_ = A.alloc([128, T], F32, "vr")
                tmp = A.alloc([128, T], F32, "tmp")
                ka = A.alloc([128, T], F32, "ka")
                tmp2 = ka
                for i, (off, dst) in enumerate(((O_RR, rr_), (O_RK, kr_), (O_RV, vr_))):
                    proj_fm(w_in, off + hp * 128, 128, KC, hrhs, HB, sk)
                    S.tt('dve', tmp, praw[:, 0:T], praw[:, 1:T + 1], ALU.subtract)
                    S.stt(dst, tmp, rvec[:, i * 8 + hp: i * 8 + hp + 1], praw[:, 1:T + 1], ALU.mult, ALU.add)
                lw = A.alloc([128, T], F32, "lw")
                a_ = A.alloc([128, T], F32, "a")
                for b in range(NB):
                    bs = slice(b * BW, (b + 1) * BW)
                    ps = bank()
                    S.mm(ps[:, 0:BW], wup_bf[0:64, cs_], twl[0:64, bs])
                    S.act(lw[:, bs], ps[:, 0:BW], AF.Sigmoid, bias=rvec[:, 24 + hp:25 + hp])
                    ps = bank()
                    S.mm(ps[:, 0:BW], aup_bf[64:128, cs_], twl[64:128, bs])
                    S.act(a_[:, bs], ps[:, 0:BW], AF.Sigmoid, bias=rvec[:, 32 + hp:33 + hp])
                S.ts('dve', lw, lw, -0.6065306597126334, None, ALU.mult)
                kap = A.alloc([128, T], F32, "kap")
                S.ts('dve', kap, kr_, rvec[:, 40 + hp:41 + hp], None, ALU.mult)
                S.tt('dve', tmp, kap, kap, ALU.mult)
                for b in range(NB):
                    bs = slice(b * BW, (b + 1) * BW)
                    ps = bank()
                    S.mm(ps[:, 0:BW], blk64_f, tmp[:, bs])
                    S.act(tmp2[:, bs], ps[:, 0:BW], AF.Sqrt)
                S.ts('dve', tmp2, tmp2, 1e-12, None, ALU.max)
                S.recip(tmp2, tmp2)
                S.tt('dve', kap, kap, tmp2, ALU.mult)
                S.ts('dve', tmp, a_, rvec[:, 48 + hp:49 + hp], omka[:, hp:hp + 1], ALU.mult, ALU.add)
                S.tt('dve', kr_, kr_, tmp, ALU.mult)
                S.tt('dve', tmp, rr_, kr_, ALU.mult)
                S.ts('dve', tmp, tmp, rvec[:, 56 + hp:57 + hp], None, ALU.mult)
                for n in range(NT):
                    ps = bank()
                    S.mm(ps[:, 0:2], tmp[:, n * 128:(n + 1) * 128], blk64_f[:, 0:128:64])
                    S.copy('dve', bon[:, n, :], ps[:, 0:2])
                S.tt('dve', ka, kap, a_, ALU.mult)
                cs = A.alloc([128, T], F32, "cs")
                S.scan(cs, mreset, lw, 0.0, ALU.mult, ALU.add)
                cs3 = cs.rearrange("p (n t) -> p n t", t=128)
                S.act(gL, cs3[:, :, 127], AF.Exp)
                S.copy('pool', vb, vr_)
                S.act(tmp, cs, AF.Exp)
                S.tt('dve', rt, rr_, tmp, ALU.mult)
                S.tt('dve', tmp, cs, lw, ALU.subtract)
                S.act(tmp, tmp, AF.Exp)
                S.stt(at, kap, -1.0, tmp, ALU.mult, ALU.mult)
                S.act(tmp, cs, AF.Exp, scale=-1.0)
                S.tt('dve', bt, ka, tmp, ALU.mult)
                S.tt('dve', kt, kr_, tmp, ALU.mult)
                tmp3 = tmp.rearrange("p (n t) -> p n t", t=128)
                S.tt('dve', tmp3, cs3[:, :, 127:128].to_broadcast([128, NT, 128]), cs3, ALU.subtract)
                S.act(tmp, tmp, AF.Exp)
                S.tt('dve', bh, ka, tmp, ALU.mult)
                S.tt('dve', kh, kr_, tmp, ALU.mult)
                stop_at('rw_proj')
                S.barrier()
                A.release(m_tmp)
                toks = []
                for X in (at, bh, kh, vb):
                    Xt = A.alloc([128, NT, 128], BF16, "tok")
                    ptr = bank()
                    ptb = ptr.bitcast(BF16)
                    for n in range(NT):
                        S.tr(ptb[:, n * 128:(n + 1) * 128], X[:, n * 128:(n + 1) * 128], ident)
                    S.copy('act', Xt.rearrange("p n c -> p (n c)"), ptb[:, 0:NT * 128])
                    toks.append(Xt)
                a_tok, bh_tok, kh_tok, v_tok = toks
                g_tok = A.alloc([128, NT, 128], F32, "gtok")
                for n in range(NT):
                    ps = bank()
                    S.mm(ps[:, 0:128], sg[:, 0, n * 128:(n + 1) * 128], gup_bf[:, 0, cs_], start=True, stop=False)
                    S.mm(ps[:, 0:128], sg[0:32, 1, n * 128:(n + 1) * 128], gup_bf[0:32, 1, cs_], start=False, stop=True)
                    S.copy('act', g_tok[:, n, :], ps[:, 0:128])
                NHC = 2 * NT
                G4 = A.alloc([128, NHC, 512], BF16, "G4")
                PP = [A.alloc([128, NHC, 256], BF16, f"PP{i}") for i in range(2)]
                TTm = [A.alloc([128, NHC, 128], BF16, f"TT{i}") for i in range(2)]
                RH2 = A.alloc([128, NHC, 64], BF16, "RH2")
                WT = A.alloc([128, NT, 128], BF16, "WT")
                Upr = A.alloc([128, NHC, 64], F32, "Upr")
                for h in range(2):
                    hs = slice(h * 64, (h + 1) * 64)
                    for n in range(NT):
                        hc = h * NT + n
                        ch = slice(n * 128, (n + 1) * 128)
                        pg = bank()
                        S.mm(pg[:, 0:128], bt[hs, ch], at[hs, ch])
                        S.mm(pg[:, 128:256], kt[hs, ch], at[hs, ch])
                        S.mm(pg[:, 256:384], bt[hs, ch], rt[hs, ch])
                        S.mm(pg[:, 384:512], kt[hs, ch], rt[hs, ch])
                        S.tt('dve', G4[:, hc, :], pg[:, 0:512], mask4, ALU.mult)
                        pa = bank()
                        S.mm(pa[:, 0:128], at[hs, ch], bt[hs, ch])
                        S.tt('dve', PP[0][:, hc, 0:128], pa[:, 0:128], m_sl, ALU.mult)
                        S.copy('pool', PP[0][:, hc, 128:256], G4[:, hc, 0:128])
                        S.tt('pool', TTm[0][:, hc, :], G4[:, hc, 0:128], ident, ALU.add)
                        p2 = bank()
                        S.mm(p2[:, 0:64], G4[:, hc, 128:256], v_tok[:, n, hs])
                        S.copy('act', RH2[:, hc, :], p2[:, 0:64])
                for lev in range(1, 7):
                    src, dst = PP[(lev - 1) % 2], PP[lev % 2]
                    tsrc, tdst = TTm[(lev - 1) % 2], TTm[lev % 2]
                    for hc in range(NHC):
                        pq = bank()
                        S.mm(pq[:, 0:128], src[:, hc, 128:256], src[:, hc, 0:128])
                        if lev < 6:
                            S.mm(pq[:, 128:256], src[:, hc, 0:128], src[:, hc, 128:256])
                            S.copy('act', dst[:, hc, :], pq[:, 0:256])
                        else:
                            S.copy('act', dst[:, hc, 0:128], pq[:, 0:128])
                        pt_ = bank()
                        S.mm(pt_[:, 0:128], dst[:, hc, 0:128], tsrc[:, hc, :])
                        S.tt('dve', tdst[:, hc, :], pt_[:, 0:128], tsrc[:, hc, :], ALU.add)
                TTf = TTm[0]
                for h in range(2):
                    hs = slice(h * 64, (h + 1) * 64)
                    for n in range(NT):
                        hc = h * NT + n
                        pw = bank()
                        S.mm(pw[hs, 0:128], a_tok[:, n, hs], TTf[:, hc, :])
                        S.copy('act', WT[hs, n, :], pw[hs, 0:128])
                        pu = bank()
                        S.mm(pu[:, 0:64], TTf[:, hc, :], RH2[:, hc, :])
                        S.copy('dve', Upr[:, hc, :], pu[:, 0:64])
                S.dma('sp', scr_rt[sidx], rt, 'st_rt')
                S.dma('sp', scr_G[sidx], G4[:, :, 256:512], 'st_G')
                S.dma('sp', scr_WT[sidx], WT, 'st_WT')
                S.dma('sp', scr_Upr[sidx], Upr, 'st_Upr')
                S.dma('sp', scr_tok[sidx][:, 0], bh_tok, 'st_t0')
                S.dma('sp', scr_tok[sidx][:, 1], kh_tok, 'st_t1')
                S.dma('sp', scr_tok[sidx][:, 2], v_tok, 'st_t2')
                S.dma('sp', scr_gt[sidx], g_tok, 'st_gt')
                S.dma('sp', scr_sm[sidx][:, 0:NT], gL, 'st_gl')
                S.dma('sp', scr_sm[sidx][:, NT:3 * NT], bon.rearrange("p n h -> p (n h)"), 'st_bon')
            else:
                bh_tok = A.alloc([128, NT, 128], BF16, "tok")
                kh_tok = A.alloc([128, NT, 128], BF16, "tok")
                v_tok = A.alloc([128, NT, 128], BF16, "tok")
                g_tok = A.alloc([128, NT, 128], F32, "gtok")
                G4 = A.alloc([128, NHC, 512], BF16, "G4")
                WT = A.alloc([128, NT, 128], BF16, "WT")
                Upr = A.alloc([128, NHC, 64], F32, "Upr")
                S.dma('sp', rt, scr_rt[sidx], 'ld_rt')
                S.dma('sp', G4[:, :, 256:512], scr_G[sidx], 'ld_G')
                S.dma('sp', WT, scr_WT[sidx], 'ld_WT')
                S.dma('sp', Upr, scr_Upr[sidx], 'ld_Upr')
                S.dma('sp', bh_tok, scr_tok[sidx][:, 0], 'ld_t0')
                S.dma('sp', kh_tok, scr_tok[sidx][:, 1], 'ld_t1')
                S.dma('sp', v_tok, scr_tok[sidx][:, 2], 'ld_t2')
                S.dma('sp', g_tok, scr_gt[sidx], 'ld_gt')
                S.dma('sp', gL, scr_sm[sidx][:, 0:NT], 'ld_gl')
                S.dma('sp', bon.rearrange("p n h -> p (n h)"), scr_sm[sidx][:, NT:3 * NT], 'ld_bon')
            stop_at('rw_gram')
            W_ = 64 if FULL else 128
            if FULL:
                Sf_, Sb_, cur_ = Sst_f, Sst_b, scur
            else:
                Sf_, Sb_, cur_ = SfA, SbA, scurA
            y_tok = A.alloc([128, NT, 128], F32, "ytok")
            UTb = [A.alloc([128, 2, W_], BF16, f"UTb{i}") for i in range(2)]
            for n in range(NT):
                ch = slice(n * 128, (n + 1) * 128)
                Sold = Sb_[cur_[hp]]
                Snew = Sb_[1 - cur_[hp]]
                ut = UTb[n % 2]
                pu = bank()
                S.mm(pu[:, 0:2 * W_], WT[:, n, :], Sold[:, hp, :])
                for h in range(2):
                    S.tt('dve', ut[:, h, 0:64], pu[:, h * W_:h * W_ + 64], Upr[:, h * NT + n, :], ALU.add)
                    if not FULL:
                        S.copy('act', ut[:, h, 64:128], pu[:, h * W_ + 64:(h + 1) * W_])
                ps_ = bank()
                if FULL:
                    py = bank()
                    S.mm(py[:, 0:128], rt[:, ch], Sold[:, hp, :], start=True, stop=False)
                for h in range(2):
                    hs = slice(h * 64, (h + 1) * 64)
                    hc = h * NT + n
                    S.mm(ps_[hs, 0:W_], bh_tok[:, n, hs], ut[:, h, :], start=True, stop=False)
                    S.mm(ps_[hs, 0:64], kh_tok[:, n, hs], v_tok[:, n, hs], start=False, stop=True)
                    if FULL:
                        S.mm(py[:, h * 64:(h + 1) * 64], G4[:, hc, 256:384], ut[:, h, :], start=False, stop=False)
                        S.mm(py[:, h * 64:(h + 1) * 64], G4[:, hc, 384:512], v_tok[:, n, hs], start=False, stop=True)
                S.stt(Sf_[:, hp, :], Sf_[:, hp, :], gL[:, n:n + 1], ps_[:, 0:W_], ALU.mult, ALU.add)
                S.copy('act', Snew[0:64, hp, 0:W_], Sf_[0:64, hp, :])
                S.copy('act', Snew[64:128, hp, W_:2 * W_], Sf_[64:128, hp, :])
                cur_[hp] = 1 - cur_[hp]
                if FULL:
                    S.copy('act', y_tok[:, n, :], py[:, 0:128])
            if not FULL:
                S.barrier()
                continue
            stop_at('rw_seq')
            y4 = y_tok.rearrange("p n (h v) -> p (n h) v", v=64)
            st1 = A.alloc([128, NHC], F32, "st1")
            st2 = A.alloc([128, NHC], F32, "st2")
            yc = A.alloc([128, NHC, 64], F32, "yc")
            ysq = A.alloc([128, NHC, 64], F32, "ysq")
            S.reduce(st1, y4)
            S.ts('dve', st1, st1, 1.0 / 64.0, None, ALU.mult)
            S.tt('dve', yc, y4, st1.rearrange("p (a o) -> p a o", o=1).to_broadcast([128, NHC, 64]), ALU.subtract)
            S.tt('pool', ysq, yc, yc, ALU.mult)
            S.reduce(st2, ysq)
            S.act(st2, st2, AF.Sqrt, bias=eps_t[:, 1:2], scale=1.0 / 64.0)
            S.recip(st2, st2)
            S.tt('dve', yc, yc, st2.rearrange("p (a o) -> p a o", o=1).to_broadcast([128, NHC, 64]), ALU.mult)
            yc3 = yc.rearrange("p (n h) v -> p n (h v)", h=2)
            lg = lngb[:, hp * 128:(hp + 1) * 128].rearrange("p (o c) -> p o c", o=1).to_broadcast([128, NT, 128])
            lb = lngb[:, 1024 + hp * 128:1024 + (hp + 1) * 128].rearrange("p (o c) -> p o c", o=1).to_broadcast([128, NT, 128])
            S.tt('dve', yc3, yc3, lg, ALU.mult)
            S.tt('dve', yc3, yc3, lb, ALU.add)
            bv = ysq
            S.tt('dve', bv, v_tok.rearrange("p n (h v) -> p (n h) v", v=64),
                 bon.rearrange("p n (h o) -> p (n h) o", o=1).to_broadcast([128, NHC, 64]), ALU.mult)
            S.tt('dve', yc, yc, bv, ALU.add)
            hb_tok = A.alloc([128, NT, 128], BF16, "hbtok")
            S.tt('dve', hb_tok, yc3, g_tok, ALU.mult)
            ptr = bank()
            ptb = ptr.bitcast(BF16)
            for n in range(NT):
                S.tr(ptb[:, n * 128:(n + 1) * 128], hb_tok[:, n, :], ident)
            S.copy('act', hbT[:, hp, :], ptb[:, 0:T])
        tap("hbT", hbT.rearrange("p a t -> p (a t)"))
        stop_at("rwkv")
        S.copy('dve', hhalo, hT[:, :, T])
        A.release(m_mix)
        if not FULL:
            S.barrier()
            return

        A.limit = A.top_off
        mergedT = A.alloc([128, 16, T], BF16, "mergedT")
        sgt = [A.alloc([128, BW], F32, f"sgt{i}") for i in range(2)]
        sgi = [0]

        def hrhs2(kc, b):
            return hT[:, kc, 1 + b * BW: 1 + (b + 1) * BW]

        def sink_ma(cc, b, p1, p2):
            t_ = sgt[sgi[0] % 2]
            sgi[0] += 1
            S.act(t_, p1, AF.Sigmoid)
            S.tt('dve', mergedT[:, cc, b * BW:(b + 1) * BW], t_, p2, ALU.mult)
        proj2(w_in, O_GA, KC, hrhs2, w_ba, 0, 8, lambda kc, b: haT[:, kc, b * BW:(b + 1) * BW], 2048, TB, sink_ma)

        def sink_mb(cc, b, p1, p2):
            t_ = sgt[sgi[0] % 2]
            sgi[0] += 1
            S.act(t_, p1, AF.Sigmoid)
            S.tt('dve', t_, t_, p2, ALU.mult)
            S.tt('pool', mergedT[:, cc, b * BW:(b + 1) * BW], mergedT[:, cc, b * BW:(b + 1) * BW], t_, ALU.add)
        proj2(w_in, O_GB, KC, hrhs2, w_bb, 0, 8, lambda kc, b: hbT[:, kc, b * BW:(b + 1) * BW], 2048, TB, sink_mb)

        def sink_R(cc, b, ps, m):
            S.copy('act' if (cc + b) % 2 else 'dve', R[:, cc, b * BW:(b + 1) * BW], ps)
        proj_fm(w_mo, 0, 2048, KC, lambda kc, b: mergedT[:, kc, b * BW:(b + 1) * BW], TB, sink_R)
        S.barrier()
        A.release(seg_mark)

        def post_norm_residual(Y, gidx, getres, dst_fn):
            m_ = A.mark()
            rs = rms_stats(lambda fc: Y[:, fc, :], BW, T)
            for fc in range(KC):
                S.stt(Y[:, fc, :], Y[:, fc, :], gcols[:, gidx * 16 + fc: gidx * 16 + fc + 1], rs, ALU.mult, ALU.mult)
                dst_fn(fc, getres(fc))
            S.barrier()
            A.release(m_)

        def dst_R(fc, res):
            S.tt('dve', R[:, fc, :], R[:, fc, :], res, ALU.add)
        post_norm_residual(R, 1, lambda fc: xload(xT[fc * 128:(fc + 1) * 128, t0:t0 + T])[:, 0:T], dst_R)
        tap("x1", R.rearrange("p a t -> p (a t)"))
        stop_at("merge")

        def pre_norm(gidx):
            h_ = A.alloc([128, 16, T], BF16, "hTn")
            m_ = A.mark()
            rs = rms_stats(lambda fc: R[:, fc, :], BW, T)
            for fc in range(KC):
                S.stt(h_[:, fc, :], R[:, fc, :], gcols[:, gidx * 16 + fc: gidx * 16 + fc + 1], rs, ALU.mult, ALU.mult)
            S.barrier()
            A.release(m_)
            return h_
        oT = A.alloc([128, 4, T], BF16, "oT")
        m_xa = A.mark()
        hT2 = pre_norm(2)
        mnT = A.alloc([128, 16, MEM], BF16, "mnT")
        m_ = A.mark()
        rsm = rms_stats(lambda fc: xload(memT[fc * 128:(fc + 1) * 128, :])[:, 0:MEM], MEM, MEM)
        for fc in range(KC):
            xb = xload(memT[fc * 128:(fc + 1) * 128, :])
            S.stt(mnT[:, fc, :], xb[:, 0:MEM], gcols[:, 3 * 16 + fc: 3 * 16 + fc + 1], rsm, ALU.mult, ALU.mult)
        S.barrier()
        A.release(m_)
        kmT = A.alloc([128, 4, MEM], BF16, "kmT")
        vm = A.alloc([128, 2, 512], BF16, "vm")
        qT2 = A.alloc([128, 4, T], BF16, "qT2")

        def sink_km(cc, b, ps, m):
            S.copy('act', kmT[:, cc, :], ps)
        proj_fm(w_kv, 0, 512, KC, lambda kc, b: mnT[:, kc, :], [(0, MEM)], sink_km)
        for cb in range(4):
            wb = wload(w_kv[:, 512 + cb * 128: 512 + (cb + 1) * 128], KC, 128)
            for mt in range(2):
                ps = bank()
                for kc in range(KC):
                    S.mm(ps[:, 0:128], mnT[:, kc, mt * 128:(mt + 1) * 128], wb[:, kc, 0:128],
                         start=(kc == 0), stop=(kc == KC - 1))
                S.copy('dve', vm[:, mt, cb * 128:(cb + 1) * 128], ps[:, 0:128])

        def sink_q2(cc, b, ps, m):
            S.act(qT2[:, cc, b * BW:(b + 1) * BW], ps, AF.Copy, scale=128 ** -0.5)
        proj_fm(w_q, 0, 512, KC, lambda kc, b: hT2[:, kc, b * BW:(b + 1) * BW], TB, sink_q2)
        pex = [A.alloc([128, MEM], F32, f"pex{i}") for i in range(2)]
        pnb = [A.alloc([128, MEM], BF16, f"pnb{i}") for i in range(2)]
        pTt = [A.alloc([128, 2, 128], BF16, f"pTt{i}") for i in range(2)]
        sm = [A.alloc([128, 4], F32, f"sm{i}") for i in range(2)]
        c2 = 0
        for h in range(4):
            for tt_ in range(NT):
                ts_ = slice(tt_ * 128, (tt_ + 1) * 128)
                psc = bank()
                S.mm(psc[:, 0:MEM], qT2[:, h, ts_], kmT[:, h, :])
                s_ = sm[c2 % 2]
                S.add('dve', (lambda o, i: (lambda e: e.tensor_reduce(o, i, AX.X, ALU.max)))(U(s_[:, 0:1]), U(psc[:, 0:MEM])),
                      r=[psc], w=[s_])
                S.ts('dve', s_[:, 1:2], s_[:, 0:1], -1.0, None, ALU.mult)
                pe_ = pex[c2 % 2]
                S.act(pe_, psc[:, 0:MEM], AF.Exp, bias=s_[:, 1:2], accum_out=s_[:, 2:3])
                S.recip(s_[:, 3:4], s_[:, 2:3])
                pn_ = pnb[c2 % 2]
                S.ts('dve', pn_, pe_, s_[:, 3:4], None, ALU.mult)
                ptr = bank()
                ptb = ptr.bitcast(BF16)
                for mt in range(2):
                    S.tr(ptb[:, mt * 128:(mt + 1) * 128], pn_[:, mt * 128:(mt + 1) * 128], ident)
                pt2 = pTt[c2 % 2]
                S.copy('act', pt2.rearrange("p a t -> p (a t)"), ptb[:, 0:256])
                po = bank()
                for mt in range(2):
                    S.mm(po[:, 0:128], vm[:, mt, h * 128:(h + 1) * 128], pt2[:, mt, :], start=(mt == 0), stop=(mt == 1))
                S.copy('dve', oT[:, h, ts_], po[:, 0:128])
                c2 += 1
        S.barrier()
        A.release(m_xa)
        Y2 = A.alloc([128, 16, T], F32, "Y2")

        def sink_Y2(cc, b, ps, m):
            S.copy('act' if (cc + b) % 2 else 'dve', Y2[:, cc, b * BW:(b + 1) * BW], ps)
        proj_fm(w_o, 0, 2048, 4, lambda kc, b: oT[:, kc, b * BW:(b + 1) * BW], TB, sink_Y2)

        def dst_R2(fc, res):
            S.tt('dve', R[:, fc, :], R[:, fc, :], res, ALU.add)
        post_norm_residual(Y2, 4, lambda fc: Y2[:, fc, :], dst_R2)
        tap("x2", R.rearrange("p a t -> p (a t)"))
        stop_at("xattn")
        S.barrier()
        A.release(seg_mark)
        S.dma('sp', xsp[seg].rearrange("(fc p) t -> p fc t", p=128), R, f'spill{seg}')
        if seg == NSEG - 1:
            sq2 = A.alloc([128, 16, 2], BF16, "sq2")
            S.act(sq2, R[:, :, T - 2:T], AF.Square)
            ps = bank()
            for fc in range(KC):
                S.mm(ps[:, 0:2], ones_bf, sq2[:, fc, :], start=(fc == 0), stop=(fc == KC - 1))
            rs2 = A.alloc([128, 2], F32, "rs2")
            S.act(rs2, ps[:, 0:2], AF.Sqrt, bias=eps_t[:, 0:1], scale=1.0 / D)
            S.recip(rs2, rs2)
            h3h = A.alloc([128, 16, 2], F32, "h3h")
            S.tt('dve', h3h, R[:, :, T - 2:T],
                 gcols[:, 80:96].rearrange("p (a o) -> p a o", o=1).to_broadcast([128, 16, 2]), ALU.mult)
            S.tt('dve', h3h, h3h, rs2.rearrange("p (o t) -> p o t", o=1).to_broadcast([128, 16, 2]), ALU.mult)
            S.dma('sp', hin, h3h.rearrange("p a t -> p (a t)"), 'hin')
        S.barrier()


    def seg_C(seg):
        t0 = seg * T
        A.release(ph_mark)
        A.limit = A.top_off
        hT3 = A.alloc([128, 16, T], BF16, "hT3")
        m_ = A.mark()
        rs3 = rms_stats(lambda fc: xload(xsp[seg][fc * 128:(fc + 1) * 128, :])[:, 0:T], BW, T)
        for fc in range(KC):
            xb = xload(xsp[seg][fc * 128:(fc + 1) * 128, :])
            S.stt(hT3[:, fc, :], xb[:, 0:T], gcols[:, 80 + fc:81 + fc], rs3, ALU.mult, ALU.mult)
        S.barrier()
        A.release(m_)
        ACC = R
        FB = ([('h', 2)] if seg == 0 else []) + TB

        def h3rhs(kc, b):
            if b == 'h':
                return h3halo[:, kc, :]
            return hT3[:, kc, b * BW:(b + 1) * BW]
        GF = 8
        actT = A.alloc([128, GF, T], BF16, "actT")
        ug = [A.alloc([128, T + 2], F32, f"ug{i}") for i in range(2)]
        uu = [A.alloc([128, T + 2], F32, f"uu{i}") for i in range(2)]
        cgs = [A.alloc([128, BW], F32, f"cg{i}") for i in range(2)]
        cus = [A.alloc([128, BW], F32, f"cu{i}") for i in range(2)]
        pls = [A.alloc([128, BW], F32, f"pl{i}") for i in range(2)]
        ci = [0]
        for g in range(DFF // 128 // GF):
            def sink_f(cc, b, pg_, pu_, g=g):
                f = g * GF + cc
                ug_, uu_ = ug[f % 2], uu[f % 2]
                if b == 'h':
                    S.copy('act', ug_[:, 0:2], pg_)
                    S.copy('act', uu_[:, 0:2], pu_)
                    return
                bs2 = slice(2 + b * BW, 2 + (b + 1) * BW)
                if b == 0 and seg > 0:
                    S.copy('pool', ug_[:, 0:2], uhalo[:, f, :])
                    S.copy('pool', uu_[:, 0:2], uhalo[:, 64 + f, :])
                S.copy('act', ug_[:, bs2], pg_)
                S.copy('act', uu_[:, bs2], pu_)
                k_ = ci[0] % 2
                ci[0] += 1
                cg, cu, pl = cgs[k_], cus[k_], pls[k_]
                for (src, dst, chn) in ((ug_, cg, f), (uu_, cu, 64 + f)):
                    S.ts('dve', dst, src[:, 2 + b * BW: 2 + (b + 1) * BW], convc[:, 256 + chn:257 + chn],
                         convc[:, 384 + chn:385 + chn], ALU.mult, ALU.add)
                    S.stt(dst, src[:, 1 + b * BW: 1 + (b + 1) * BW], convc[:, 128 + chn:129 + chn], dst, ALU.mult, ALU.add)
                    S.stt(dst, src[:, b * BW: (b + 1) * BW], convc[:, chn:chn + 1], dst, ALU.mult, ALU.add)
                S.tt('pool', pl, cg, cg, ALU.mult)
                S.ts('pool', pl, pl, 0.044715, 1.0, ALU.mult, ALU.add)
                S.tt('pool', pl, pl, cg, ALU.mult)
                S.act(pl, pl, AF.Sigmoid, scale=1.5957691216057308)
                S.tt('dve', cg, cg, pl, ALU.mult)
                S.tt('dve', actT[:, cc, b * BW:(b + 1) * BW], cg, cu, ALU.mult)
                if b == NB - 1:
                    S.copy('pool', uhalo[:, f, :], ug_[:, T:T + 2])
                    S.copy('pool', uhalo[:, 64 + f, :], uu_[:, T:T + 2])
            proj2(w_fu, g * GF * 128, KC, h3rhs,
                  w_fu, DFF + g * GF * 128, KC, h3rhs, GF * 128, FB, sink_f)

            def sink_acc(cc, b, ps, m, g=g):
                dst = ACC[:, cc, b * BW:(b + 1) * BW]
                if g == 0:
                    S.copy('act', dst, ps)
                else:
                    S.tt('dve', dst, dst, ps, ALU.add)
            proj_fm(w_fd[g * GF * 128:(g + 1) * GF * 128, :], 0, 2048, GF,
                    lambda kc, b: actT[:, kc, b * BW:(b + 1) * BW], TB, sink_acc)

        def dst_out(fc, res):
            S.tt('dve', ACC[:, fc, :], ACC[:, fc, :], res, ALU.add)
            S.dma('sp', outT[fc * 128:(fc + 1) * 128, t0:t0 + T], ACC[:, fc, :], 'out')
        post_norm_residual(ACC, 6, lambda fc: xload(xsp[seg][fc * 128:(fc + 1) * 128, :])[:, 0:T], dst_out)
        S.barrier()


    A.release(base_mark)
    SfA = A.alloc([128, 8, 128], F32, "SfA")
    SbA = [A.alloc([128, 8, 256], BF16, f"SbA{i}") for i in range(2)]
    scurA = [0] * 8
    S.memset('dve', SfA, 0.0)
    S.memset('dve', SbA[0], 0.0)
    S.memset('dve', SbA[1], 0.0)
    for hp in range(8):
        S.copy('dve', SfA[0:64, hp, 64:128], ident_f[0:64, 0:64])
        S.copy('dve', SfA[64:128, hp, 64:128], ident_f[64:128, 64:128])
        S.copy('dve', SbA[0][0:64, hp, 64:128], ident_f[0:64, 0:64])
        S.copy('dve', SbA[0][64:128, hp, 192:256], ident_f[64:128, 64:128])
    ph_mark = A.mark()
    for seg in range(NSEG):
        seg_AB(seg, 'A')
    A.release(ph_mark)
    S.dma('sp', summR.rearrange("(hp p) c -> p hp c", p=128), SfA, 'ex0')
    Cex = A.alloc([128, 4, 258], F32, "Cex")
    S.copy('dve', Cex[:, :, 0:257], Cst_f)
    S.copy('dve', Cex[:, :, 257], ebtot)
    S.dma('sp', summM.rearrange("(h p) c -> p h c", p=128), Cex, 'ex1')
    S.add('pool', lambda e: e.collective_compute("AllGather", ALU.bypass, replica_groups=[[0, 1, 2, 3], [4, 5, 6, 7]],
                                                 ins=[summR], outs=[gathR]),
          r=[summR], w=[gathR], dma=('cc', 'cc1'))
    S.add('pool', lambda e: e.collective_compute("AllGather", ALU.bypass, replica_groups=[[0, 1, 2, 3], [4, 5, 6, 7]],
                                                 ins=[summM], outs=[gathM]),
          r=[summM], w=[gathM], dma=('cc', 'cc1b'))
    ccd = A.alloc([128, 2], F32, "ccd")
    S.add('pool', lambda e: e.memset(U(ccd), 0.0), r=[gathR, gathM], w=[gathR, gathM, ccd])
    GR = A.alloc([128, 8, 128], F32, "GR")
    GM = A.alloc([128, 4, 258], F32, "GM")
    XA = A.alloc([128, 128], F32, "XA")
    XAT = A.alloc([128, 128], BF16, "XAT")
    Sp = A.alloc([128, 64], F32, "Sp")
    Cp = A.alloc([128, 257], F32, "Cp")
    S.memset('dve', Cst_f, 0.0)
    S.memset('dve', XA, 0.0)
    for r in range(3):
        S.dma('sp', GR, gathR[r * 1024:(r + 1) * 1024, :].rearrange("(hp p) c -> p hp c", p=128), 'gr')
        S.dma('sp', GM, gathM[r * 512:(r + 1) * 512, :].rearrange("(h p) c -> p h c", p=128), 'gm')
        for hp in range(8):
            S.copy('dve', XA[0:64, 0:64], GR[0:64, hp, 64:128])
            S.copy('dve', XA[64:128, 64:128], GR[64:128, hp, 64:128])
            ptr = bank()
            S.tr(ptr[:, 0:128], XA, ident_f)
            S.copy('act', XAT, ptr[:, 0:128])
            pf = bank()
            S.mm(pf[:, 0:128], XAT, Sst_b[scur[hp]][:, hp, :])
            for h in range(2):
                hs = slice(h * 64, (h + 1) * 64)
                S.tt('dve', Sp[hs, :], pf[hs, h * 64:(h + 1) * 64], GR[hs, hp, 0:64], ALU.add)
            S.tt('dve', Sp, Sp, Sst_f[:, hp, :], ALU.subtract)
            S.stt(Sst_f[:, hp, :], Sp, cmask[:, r:r + 1], Sst_f[:, hp, :], ALU.mult, ALU.add)
            S.copy('act', Sst_b[scur[hp]][0:64, hp, 0:64], Sst_f[0:64, hp, :])
            S.copy('act', Sst_b[scur[hp]][64:128, hp, 64:128], Sst_f[64:128, hp, :])
        for hh in range(4):
            S.stt(Cp, Cst_f[:, hh, :], GM[:, hh, 257:258], GM[:, hh, 0:257], ALU.mult, ALU.add)
            S.tt('dve', Cp, Cp, Cst_f[:, hh, :], ALU.subtract)
            S.stt(Cst_f[:, hh, :], Cp, cmask[:, r:r + 1], Cst_f[:, hh, :], ALU.mult, ALU.add)
    S.copy('act', Cst_b, Cst_f)
    S.barrier()

    A.release(base_mark)
    ph_mark = A.mark()
    for seg in range(NSEG):
        seg_AB(seg, 'B')
    A.release(ph_mark)
    S.add('pool', lambda e: e.collective_compute("AllGather", ALU.bypass, replica_groups=[[0, 1, 2, 3], [4, 5, 6, 7]],
                                                 ins=[hin], outs=[hout]),
          r=[hin], w=[hout], dma=('cc', 'cc2'))
    ccd2 = A.alloc([128, 2], F32, "ccd2")
    S.add('pool', lambda e: e.memset(U(ccd2), 0.0), r=[hout], w=[hout, ccd2])
    HG = A.alloc([128, 4, 32], F32, "HG")
    hacc = A.alloc([128, 32], F32, "hacc")
    S.dma('sp', HG, hout.rearrange("(r p) c -> p r c", p=128), 'hg')
    S.ts('dve', hacc, HG[:, 0, :], csel[:, 0:1], None, ALU.mult)
    for r in range(1, 4):
        S.stt(hacc, HG[:, r, :], csel[:, r:r + 1], hacc, ALU.mult, ALU.add)
    S.copy('dve', h3halo.rearrange("p a t -> p (a t)"), hacc)
    S.barrier()

    for seg in range(NSEG):
        seg_C(seg)
    S.finish()
    S.emit()
    S.stack.close()
    print("arena peak words", A.peak, "ops", {e: len(v) for e, v in S.ops.items()})
    return nc


_CACHE = {}


def _consts(T):
    r = np.arange(128)
    ident = np.eye(128, dtype=np.float32)
    ones = np.ones((128, 128), np.float32)
    m_su = (r[None, :] > r[:, None]).astype(np.float32)
    m_iu = (r[None, :] >= r[:, None]).astype(np.float32)
    m_sl = (r[None, :] < r[:, None]).astype(np.float32)
    blk = ((r[None, :] // 64) == (r[:, None] // 64)).astype(np.float32)
    cst = np.concatenate([ident, ones, m_su, m_iu, m_sl, blk], axis=1)
    mreset = np.ones((128, T), np.float32)
    mreset[:, ::128] = 0.0
    return np.ascontiguousarray(cst), mreset


def colmaj(v, nch):
    return np.ascontiguousarray(np.asarray(v, np.float32).reshape(nch, 128).T)


def prepare_shared(inp, T):
    f = lambda k: np.ascontiguousarray(np.asarray(inp[k], np.float32)[0])
    cst, mreset = _consts(T)
    gnames = ["mix_pre_norm", "mix_post_norm", "xattn_pre_norm", "mem_norm", "xattn_post_norm", "ffn_pre_norm", "ffn_post_norm"]
    gcols = np.concatenate([colmaj(f(n), 16) for n in gnames], axis=1)
    bif = np.ascontiguousarray(np.broadcast_to(np.concatenate([f("mlstm_b_i"), f("mlstm_b_f")])[None, :], (128, 8)))
    hncols = colmaj(f("mlstm_head_norm"), 8)
    mu = f("rwkv_mu")
    rvec = np.concatenate([colmaj(mu[0:1024], 8), colmaj(mu[1024:2048], 8), colmaj(mu[2048:3072], 8),
                           colmaj(f("rwkv_w0"), 8), colmaj(f("rwkv_a0"), 8), colmaj(f("rwkv_k_k"), 8),
                           colmaj(f("rwkv_k_a"), 8), colmaj(f("rwkv_r_k").reshape(-1), 8),
                           np.zeros((128, 8), np.float32)], axis=1)
    mulr = np.zeros((128, 4), np.float32)
    mulr[0:64, 0] = mu[3072:3136]
    mulr[64:128, 0] = mu[3136:3200]
    mulr[:, 1] = mu[3200:3328]
    mulr[0:32, 2] = mu[3328:3360]
    lngb = np.ascontiguousarray(np.broadcast_to(np.concatenate([f("rwkv_ln_g"), f("rwkv_ln_b")])[None, :], (128, 2048)))
    cw = f("ffn_conv_w")
    convc = np.concatenate([colmaj(cw[0], 128), colmaj(cw[1], 128), colmaj(cw[2], 128), colmaj(f("ffn_conv_b"), 128)], axis=1)
    return {
        "w_in": f("w_in"), "w_ba": f("w_branch_a"), "w_bb": f("w_branch_b"), "w_mo": f("w_mix_out"),
        "w_q": f("xattn_wq"), "w_kv": f("xattn_wkv"), "w_o": f("xattn_wo"), "w_fu": f("ffn_w_up"),
        "w_fd": f("ffn_w_down"), "rw_up": f("rwkv_w_up"), "ra_up": f("rwkv_a_up"), "rg_up": f("rwkv_g_up"),
        "gcols": gcols, "bif": bif, "hncols": hncols, "rvec": np.ascontiguousarray(rvec), "mulr": mulr,
        "lngb": lngb, "convc": np.ascontiguousarray(convc), "cst": cst, "mreset": mreset,
    }


T_SEG = 512
NSEG_CORE = 2
TOK_CORE = T_SEG * NSEG_CORE


def run(inputs, debug_taps=()):
    x = np.asarray(inputs["x"], np.float32)
    mem = np.asarray(inputs["mem"], np.float32)
    B, SQ, _ = x.shape
    G = SQ // TOK_CORE
    assert B * G == 8 and G == 4
    nc = build_program(NSEG_CORE, T_SEG, debug_taps)
    shared = prepare_shared(inputs, T_SEG)
    in_maps = []
    for c in range(8):
        b, g = c // G, c % G
        m = dict(shared)
        m["xT"] = np.ascontiguousarray(x[b, g * TOK_CORE:(g + 1) * TOK_CORE].T)
        m["memT"] = np.ascontiguousarray(mem[b].T)
        prev = x[b, g * TOK_CORE - 1] if g > 0 else np.zeros((D,), np.float32)
        m["xh"] = colmaj(prev, 16)
        cm = np.zeros((128, 4), np.float32)
        cm[:, :g] = 1.0
        cs = np.zeros((128, 4), np.float32)
        if g > 0:
            cs[:, g - 1] = 1.0
        m["cmask"] = cm
        m["csel"] = cs
        in_maps.append(m)
    res = run_bass_kernel_spmd(nc, in_maps, core_ids=list(range(8)))
    return res


def kernel(**inputs):
    x = np.asarray(inputs["x"], np.float32)
    B, SQ, _ = x.shape
    G = SQ // TOK_CORE
    res = run(inputs)
    out = np.empty((B, SQ, D), np.float32)
    for c in range(8):
        b, g = c // G, c % G
        out[b, g * TOK_CORE:(g + 1) * TOK_CORE] = res.results[c]["outT"].T
    return out
```

```python
import numpy as np
import contextlib
import concourse.bass as bass
import concourse.mybir as mybir
from concourse.bass_utils import run_bass_kernel_spmd

F32 = mybir.dt.float32
BF16 = mybir.dt.bfloat16
AF = mybir.ActivationFunctionType
ALU = mybir.AluOpType
AX = mybir.AxisListType

D = 2048
KC = 16
SEQ = 4096
BATCH = 2
MEM = 256
IN_COLS = 10536
DFF = 8192
EPS = 1e-6
GN_EPS = 64e-5
O_Q, O_K, O_V, O_O, O_I, O_F = 0, 512, 1024, 2048, 3072, 3076
O_RW = 3080
O_RR, O_RK, O_RV, O_RWL, O_RAL, O_RGL = O_RW, O_RW + 1024, O_RW + 2048, O_RW + 3072, O_RW + 3136, O_RW + 3200
O_GA = O_RW + 3360
O_GB = O_GA + 2048


class V:
    __slots__ = ('ap', 'key')

    def __init__(self, ap, key):
        self.ap = ap
        self.key = key

    def __getitem__(self, idx):
        return V(self.ap[idx], self.key)

    def rearrange(self, *a, **k):
        return V(self.ap.rearrange(*a, **k), self.key)

    def bitcast(self, dt):
        return V(self.ap.bitcast(dt), self.key)

    def to_broadcast(self, shape):
        return V(self.ap.to_broadcast(shape), self.key)

    def k(self, sub):
        return V(self.ap, (self.key, sub))


def U(x):
    return x.ap if isinstance(x, V) else x


class Sched:
    ENGS = ['pe', 'act', 'dve', 'pool', 'sp']
    SAME_ENG_SYNC = {'act', 'dve', 'pool'}

    def __init__(self, nc):
        self.nc = nc
        self.ops = {e: [] for e in self.ENGS}
        self.last_real = {e: None for e in self.ENGS}
        self.last_w = {}
        self.reads = {}
        self.dma_cnt = {}
        self.stack = contextlib.ExitStack()

    @staticmethod
    def _key(x):
        if isinstance(x, V):
            return x.key
        if isinstance(x, (str, tuple)):
            return x
        return x.name

    def add(self, eng, fn, r=(), w=(), dma=None):
        deps = {}

        def dep(ev):
            if ev is None:
                return
            k = ev[:2]
            if deps.get(k, -1) < ev[2]:
                deps[k] = ev[2]
        rk = [self._key(x) for x in r]
        wk = [self._key(x) for x in w]
        for k in rk:
            dep(self.last_w.get(k))
        for k in wk:
            dep(self.last_w.get(k))
            for kk, v in self.reads.get(k, {}).items():
                dep(kk + (v,))
        idx = len(self.ops[eng])
        if dma is None:
            ev = ('op', eng, idx)
            self.last_real[eng] = idx
        else:
            self.dma_cnt[dma] = self.dma_cnt.get(dma, 0) + (1 if isinstance(dma, tuple) else 16)
            ev = ('dma', dma, self.dma_cnt[dma])
        for k in rk:
            d = self.reads.setdefault(k, {})
            if d.get(ev[:2], -1) < ev[2]:
                d[ev[:2]] = ev[2]
        for k in wk:
            self.last_w[k] = ev
            self.reads[k] = {}
        self.ops[eng].append(dict(fn=fn, deps=deps, dma=dma, signal=False, waits=[]))
        return ev

    def barrier(self):
        for e in self.ENGS:
            deps = {}
            for e2 in self.ENGS:
                if (e2 != e or e in self.SAME_ENG_SYNC) and self.last_real[e2] is not None:
                    deps[('op', e2)] = self.last_real[e2]
            for s, v in self.dma_cnt.items():
                if isinstance(s, tuple) and e != 'pool':
                    continue
                deps[('dma', s)] = v
            self.ops[e].append(dict(fn=None, deps=deps, dma=None, signal=False, waits=[]))
        self.last_w = {}
        self.reads = {}

    def finish(self):
        self.barrier()
        for e in self.ENGS:
            seen = {}
            for i, op in enumerate(self.ops[e]):
                for k, v in op['deps'].items():
                    if k[0] == 'op':
                        if self.ops[k[1]][v]['dma'] is not None or self.ops[k[1]][v]['fn'] is None:
                            continue
                        if k[1] == e and e not in self.SAME_ENG_SYNC:
                            continue
                    if seen.get(k, -1) >= v:
                        continue
                    seen[k] = v
                    op['waits'].append((k, v))
                    if k[0] == 'op':
                        self.ops[k[1]][v]['signal'] = True
        self.sigval = {}
        for e in self.ENGS:
            c = 0
            for i, op in enumerate(self.ops[e]):
                if op['signal']:
                    c += 1
                    self.sigval[(e, i)] = c

    def emit(self):
        nc = self.nc
        st = self.stack
        esem = {e: st.enter_context(nc.semaphore(f"sem_{e}")) for e in self.ENGS}
        dsem = {s: st.enter_context(nc.semaphore("dsem_" + (s if isinstance(s, str) else "_".join(s)))) for s in self.dma_cnt}
        block = st.enter_context(nc.Block())
        names = {'pe': 'tensor', 'act': 'scalar', 'dve': 'vector', 'pool': 'gpsimd', 'sp': 'sync'}
        sched = self

        def make(e):
            def body(eng):
                for i, op in enumerate(sched.ops[e]):
                    for k, v in op['waits']:
                        if k[0] == 'op':
                            eng.wait_ge(esem[k[1]], sched.sigval[(k[1], v)])
                        else:
                            eng.wait_ge(dsem[k[1]], v)
                    if op['fn'] is None:
                        continue
                    ins = op['fn'](eng)
                    if isinstance(op['dma'], tuple):
                        ins.then_inc(dsem[op['dma']])
                    elif op['dma'] is not None:
                        ins.then_inc(dsem[op['dma']], 16)
                    elif op['signal']:
                        ins.then_inc(esem[e], 1)
            return body
        for e in self.ENGS:
            getattr(block, names[e])(make(e))

    def mm(self, out, lhsT, rhs, start=True, stop=True):
        o, l, rh = U(out), U(lhsT), U(rhs)
        return self.add('pe', lambda e: e.matmul(o, l, rh, start=start, stop=stop), r=[lhsT, rhs], w=[out])

    def tr(self, out, in_, ident):
        o, i, d = U(out), U(in_), U(ident)
        return self.add('pe', lambda e: e.transpose(o, i, d), r=[in_, ident], w=[out])

    def act(self, out, in_, func, bias=None, scale=None, accum_out=None):
        kw = {}
        rr = [in_]
        ww = [out]
        if bias is not None:
            kw['bias'] = U(bias)
            if isinstance(bias, V):
                rr.append(bias)
        if scale is not None:
            kw['scale'] = U(scale)
            if isinstance(scale, V):
                rr.append(scale)
        if accum_out is not None:
            kw['accum_out'] = U(accum_out)
            ww.append(accum_out)
        o, i = U(out), U(in_)
        return self.add('act', lambda e: e.activation(o, i, func, **kw), r=rr, w=ww)

    def tt(self, eng, out, a, b, op):
        o, aa, bb = U(out), U(a), U(b)
        return self.add(eng, lambda e: e.tensor_tensor(o, aa, bb, op), r=[a, b], w=[out])

    def ts(self, eng, out, a, s1, s2, op0, op1=None, accum_out=None):
        rr = [a] + [s for s in (s1, s2) if isinstance(s, V)]
        ww = [out] + ([accum_out] if accum_out is not None else [])
        kw = {}
        if accum_out is not None:
            kw['accum_out'] = U(accum_out)
        o, aa, u1, u2 = U(out), U(a), U(s1), U(s2)
        if op1 is None:
            fn = lambda e: e.tensor_scalar(o, aa, u1, None, op0, **kw)
        else:
            fn = lambda e: e.tensor_scalar(o, aa, u1, u2, op0, op1, **kw)
        return self.add(eng, fn, r=rr, w=ww)

    def stt(self, out, a, s, b, op0, op1, accum_out=None):
        rr = [a, b] + ([s] if isinstance(s, V) else [])
        ww = [out] + ([accum_out] if accum_out is not None else [])
        kw = {}
        if accum_out is not None:
            kw['accum_out'] = U(accum_out)
        o, aa, ss, bb = U(out), U(a), U(s), U(b)
        return self.add('dve', lambda e: e.scalar_tensor_tensor(o, aa, ss, bb, op0, op1, **kw), r=rr, w=ww)

    def copy(self, eng, out, in_):
        o, i = U(out), U(in_)
        if eng == 'act':
            fn = lambda e: e.copy(o, i)
        else:
            fn = lambda e: e.tensor_copy(o, i)
        return self.add(eng, fn, r=[in_], w=[out])

    def memset(self, eng, out, val):
        o = U(out)
        return self.add(eng, lambda e: e.memset(o, val), r=[], w=[out])

    def recip(self, out, in_):
        o, i = U(out), U(in_)
        return self.add('dve', lambda e: e.reciprocal(o, i), r=[in_], w=[out])

    def reduce(self, out, in_, op=ALU.add):
        o, i = U(out), U(in_)
        return self.add('dve', lambda e: e.tensor_reduce(o, i, AX.X, op), r=[in_], w=[out])

    def scan(self, out, d0, d1, init, op0, op1):
        o, a, b = U(out), U(d0), U(d1)
        return self.add('dve', lambda e: e.tensor_tensor_scan(o, a, b, init, op0, op1), r=[d0, d1], w=[out])

    def dma(self, q, out, in_, sem):
        o, i = U(out), U(in_)
        return self.add(q, lambda e: e.dma_start(out=o, in_=i), r=[in_], w=[out], dma=sem)


class Arena:
    def __init__(self, S, words):
        self.S = S
        self.words = words
        self.t = S.stack.enter_context(S.nc.sbuf_tensor("arena", [128, words], F32))
        self.off = 0
        self.n = 0
        self.peak = 0
        self.limit = words

    def alloc(self, shape, dtype, name=None):
        free = int(np.prod(shape[1:]))
        w = free if dtype == F32 else (free + 1) // 2
        assert self.off + w <= self.limit, f"arena overflow {self.off}+{w}>{self.limit} ({name})"
        ap = self.t[:, self.off:self.off + w]
        if dtype != F32:
            ap = ap.bitcast(dtype)[:, 0:free]
        self.off += w
        self.peak = max(self.peak, self.off)
        self.n += 1
        key = f"{name or 't'}@{self.off - w}"
        if len(shape) == 3:
            ap = ap.rearrange("p (a b) -> p a b", b=shape[2])
        elif len(shape) == 4:
            ap = ap.rearrange("p (a b c) -> p a b c", b=shape[2], c=shape[3])
        return V(ap, key)

    def alloc_top(self, shape, dtype, name):
        free = int(np.prod(shape[1:]))
        assert dtype == F32
        save = self.off
        self.off = self.words - free
        v = self.alloc(shape, dtype, name)
        self.top_off = self.words - free
        self.off = save
        return v

    def mark(self):
        return self.off

    def release(self, m):
        self.off = m


STOP = None


def build_program(NSEG, T, debug_taps=()):
    NT = T // 128
    BW = min(512, T)
    NB = T // BW
    TT = NSEG * T
    nc = bass.Bass("TRN2", target_bir_lowering=False)
    S = Sched(nc)

    def din(name, shape):
        return nc.dram_tensor(name, list(shape), F32, kind="ExternalInput").ap()
    xT = din("xT", [D, TT])
    memT = din("memT", [D, MEM])
    w_in = din("w_in", [D, IN_COLS])
    w_ba = din("w_ba", [1024, D])
    w_bb = din("w_bb", [1024, D])
    w_mo = din("w_mo", [D, D])
    w_q = din("w_q", [D, 512])
    w_kv = din("w_kv", [D, 1024])
    w_o = din("w_o", [512, D])
    w_fu = din("w_fu", [D, 2 * DFF])
    w_fd = din("w_fd", [DFF, D])
    rw_up = din("rw_up", [64, 1024])
    ra_up = din("ra_up", [64, 1024])
    rg_up = din("rg_up", [160, 1024])
    gcols_d = din("gcols", [128, 7 * 16])
    bif_d = din("bif", [128, 8])
    hncols_d = din("hncols", [128, 8])
    rvec_d = din("rvec", [128, 72])
    mulr_d = din("mulr", [128, 4])
    lngb_d = din("lngb", [128, 2048])
    convc_d = din("convc", [128, 512])
    cst_d = din("cst", [128, 6 * 128])
    mreset_d = din("mreset", [128, T])
    outT = nc.dram_tensor("outT", [D, TT], F32, kind="ExternalOutput").ap()
    xsp = [nc.dram_tensor(f"xspill{i}", [D, T], F32).ap() for i in range(NSEG)]
    cmask_d = din("cmask", [128, 4])
    csel_d = din("csel", [128, 4])
    xh_d = din("xh", [128, 16])
    SROWS = 1536
    summR = nc.dram_tensor("summR", [1024, 128], F32).ap()
    gathR = nc.dram_tensor("gathR", [4 * 1024, 128], F32).ap()
    summM = nc.dram_tensor("summM", [512, 258], F32).ap()
    gathM = nc.dram_tensor("gathM", [4 * 512, 258], F32).ap()
    hin = nc.dram_tensor("hin", [128, 32], F32).ap()
    NIDX = NSEG * 8
    scr_kT = nc.dram_tensor("scr_kT", [NSEG, 128, 4, T], BF16).ap()
    scr_vt = nc.dram_tensor("scr_vt", [NSEG, 128, NT * 4 * 257], BF16).ap()
    scr_gs = nc.dram_tensor("scr_gs", [NSEG, 128, 16 * NT + 8], F32).ap()
    NHC_ = 2 * NT
    scr_rt = nc.dram_tensor("scr_rt", [NIDX, 128, T], BF16).ap()
    scr_G = nc.dram_tensor("scr_G", [NIDX, 128, NHC_, 256], BF16).ap()
    scr_WT = nc.dram_tensor("scr_WT", [NIDX, 128, NT, 128], BF16).ap()
    scr_Upr = nc.dram_tensor("scr_Upr", [NIDX, 128, NHC_, 64], F32).ap()
    scr_tok = nc.dram_tensor("scr_tok", [NIDX, 128, 3, NT, 128], BF16).ap()
    scr_gt = nc.dram_tensor("scr_gt", [NIDX, 128, NT, 128], F32).ap()
    scr_sm = nc.dram_tensor("scr_sm", [NIDX, 128, 3 * NT], F32).ap()
    hout = nc.dram_tensor("hout", [512, 32], F32).ap()
    taps = {}
    for nm, shp in debug_taps:
        taps[nm] = nc.dram_tensor("dbg_" + nm, list(shp), F32, kind="ExternalOutput").ap()

    A = Arena(S, 42 * 1024)
    PB = [V(S.stack.enter_context(nc.psum_tensor(f"PB{i}", [128, 512], F32))[:, :], f"PB{i}") for i in range(8)]
    pbi = [0]

    def bank():
        b = PB[pbi[0] % 8]
        pbi[0] += 1
        return b

    cst_bf = A.alloc([128, 6 * 128], BF16, "cstbf")
    cst_f = A.alloc([128, 6 * 128], F32, "cstf")
    ident = cst_bf[:, 0:128]
    ones_bf = cst_bf[:, 128:256]
    m_su, m_iu, m_sl = cst_bf[:, 256:384], cst_bf[:, 384:512], cst_bf[:, 512:640]
    ident_f = cst_f[:, 0:128]
    ones_f = cst_f[:, 128:256]
    tri_f = cst_f[:, 384:512]
    blk64_f = cst_f[:, 640:768]
    mask4 = A.alloc([128, 512], BF16, "mask4")
    mreset = A.alloc([128, T], F32, "mreset")
    gcols = A.alloc([128, 112], F32, "gcols")
    bif = A.alloc([128, 8], F32, "bif")
    hncols = A.alloc([128, 8], F32, "hncols")
    rvec = A.alloc([128, 72], F32, "rvec")
    omka = A.alloc([128, 8], F32, "omka")
    mulr = A.alloc([128, 4], F32, "mulr")
    lngb = A.alloc([128, 2048], F32, "lngb")
    convc = A.alloc([128, 512], F32, "convc")
    wup_bf = A.alloc([128, 1024], BF16, "wupbf")
    aup_bf = A.alloc([128, 1024], BF16, "aupbf")
    gup_bf = A.alloc([128, 2, 1024], BF16, "gupbf")
    Cst_f = A.alloc([128, 4, 257], F32, "Cst_f")
    Cst_b = A.alloc([128, 4, 257], BF16, "Cst_b")
    Sst_f = A.alloc([128, 8, 64], F32, "Sst_f")
    Sst_b = [A.alloc([128, 8, 128], BF16, f"Sst_b{i}") for i in range(2)]
    scur = [0] * 8
    hhalo = A.alloc([128, 16], BF16, "hhalo")
    hhalo0 = A.alloc([128, 16], BF16, "hhalo0")
    ebtot = A.alloc([128, 4], F32, "ebtot")
    cmask = A.alloc([128, 4], F32, "cmask")
    csel = A.alloc([128, 4], F32, "csel")
    xh = A.alloc([128, 16], F32, "xh")
    h3halo = A.alloc([128, 16, 2], BF16, "h3halo")
    uhalo = A.alloc([128, 128, 2], F32, "uhalo")
    eps_t = A.alloc([128, 2], F32, "eps")
    NWB = 4
    GW = 128
    WB = [A.alloc([128, 16 * GW], BF16, f"WB{i}") for i in range(NWB)]
    wbi = [0]
    xbufs = [A.alloc([128, T], F32, f"xbuf{i}") for i in range(2)]
    xbi = [0]

    for (dst, src) in ((cst_f, cst_d), (mreset, mreset_d), (gcols, gcols_d), (bif, bif_d), (hncols, hncols_d),
                       (rvec, rvec_d), (mulr, mulr_d), (lngb, lngb_d), (convc, convc_d),
                       (cmask, cmask_d), (csel, csel_d), (xh, xh_d)):
        S.dma('sp', dst, src, 'cst_' + dst.key)
    S.dma('pool', cst_bf, cst_d, 'cstb0')
    S.dma('pool', wup_bf[0:64, :], rw_up, 'cstb1')
    S.dma('pool', aup_bf[64:128, :], ra_up, 'cstb2')
    S.dma('pool', gup_bf[:, 0, :], rg_up[0:128, :], 'cstb3')
    S.dma('pool', gup_bf[0:32, 1, :], rg_up[128:160, :], 'cstb4')
    S.ts('dve', omka, rvec[:, 48:56], -1.0, 1.0, ALU.mult, ALU.add)
    for t_ in (Cst_f, Cst_b, Sst_f, Sst_b[0], Sst_b[1], hhalo, uhalo):
        S.memset('dve', t_, 0.0)
    S.memset('dve', eps_t[:, 0:1], EPS)
    S.memset('dve', eps_t[:, 1:2], GN_EPS)
    S.copy('dve', mask4[:, 0:128], m_su)
    S.copy('dve', mask4[:, 128:256], m_su)
    S.copy('dve', mask4[:, 256:384], m_iu)
    S.copy('dve', mask4[:, 384:512], m_iu)

    def tap(nm, v):
        if nm in taps:
            S.dma('pool', taps[nm], v, 'tap')

    def wload(src_ap, kch, ncols, prow=128):
        assert kch * ncols <= 16 * GW
        wb = WB[wbi[0] % NWB]
        wbi[0] += 1
        view = wb[:, 0:kch * ncols].rearrange("p (k n) -> p k n", n=ncols)
        S.dma('pool', view[0:prow], src_ap.rearrange("(kc p) n -> p kc n", p=prow), wb.key)
        return view

    def xload(src_ap):
        xb = xbufs[xbi[0] % 2]
        xbi[0] += 1
        S.dma('sp', xb[:, 0:src_ap.shape[1]], src_ap, xb.key)
        return xb

    def plan(ncol, kch, nsrc, nblk):
        chunks = max(1, 4 // (nsrc * nblk))
        GC = min(128 * chunks, ((ncol + 127) // 128) * 128)
        KP = max(1, min(kch, (16 * GW) // GC))
        return GC, KP

    def proj_fm(wsrc, col0, ncol, kch, rhs, blocks, sink):
        GC, KP = plan(ncol, kch, 1, len(blocks))
        loads = []
        for c in range(0, ncol, GC):
            g = min(GC, ncol - c)
            for k0 in range(0, kch, KP):
                loads.append((c, g, k0, min(KP, kch - k0)))
        def ld(l):
            c_, g_, k_, kp_ = l
            return wload(wsrc[k_ * 128:(k_ + kp_) * 128, col0 + c_: col0 + c_ + g_], kp_, g_)
        nxt = [ld(loads[0]), ld(loads[1]) if len(loads) > 1 else None]
        pss = None
        for li, (c, g, k0, kp) in enumerate(loads):
            wb = nxt[0]
            nxt = [nxt[1], ld(loads[li + 2]) if li + 2 < len(loads) else None]
            subs = list(range(0, g, 128))
            if k0 == 0:
                pss = {(sub, tag): bank() for sub in subs for (tag, width) in blocks}
            for sub in subs:
                m = min(128, g - sub)
                for (tag, width) in blocks:
                    ps = pss[(sub, tag)]
                    for kk in range(kp):
                        kc = k0 + kk
                        S.mm(ps[0:m, 0:width], wb[:, kk, sub:sub + m], rhs(kc, tag), start=(kc == 0), stop=(kc == kch - 1))
            if k0 + kp >= kch:
                for sub in subs:
                    m = min(128, g - sub)
                    for (tag, width) in blocks:
                        sink((c + sub) // 128, tag, pss[(sub, tag)][0:m, 0:width], m)

    def proj2(ws1, c1, k1, rhs1, ws2, c2, k2, rhs2, ncol, blocks, sink):
        GC, _ = plan(ncol, max(k1, k2), 2, len(blocks))
        srcs = ((ws1, c1, k1, rhs1), (ws2, c2, k2, rhs2))
        loads = []
        for c in range(0, ncol, GC):
            g = min(GC, ncol - c)
            for si, (ws, cb, kch, rh) in enumerate(srcs):
                KP = max(1, min(kch, (16 * GW) // GC))
                for k0 in range(0, kch, KP):
                    loads.append((c, g, si, k0, min(KP, kch - k0)))

        def ld(l):
            c, g, si, k0, kp = l
            ws, cb, kch, rh = srcs[si]
            return wload(ws[k0 * 128:(k0 + kp) * 128, cb + c: cb + c + g], kp, g)
        nxt = [ld(loads[0]), ld(loads[1]) if len(loads) > 1 else None]
        pss = {}
        for li, (c, g, si, k0, kp) in enumerate(loads):
            wb = nxt[0]
            nxt = [nxt[1], ld(loads[li + 2]) if li + 2 < len(loads) else None]
            ws, cb, kch, rh = srcs[si]
            subs = list(range(0, g, 128))
            if k0 == 0:
                for sub in subs:
                    for (tag, width) in blocks:
                        pss[(si, sub, tag)] = bank()
            for sub in subs:
                for (tag, width) in blocks:
                    ps = pss[(si, sub, tag)]
                    for kk in range(kp):
                        kc = k0 + kk
                        S.mm(ps[:, 0:width], wb[:, kk, sub:sub + 128], rh(kc, tag), start=(kc == 0), stop=(kc == kch - 1))
            if si == 1 and k0 + kp >= kch:
                for sub in subs:
                    for (tag, width) in blocks:
                        sink((c + sub) // 128, tag, pss[(0, sub, tag)][:, 0:width], pss[(1, sub, tag)][:, 0:width])

    def rms_stats(getx, bw, ncols):
        rstd = A.alloc([128, ncols], F32, "rstd")
        sqs = [A.alloc([128, ncols], BF16, f"sq{i}") for i in range(2)]
        nb = ncols // bw
        pss = [bank() for _ in range(nb)]
        for fc in range(KC):
            sq = sqs[fc % 2]
            S.act(sq, getx(fc), AF.Square)
            for b in range(nb):
                S.mm(pss[b][:, 0:bw], ones_bf, sq[:, b * bw:(b + 1) * bw], start=(fc == 0), stop=(fc == KC - 1))
        for b in range(nb):
            sl = rstd[:, b * bw:(b + 1) * bw]
            S.act(sl, pss[b][:, 0:bw], AF.Sqrt, bias=eps_t[:, 0:1], scale=1.0 / D)
            S.recip(sl, sl)
        return rstd

    TB = [(b, BW) for b in range(NB)]
    R = A.alloc_top([128, 16, T], F32, "R")
    S.memset('dve', ebtot, 1.0)
    S.memset('dve', h3halo, 0.0)
    hsq = A.alloc([128, 16], F32, "hsq")
    hss = A.alloc([128, 2], F32, "hss")
    S.tt('dve', hsq, xh, xh, ALU.mult)
    S.reduce(hss[:, 0:1], hsq)
    ps = bank()
    S.mm(ps[:, 0:1], ones_f, hss[:, 0:1])
    S.act(hss[:, 1:2], ps[:, 0:1], AF.Sqrt, bias=eps_t[:, 0:1], scale=1.0 / D)
    S.recip(hss[:, 1:2], hss[:, 1:2])
    S.tt('dve', hsq, xh, gcols[:, 0:16], ALU.mult)
    S.ts('dve', hhalo0, hsq, hss[:, 1:2], None, ALU.mult)
    base_mark = A.mark()

    def stop_at(nm):
        pass

    def post_norm_residual(Y, gidx, getres, dst_fn):
        m_ = A.mark()
        rs = rms_stats(lambda fc: Y[:, fc, :], BW, T)
        for fc in range(KC):
            S.stt(Y[:, fc, :], Y[:, fc, :], gcols[:, gidx * 16 + fc: gidx * 16 + fc + 1], rs, ALU.mult, ALU.mult)
            dst_fn(fc, getres(fc))
        S.barrier()
        A.release(m_)
    def seg_AB(seg, mode):
        t0 = seg * T
        A.release(ph_mark)
        A.limit = A.words
        seg_mark = A.mark()
        FULL = (mode == 'B')
        if seg == 0:
            S.copy('dve', hhalo, hhalo0)
        hT = A.alloc([128, 16, T + 1], BF16, "hT")
        haT = A.alloc([128, 8, T], BF16, "haT")
        hbT = A.alloc([128, 8, T], BF16, "hbT")
        m_h = A.mark()
        rstd = rms_stats(lambda fc: xload(xT[fc * 128:(fc + 1) * 128, t0:t0 + T])[:, 0:T], BW, T)
        for fc in range(KC):
            xb = xload(xT[fc * 128:(fc + 1) * 128, t0:t0 + T])
            S.stt(hT[:, fc, 1:T + 1], xb[:, 0:T], gcols[:, fc:fc + 1], rstd, ALU.mult, ALU.mult)
        S.copy('dve', hT[:, :, 0], hhalo)
        S.barrier()
        A.release(m_h)
        m_mix = A.mark()

        def hrhs(kc, b):
            if b == 'h':
                return hT[:, kc, 0:1]
            return hT[:, kc, 1 + b * BW: 1 + (b + 1) * BW]

        qT = A.alloc([128, 4, T], BF16, "qT")
        kT = A.alloc([128, 4, T], BF16, "kT")
        soT = A.alloc([128, 8, T], BF16, "soT")
        vtok = A.alloc([128, NT, 4, 257], BF16, "vtok")
        gates = A.alloc([128, NT, 8], F32, "gates")
        graw = A.alloc([128, NT, 8], F32, "graw")
        S.memset('dve', vtok[:, :, :, 256:257], 1.0)

        def sink_qk(cc, b, ps, m):
            if cc < 4:
                S.copy('act', qT[:, cc, b * BW:(b + 1) * BW], ps)
            else:
                S.act(kT[:, cc - 4, b * BW:(b + 1) * BW], ps, AF.Copy, scale=128 ** -0.5)
        if FULL:
            proj_fm(w_in, O_Q, 512, KC, hrhs, TB, sink_qk)
        else:
            proj_fm(w_in, O_K, 512, KC, hrhs, TB, lambda cc, b, ps, m: sink_qk(cc + 4, b, ps, m))

        def sink_o(cc, b, ps, m):
            S.act(soT[:, cc, b * BW:(b + 1) * BW], ps, AF.Sigmoid)
        if FULL:
            proj_fm(w_in, O_O, 1024, KC, hrhs, TB, sink_o)
        gsm = A.alloc([128, 4, NT, 4], F32, "gsm")
        gsm2 = A.alloc([128, 2, 4], F32, "gsm2")
        bcum, imb, expb, wst = gsm[:, 0], gsm[:, 1], gsm[:, 2], gsm[:, 3]
        bend, ebend = gsm2[:, 0], gsm2[:, 1]
        if not FULL:
            for cb in range(8):
                wb = wload(w_in[:, O_V + cb * 128: O_V + (cb + 1) * 128], KC, 128)
                for tt_ in range(NT):
                    ps = bank()
                    for kc in range(KC):
                        S.mm(ps[:, 0:128], hT[:, kc, 1 + tt_ * 128: 1 + (tt_ + 1) * 128], wb[:, kc, 0:128],
                             start=(kc == 0), stop=(kc == KC - 1))
                    S.copy('act' if tt_ % 2 else 'dve', vtok[:, tt_, cb // 2, (cb % 2) * 128:(cb % 2) * 128 + 128], ps[:, 0:128])
            wb = wload(w_in[:, O_I: O_I + 8], KC, 8)
            for tt_ in range(NT):
                ps = bank()
                for kc in range(KC):
                    S.mm(ps[:, 0:8], hT[:, kc, 1 + tt_ * 128: 1 + (tt_ + 1) * 128], wb[:, kc, 0:8],
                         start=(kc == 0), stop=(kc == KC - 1))
                S.tt('dve', graw[:, tt_, :], ps[:, 0:8], bif, ALU.add)
            S.act(graw, graw, AF.Tanh, scale=1.0 / 15.0)
            S.ts('dve', gates[:, :, 0:4], graw[:, :, 0:4], 15.0, None, ALU.mult)
            S.act(graw[:, :, 4:8], graw[:, :, 4:8], AF.Exp, scale=-15.0)
            S.act(graw[:, :, 4:8], graw[:, :, 4:8], AF.Ln, bias=1.0)
            S.ts('dve', gates[:, :, 4:8], graw[:, :, 4:8], -1.0, None, ALU.mult)
            for tt_ in range(NT):
                ps = bank()
                for j in range(tt_ + 1):
                    S.mm(ps[:, 0:4], (tri_f if j == tt_ else ones_f), gates[:, j, 4:8], start=(j == 0), stop=(j == tt_))
                S.copy('dve', bcum[:, tt_, :], ps[:, 0:4])
            ps = bank()
            for j in range(NT):
                S.mm(ps[:, 0:4], ones_f, gates[:, j, 4:8], start=(j == 0), stop=(j == NT - 1))
            S.copy('dve', bend, ps[:, 0:4])
            S.tt('dve', imb, gates[:, :, 0:4], bcum, ALU.subtract)
            S.act(expb, bcum, AF.Exp)
            for hh in range(4):
                S.act(wst[:, :, hh], imb[:, :, hh], AF.Exp, bias=bend[:, hh:hh + 1])
            S.act(ebend, bend, AF.Exp)
            S.dma('sp', scr_kT[seg], kT, 'st_kT')
            S.dma('sp', scr_vt[seg], vtok.rearrange("p n h v -> p (n h v)"), 'st_vt')
            S.dma('sp', scr_gs[seg][:, 0:16 * NT], gsm.rearrange("p a n h -> p (a n h)"), 'st_gs')
            S.dma('sp', scr_gs[seg][:, 16 * NT:16 * NT + 8], gsm2.rearrange("p a h -> p (a h)"), 'st_gs2')
        else:
            S.dma('sp', kT, scr_kT[seg], 'ld_kT')
            S.dma('sp', vtok.rearrange("p n h v -> p (n h v)"), scr_vt[seg], 'ld_vt')
            S.dma('sp', gsm.rearrange("p a n h -> p (a n h)"), scr_gs[seg][:, 0:16 * NT], 'ld_gs')
            S.dma('sp', gsm2.rearrange("p a h -> p (a h)"), scr_gs[seg][:, 16 * NT:16 * NT + 8], 'ld_gs2')
        diag = [A.alloc([128, 128], F32, f"diag{i}") for i in range(2)]
        Dm = [A.alloc([128, 128], F32, f"Dm{i}") for i in range(2)]
        Pm = [A.alloc([128, 128], BF16, f"Pm{i}") for i in range(3)]
        hmn = [A.alloc([128, 256], BF16, f"hmn{i}") for i in range(2)]
        sc = [A.alloc([128, 8], F32, f"sc{i}") for i in range(2)]
        numts = [A.alloc([128, 257], F32, f"numt{i}") for i in range(2)]
        tmpis = [A.alloc([128, 257], F32, f"tmpi{i}") for i in range(2)]
        junk = A.alloc([128, 256], F32, "junk")
        kw_t = [A.alloc([128, 128], BF16, f"kw{i}") for i in range(2)]
        cnt = 0
        for hh in range(4):
            for jt in (range(NT) if FULL else []):
                dg = diag[cnt % 2]
                S.ts('dve', dg, ident_f, bcum[:, jt, hh:hh + 1], None, ALU.mult)
                pbrow = bank()
                S.mm(pbrow[:, 0:128], ones_f, dg)
                pnum = bank()
                for js in range(jt + 1):
                    pst = bank()
                    S.mm(pst[:, 0:128], kT[:, hh, js * 128:(js + 1) * 128], qT[:, hh, jt * 128:(jt + 1) * 128])
                    dm = Dm[js % 2]
                    S.act(dm, pbrow[:, 0:128], AF.Exp, bias=imb[:, js, hh:hh + 1])
                    if js == jt:
                        S.tt('pool', dm, dm, m_iu, ALU.mult)
                    pm = Pm[js % 3]
                    S.tt('dve', pm, pst[:, 0:128], dm, ALU.mult)
                    S.mm(pnum[:, 0:257], pm, vtok[:, js, hh, :], start=(js == 0), stop=(js == jt))
                pint = bank()
                S.mm(pint[:, 0:257], qT[:, hh, jt * 128:(jt + 1) * 128], Cst_b[:, hh, :])
                tmpi = tmpis[cnt % 2]
                S.ts('dve', tmpi, pint[:, 0:257], expb[:, jt, hh:hh + 1], None, ALU.mult)
                s_ = sc[cnt % 2]
                numt = numts[cnt % 2]
                S.tt('dve', numt, pnum[:, 0:257], tmpi, ALU.add)
                S.ts('dve', s_[:, 0:1], numt[:, 256:257], -1.0, None, ALU.mult)
                S.tt('dve', s_[:, 0:1], s_[:, 0:1], numt[:, 256:257], ALU.max)
                S.ts('dve', s_[:, 0:1], s_[:, 0:1], 1.0, None, ALU.max)
                S.recip(s_[:, 0:1], s_[:, 0:1])
                S.stt(junk, numt[:, 0:256], 1.0, numt[:, 0:256], ALU.mult, ALU.mult, accum_out=s_[:, 1:2])
                S.tt('dve', s_[:, 2:3], s_[:, 0:1], s_[:, 0:1], ALU.mult)
                S.tt('dve', s_[:, 2:3], s_[:, 2:3], s_[:, 1:2], ALU.mult)
                S.act(s_[:, 3:4], s_[:, 2:3], AF.Sqrt, bias=eps_t[:, 0:1], scale=1.0 / 256.0)
                S.recip(s_[:, 3:4], s_[:, 3:4])
                S.tt('dve', s_[:, 4:5], s_[:, 3:4], s_[:, 0:1], ALU.mult)
                hm = hmn[cnt % 2]
                S.ts('dve', hm, numt[:, 0:256], s_[:, 4:5], None, ALU.mult)
                ptr = bank()
                ptb = ptr.bitcast(BF16)
                for vc in range(2):
                    S.tr(ptb[:, vc * 128:(vc + 1) * 128], hm[:, vc * 128:(vc + 1) * 128], ident)
                for vc in range(2):
                    fcx = hh * 2 + vc
                    S.stt(haT[:, fcx, jt * 128:(jt + 1) * 128], ptb[:, vc * 128:(vc + 1) * 128],
                          hncols[:, fcx:fcx + 1], soT[:, fcx, jt * 128:(jt + 1) * 128], ALU.mult, ALU.mult)
                cnt += 1
            pc = bank()
            for js in range(NT):
                ptr = bank()
                ptb = ptr.bitcast(BF16)
                S.tr(ptb[:, 0:128], kT[:, hh, js * 128:(js + 1) * 128], ident)
                kw_ = kw_t[js % 2]
                S.ts('dve', kw_, ptb[:, 0:128], wst[:, js, hh:hh + 1], None, ALU.mult)
                S.mm(pc[:, 0:257], kw_, vtok[:, js, hh, :], start=(js == 0), stop=(js == NT - 1))
            S.stt(Cst_f[:, hh, :], Cst_f[:, hh, :], ebend[:, hh:hh + 1], pc[:, 0:257], ALU.mult, ALU.add)
            S.copy('act', Cst_b[:, hh, :], Cst_f[:, hh, :])
        if not FULL:
            S.tt('dve', ebtot, ebtot, ebend, ALU.mult)
        tap("haT", haT.rearrange("p a t -> p (a t)"))
        stop_at("mlstm")
        S.barrier()
        A.release(m_mix)

        HB = TB + [('h', 1)]

        def colof(b):
            return (0, 1) if b == 'h' else (1 + b * BW, BW)
        if not FULL:
            twl = A.alloc([128, T], BF16, "twl")
            sg = A.alloc([128, 2, T], BF16, "sg")
            m_lr = A.mark()
            lrraw = A.alloc([128, 3, T + 1], F32, "lrraw")

            def sink_lr(cc, b, ps, m):
                c0, w_ = colof(b)
                S.copy('act', lrraw[0:m, cc, c0:c0 + w_], ps)
            proj_fm(w_in, O_RWL, 288, KC, hrhs, HB, sink_lr)
            lrt = A.alloc([128, T], F32, "lrt")
            S.tt('dve', lrt, lrraw[:, 0, 0:T], lrraw[:, 0, 1:T + 1], ALU.subtract)
            S.stt(lrt, lrt, mulr[:, 0:1], lrraw[:, 0, 1:T + 1], ALU.mult, ALU.add)
            S.act(twl[0:64, :], lrt[0:64, :], AF.Tanh)
            S.copy('dve', twl[64:128, :], lrt[64:128, :])
            S.tt('dve', lrt, lrraw[:, 1, 0:T], lrraw[:, 1, 1:T + 1], ALU.subtract)
            S.stt(lrt, lrt, mulr[:, 1:2], lrraw[:, 1, 1:T + 1], ALU.mult, ALU.add)
            S.act(sg[:, 0, :], lrt, AF.Sigmoid)
            S.tt('dve', lrt[0:32, :], lrraw[0:32, 2, 0:T], lrraw[0:32, 2, 1:T + 1], ALU.subtract)
            S.stt(lrt[0:32, :], lrt[0:32, :], mulr[0:32, 2:3], lrraw[0:32, 2, 1:T + 1], ALU.mult, ALU.add)
            S.act(sg[0:32, 1, :], lrt[0:32, :], AF.Sigmoid)
            S.barrier()
            A.release(m_lr)
        stop_at('rw_lr')
        m_rw = A.mark()
        for hp in range(8):
            A.release(m_rw)
            cs_ = slice(hp * 128, (hp + 1) * 128)
            bon = A.alloc([128, NT, 2], F32, "bon")
            gL = A.alloc([128, NT], F32, "gL")
            rt = A.alloc([128, T], BF16, "rt")
            at = A.alloc([128, T], BF16, "at")
            bt = A.alloc([128, T], BF16, "bt")
            kt = A.alloc([128, T], BF16, "kt")
            bh = A.alloc([128, T], BF16, "bh")
            kh = A.alloc([128, T], BF16, "kh")
            vb = A.alloc([128, T], BF16, "vb")
            NHC = 2 * NT
            sidx = seg * 8 + hp
            if not FULL:
                m_tmp = A.mark()
                praw = A.alloc([128, T + 1], F32, "praw")

                def sk(cc, b, ps, m):
                    c0, w_ = colof(b)
                    S.copy('act', praw[:, c0:c0 + w_], ps)
                rr_ = A.alloc([128, T], F32, "r")
                kr_ = A.alloc([128, T], F32, "kr")
                vr_ = A.alloc([128, T], F32, "vr")
                tmp = A.alloc([128, T], F32, "tmp")
                ka = A.alloc([128, T], F32, "ka")
                tmp2 = ka
                for i, (off, dst) in enumerate(((O_RR, rr_), (O_RK, kr_), (O_RV, vr_))):
                    proj_fm(w_in, off + hp * 128, 128, KC, hrhs, HB, sk)
                    S.tt('dve', tmp, praw[:, 0:T], praw[:, 1:T + 1], ALU.subtract)
                    S.stt(dst, tmp, rvec[:, i * 8 + hp: i * 8 + hp + 1], praw[:, 1:T + 1], ALU.mult, ALU.add)
                lw = A.alloc([128, T], F32, "lw")
                a_ = A.alloc([128, T], F32, "a")
                for b in range(NB):
                    bs = slice(b * BW, (b + 1) * BW)
                    ps = bank()
                    S.mm(ps[:, 0:BW], wup_bf[0:64, cs_], twl[0:64, bs])
                    S.act(lw[:, bs], ps[:, 0:BW], AF.Sigmoid, bias=rvec[:, 24 + hp:25 + hp])
                    ps = bank()
                    S.mm(ps[:, 0:BW], aup_bf[64:128, cs_], twl[64:128, bs])
                    S.act(a_[:, bs], ps[:, 0:BW], AF.Sigmoid, bias=rvec[:, 32 + hp:33 + hp])
                S.ts('dve', lw, lw, -0.6065306597126334, None, ALU.mult)
                kap = A.alloc([128, T], F32, "kap")
                S.ts('dve', kap, kr_, rvec[:, 40 + hp:41 + hp], None, ALU.mult)
                S.tt('dve', tmp, kap, kap, ALU.mult)
                for b in range(NB):
                    bs = slice(b * BW, (b + 1) * BW)
                    ps = bank()
                    S.mm(ps[:, 0:BW], blk64_f, tmp[:, bs])
                    S.act(tmp2[:, bs], ps[:, 0:BW], AF.Sqrt)
                S.ts('dve', tmp2, tmp2, 1e-12, None, ALU.max)
                S.recip(tmp2, tmp2)
                S.tt('dve', kap, kap, tmp2, ALU.mult)
                S.ts('dve', tmp, a_, rvec[:, 48 + hp:49 + hp], omka[:, hp:hp + 1], ALU.mult, ALU.add)
                S.tt('dve', kr_, kr_, tmp, ALU.mult)
                S.tt('dve', tmp, rr_, kr_, ALU.mult)
                S.ts('dve', tmp, tmp, rvec[:, 56 + hp:57 + hp], None, ALU.mult)
                for n in range(NT):
                    ps = bank()
                    S.mm(ps[:, 0:2], tmp[:, n * 128:(n + 1) * 128], blk64_f[:, 0:128:64])
                    S.copy('dve', bon[:, n, :], ps[:, 0:2])
                S.tt('dve', ka, kap, a_, ALU.mult)
                cs = A.alloc([128, T], F32, "cs")
                S.scan(cs, mreset, lw, 0.0, ALU.mult, ALU.add)
                cs3 = cs.rearrange("p (n t) -> p n t", t=128)
                S.act(gL, cs3[:, :, 127], AF.Exp)
                S.copy('pool', vb, vr_)
                S.act(tmp, cs, AF.Exp)
                S.tt('dve', rt, rr_, tmp, ALU.mult)
                S.tt('dve', tmp, cs, lw, ALU.subtract)
                S.act(tmp, tmp, AF.Exp)
                S.stt(at, kap, -1.0, tmp, ALU.mult, ALU.mult)
                S.act(tmp, cs, AF.Exp, scale=-1.0)
                S.tt('dve', bt, ka, tmp, ALU.mult)
                S.tt('dve', kt, kr_, tmp, ALU.mult)
                tmp3 = tmp.rearrange("p (n t) -> p n t", t=128)
                S.tt('dve', tmp3, cs3[:, :, 127:128].to_broadcast([128, NT, 128]), cs3, ALU.subtract)
                S.act(tmp, tmp, AF.Exp)
                S.tt('dve', bh, ka, tmp, ALU.mult)
                S.tt('dve', kh, kr_, tmp, ALU.mult)
                stop_at('rw_proj')
                S.barrier()
                A.release(m_tmp)
                toks = []
                for X in (at, bh, kh, vb):
                    Xt = A.alloc([128, NT, 128], BF16, "tok")
                    ptr = bank()
                    ptb = ptr.bitcast(BF16)
                    for n in range(NT):
                        S.tr(ptb[:, n * 128:(n + 1) * 128], X[:, n * 128:(n + 1) * 128], ident)
                    S.copy('act', Xt.rearrange("p n c -> p (n c)"), ptb[:, 0:NT * 128])
                    toks.append(Xt)
                a_tok, bh_tok, kh_tok, v_tok = toks
                g_tok = A.alloc([128, NT, 128], F32, "gtok")
                for n in range(NT):
                    ps = bank()
                    S.mm(ps[:, 0:128], sg[:, 0, n * 128:(n + 1) * 128], gup_bf[:, 0, cs_], start=True, stop=False)
                    S.mm(ps[:, 0:128], sg[0:32, 1, n * 128:(n + 1) * 128], gup_bf[0:32, 1, cs_], start=False, stop=True)
                    S.copy('act', g_tok[:, n, :], ps[:, 0:128])
                NHC = 2 * NT
                G4 = A.alloc([128, NHC, 512], BF16, "G4")
                PP = [A.alloc([128, NHC, 256], BF16, f"PP{i}") for i in range(2)]
                TTm = [A.alloc([128, NHC, 128], BF16, f"TT{i}") for i in range(2)]
                RH2 = A.alloc([128, NHC, 64], BF16, "RH2")
                WT = A.alloc([128, NT, 128], BF16, "WT")
                Upr = A.alloc([128, NHC, 64], F32, "Upr")
                for h in range(2):
                    hs = slice(h * 64, (h + 1) * 64)
                    for n in range(NT):
                        hc = h * NT + n
                        ch = slice(n * 128, (n + 1) * 128)
                        pg = bank()
                        S.mm(pg[:, 0:128], bt[hs, ch], at[hs, ch])
                        S.mm(pg[:, 128:256], kt[hs, ch], at[hs, ch])
                        S.mm(pg[:, 256:384], bt[hs, ch], rt[hs, ch])
                        S.mm(pg[:, 384:512], kt[hs, ch], rt[hs, ch])
                        S.tt('dve', G4[:, hc, :], pg[:, 0:512], mask4, ALU.mult)
                        pa = bank()
                        S.mm(pa[:, 0:128], at[hs, ch], bt[hs, ch])
                        S.tt('dve', PP[0][:, hc, 0:128], pa[:, 0:128], m_sl, ALU.mult)
                        S.copy('pool', PP[0][:, hc, 128:256], G4[:, hc, 0:128])
                        S.tt('pool', TTm[0][:, hc, :], G4[:, hc, 0:128], ident, ALU.add)
                        p2 = bank()
                        S.mm(p2[:, 0:64], G4[:, hc, 128:256], v_tok[:, n, hs])
                        S.copy('act', RH2[:, hc, :], p2[:, 0:64])
                for lev in range(1, 7):
                    src, dst = PP[(lev - 1) % 2], PP[lev % 2]
                    tsrc, tdst = TTm[(lev - 1) % 2], TTm[lev % 2]
                    for hc in range(NHC):
                        pq = bank()
                        S.mm(pq[:, 0:128], src[:, hc, 128:256], src[:, hc, 0:128])
                        if lev < 6:
                            S.mm(pq[:, 128:256], src[:, hc, 0:128], src[:, hc, 128:256])
                            S.copy('act', dst[:, hc, :], pq[:, 0:256])
                        else:
                            S.copy('act', dst[:, hc, 0:128], pq[:, 0:128])
                        pt_ = bank()
                        S.mm(pt_[:, 0:128], dst[:, hc, 0:128], tsrc[:, hc, :])
                        S.tt('dve', tdst[:, hc, :], pt_[:, 0:128], tsrc[:, hc, :], ALU.add)
                TTf = TTm[0]
                for h in range(2):
                    hs = slice(h * 64, (h + 1) * 64)
                    for n in range(NT):
                        hc = h * NT + n
                        pw = bank()
                        S.mm(pw[hs, 0:128], a_tok[:, n, hs], TTf[:, hc, :])
                        S.copy('act', WT[hs, n, :], pw[hs, 0:128])
                        pu = bank()
                        S.mm(pu[:, 0:64], TTf[:, hc, :], RH2[:, hc, :])
                        S.copy('dve', Upr[:, hc, :], pu[:, 0:64])
                S.dma('sp', scr_rt[sidx], rt, 'st_rt')
                S.dma('sp', scr_G[sidx], G4[:, :, 256:512], 'st_G')
                S.dma('sp', scr_WT[sidx], WT, 'st_WT')
                S.dma('sp', scr_Upr[sidx], Upr, 'st_Upr')
                S.dma('sp', scr_tok[sidx][:, 0], bh_tok, 'st_t0')
                S.dma('sp', scr_tok[sidx][:, 1], kh_tok, 'st_t1')
                S.dma('sp', scr_tok[sidx][:, 2], v_tok, 'st_t2')
                S.dma('sp', scr_gt[sidx], g_tok, 'st_gt')
                S.dma('sp', scr_sm[sidx][:, 0:NT], gL, 'st_gl')
                S.dma('sp', scr_sm[sidx][:, NT:3 * NT], bon.rearrange("p n h -> p (n h)"), 'st_bon')
            else:
                bh_tok = A.alloc([128, NT, 128], BF16, "tok")
                kh_tok = A.alloc([128, NT, 128], BF16, "tok")
                v_tok = A.alloc([128, NT, 128], BF16, "tok")
                g_tok = A.alloc([128, NT, 128], F32, "gtok")
                G4 = A.alloc([128, NHC, 512], BF16, "G4")
                WT = A.alloc([128, NT, 128], BF16, "WT")
                Upr = A.alloc([128, NHC, 64], F32, "Upr")
                S.dma('sp', rt, scr_rt[sidx], 'ld_rt')
                S.dma('sp', G4[:, :, 256:512], scr_G[sidx], 'ld_G')
                S.dma('sp', WT, scr_WT[sidx], 'ld_WT')
                S.dma('sp', Upr, scr_Upr[sidx], 'ld_Upr')
                S.dma('sp', bh_tok, scr_tok[sidx][:, 0], 'ld_t0')
                S.dma('sp', kh_tok, scr_tok[sidx][:, 1], 'ld_t1')
                S.dma('sp', v_tok, scr_tok[sidx][:, 2], 'ld_t2')
                S.dma('sp', g_tok, scr_gt[sidx], 'ld_gt')
                S.dma('sp', gL, scr_sm[sidx][:, 0:NT], 'ld_gl')
                S.dma('sp', bon.rearrange("p n h -> p (n h)"), scr_sm[sidx][:, NT:3 * NT], 'ld_bon')
            stop_at('rw_gram')
            W_ = 64 if FULL else 128
            if FULL:
                Sf_, Sb_, cur_ = Sst_f, Sst_b, scur
            else:
                Sf_, Sb_, cur_ = SfA, SbA, scurA
            y_tok = A.alloc([128, NT, 128], F32, "ytok")
            UTb = [A.alloc([128, 2, W_], BF16, f"UTb{i}") for i in range(2)]
            for n in range(NT):
                ch = slice(n * 128, (n + 1) * 128)
                Sold = Sb_[cur_[hp]]
                Snew = Sb_[1 - cur_[hp]]
                ut = UTb[n % 2]
                pu = bank()
                S.mm(pu[:, 0:2 * W_], WT[:, n, :], Sold[:, hp, :])
                for h in range(2):
                    S.tt('dve', ut[:, h, 0:64], pu[:, h * W_:h * W_ + 64], Upr[:, h * NT + n, :], ALU.add)
                    if not FULL:
                        S.copy('act', ut[:, h, 64:128], pu[:, h * W_ + 64:(h + 1) * W_])
                ps_ = bank()
                if FULL:
                    py = bank()
                    S.mm(py[:, 0:128], rt[:, ch], Sold[:, hp, :], start=True, stop=False)
                for h in range(2):
                    hs = slice(h * 64, (h + 1) * 64)
                    hc = h * NT + n
                    S.mm(ps_[hs, 0:W_], bh_tok[:, n, hs], ut[:, h, :], start=True, stop=False)
                    S.mm(ps_[hs, 0:64], kh_tok[:, n, hs], v_tok[:, n, hs], start=False, stop=True)
                    if FULL:
                        S.mm(py[:, h * 64:(h + 1) * 64], G4[:, hc, 256:384], ut[:, h, :], start=False, stop=False)
                        S.mm(py[:, h * 64:(h + 1) * 64], G4[:, hc, 384:512], v_tok[:, n, hs], start=False, stop=True)
                S.stt(Sf_[:, hp, :], Sf_[:, hp, :], gL[:, n:n + 1], ps_[:, 0:W_], ALU.mult, ALU.add)
                S.copy('act', Snew[0:64, hp, 0:W_], Sf_[0:64, hp, :])
                S.copy('act', Snew[64:128, hp, W_:2 * W_], Sf_[64:128, hp, :])
                cur_[hp] = 1 - cur_[hp]
                if FULL:
                    S.copy('act', y_tok[:, n, :], py[:, 0:128])
            if not FULL:
                S.barrier()
                continue
            stop_at('rw_seq')
            y4 = y_tok.rearrange("p n (h v) -> p (n h) v", v=64)
            st1 = A.alloc([128, NHC], F32, "st1")
            st2 = A.alloc([128, NHC], F32, "st2")
            yc = A.alloc([128, NHC, 64], F32, "yc")
            ysq = A.alloc([128, NHC, 64], F32, "ysq")
            S.reduce(st1, y4)
            S.ts('dve', st1, st1, 1.0 / 64.0, None, ALU.mult)
            S.tt('dve', yc, y4, st1.rearrange("p (a o) -> p a o", o=1).to_broadcast([128, NHC, 64]), ALU.subtract)
            S.tt('pool', ysq, yc, yc, ALU.mult)
            S.reduce(st2, ysq)
            S.act(st2, st2, AF.Sqrt, bias=eps_t[:, 1:2], scale=1.0 / 64.0)
            S.recip(st2, st2)
            S.tt('dve', yc, yc, st2.rearrange("p (a o) -> p a o", o=1).to_broadcast([128, NHC, 64]), ALU.mult)
            yc3 = yc.rearrange("p (n h) v -> p n (h v)", h=2)
            lg = lngb[:, hp * 128:(hp + 1) * 128].rearrange("p (o c) -> p o c", o=1).to_broadcast([128, NT, 128])
            lb = lngb[:, 1024 + hp * 128:1024 + (hp + 1) * 128].rearrange("p (o c) -> p o c", o=1).to_broadcast([128, NT, 128])
            S.tt('dve', yc3, yc3, lg, ALU.mult)
            S.tt('dve', yc3, yc3, lb, ALU.add)
            bv = ysq
            S.tt('dve', bv, v_tok.rearrange("p n (h v) -> p (n h) v", v=64),
                 bon.rearrange("p n (h o) -> p (n h) o", o=1).to_broadcast([128, NHC, 64]), ALU.mult)
            S.tt('dve', yc, yc, bv, ALU.add)
            hb_tok = A.alloc([128, NT, 128], BF16, "hbtok")
            S.tt('dve', hb_tok, yc3, g_tok, ALU.mult)
            ptr = bank()
            ptb = ptr.bitcast(BF16)
            for n in range(NT):
                S.tr(ptb[:, n * 128:(n + 1) * 128], hb_tok[:, n, :], ident)
            S.copy('act', hbT[:, hp, :], ptb[:, 0:T])
        tap("hbT", hbT.rearrange("p a t -> p (a t)"))
        stop_at("rwkv")
        S.copy('dve', hhalo, hT[:, :, T])
        A.release(m_mix)
        if not FULL:
            S.barrier()
            return

        A.limit = A.top_off
        mergedT = A.alloc([128, 16, T], BF16, "mergedT")
        sgt = [A.alloc([128, BW], F32, f"sgt{i}") for i in range(2)]
        sgi = [0]

        def hrhs2(kc, b):
            return hT[:, kc, 1 + b * BW: 1 + (b + 1) * BW]

        def sink_ma(cc, b, p1, p2):
            t_ = sgt[sgi[0] % 2]
            sgi[0] += 1
            S.act(t_, p1, AF.Sigmoid)
            S.tt('dve', mergedT[:, cc, b * BW:(b + 1) * BW], t_, p2, ALU.mult)
        proj2(w_in, O_GA, KC, hrhs2, w_ba, 0, 8, lambda kc, b: haT[:, kc, b * BW:(b + 1) * BW], 2048, TB, sink_ma)

        def sink_mb(cc, b, p1, p2):
            t_ = sgt[sgi[0] % 2]
            sgi[0] += 1
            S.act(t_, p1, AF.Sigmoid)
            S.tt('dve', t_, t_, p2, ALU.mult)
            S.tt('pool', mergedT[:, cc, b * BW:(b + 1) * BW], mergedT[:, cc, b * BW:(b + 1) * BW], t_, ALU.add)
        proj2(w_in, O_GB, KC, hrhs2, w_bb, 0, 8, lambda kc, b: hbT[:, kc, b * BW:(b + 1) * BW], 2048, TB, sink_mb)

        def sink_R(cc, b, ps, m):
            S.copy('act' if (cc + b) % 2 else 'dve', R[:, cc, b * BW:(b + 1) * BW], ps)
        proj_fm(w_mo, 0, 2048, KC, lambda kc, b: mergedT[:, kc, b * BW:(b + 1) * BW], TB, sink_R)
        S.barrier()
        A.release(seg_mark)

        def post_norm_residual(Y, gidx, getres, dst_fn):
            m_ = A.mark()
            rs = rms_stats(lambda fc: Y[:, fc, :], BW, T)
            for fc in range(KC):
                S.stt(Y[:, fc, :], Y[:, fc, :], gcols[:, gidx * 16 + fc: gidx * 16 + fc + 1], rs, ALU.mult, ALU.mult)
                dst_fn(fc, getres(fc))
            S.barrier()
            A.release(m_)

        def dst_R(fc, res):
            S.tt('dve', R[:, fc, :], R[:, fc, :], res, ALU.add)
        post_norm_residual(R, 1, lambda fc: xload(xT[fc * 128:(fc + 1) * 128, t0:t0 + T])[:, 0:T], dst_R)
        tap("x1", R.rearrange("p a t -> p (a t)"))
        stop_at("merge")

        def pre_norm(gidx):
            h_ = A.alloc([128, 16, T], BF16, "hTn")
            m_ = A.mark()
            rs = rms_stats(lambda fc: R[:, fc, :], BW, T)
            for fc in range(KC):
                S.stt(h_[:, fc, :], R[:, fc, :], gcols[:, gidx * 16 + fc: gidx * 16 + fc + 1], rs, ALU.mult, ALU.mult)
            S.barrier()
            A.release(m_)
            return h_
        oT = A.alloc([128, 4, T], BF16, "oT")
        m_xa = A.mark()
        hT2 = pre_norm(2)
        mnT = A.alloc([128, 16, MEM], BF16, "mnT")
        m_ = A.mark()
        rsm = rms_stats(lambda fc: xload(memT[fc * 128:(fc + 1) * 128, :])[:, 0:MEM], MEM, MEM)
        for fc in range(KC):
            xb = xload(memT[fc * 128:(fc + 1) * 128, :])
            S.stt(mnT[:, fc, :], xb[:, 0:MEM], gcols[:, 3 * 16 + fc: 3 * 16 + fc + 1], rsm, ALU.mult, ALU.mult)
        S.barrier()
        A.release(m_)
        kmT = A.alloc([128, 4, MEM], BF16, "kmT")
        vm = A.alloc([128, 2, 512], BF16, "vm")
        qT2 = A.alloc([128, 4, T], BF16, "qT2")

        def sink_km(cc, b, ps, m):
            S.copy('act', kmT[:, cc, :], ps)
        proj_fm(w_kv, 0, 512, KC, lambda kc, b: mnT[:, kc, :], [(0, MEM)], sink_km)
        for cb in range(4):
            wb = wload(w_kv[:, 512 + cb * 128: 512 + (cb + 1) * 128], KC, 128)
            for mt in range(2):
                ps = bank()
                for kc in range(KC):
                    S.mm(ps[:, 0:128], mnT[:, kc, mt * 128:(mt + 1) * 128], wb[:, kc, 0:128],
                         start=(kc == 0), stop=(kc == KC - 1))
                S.copy('dve', vm[:, mt, cb * 128:(cb + 1) * 128], ps[:, 0:128])

        def sink_q2(cc, b, ps, m):
            S.act(qT2[:, cc, b * BW:(b + 1) * BW], ps, AF.Copy, scale=128 ** -0.5)
        proj_fm(w_q, 0, 512, KC, lambda kc, b: hT2[:, kc, b * BW:(b + 1) * BW], TB, sink_q2)
        pex = [A.alloc([128, MEM], F32, f"pex{i}") for i in range(2)]
        pnb = [A.alloc([128, MEM], BF16, f"pnb{i}") for i in range(2)]
        pTt = [A.alloc([128, 2, 128], BF16, f"pTt{i}") for i in range(2)]
        sm = [A.alloc([128, 4], F32, f"sm{i}") for i in range(2)]
        c2 = 0
        for h in range(4):
            for tt_ in range(NT):
                ts_ = slice(tt_ * 128, (tt_ + 1) * 128)
                psc = bank()
                S.mm(psc[:, 0:MEM], qT2[:, h, ts_], kmT[:, h, :])
                s_ = sm[c2 % 2]
                S.add('dve', (lambda o, i: (lambda e: e.tensor_reduce(o, i, AX.X, ALU.max)))(U(s_[:, 0:1]), U(psc[:, 0:MEM])),
                      r=[psc], w=[s_])
                S.ts('dve', s_[:, 1:2], s_[:, 0:1], -1.0, None, ALU.mult)
                pe_ = pex[c2 % 2]
                S.act(pe_, psc[:, 0:MEM], AF.Exp, bias=s_[:, 1:2], accum_out=s_[:, 2:3])
                S.recip(s_[:, 3:4], s_[:, 2:3])
                pn_ = pnb[c2 % 2]
                S.ts('dve', pn_, pe_, s_[:, 3:4], None, ALU.mult)
                ptr = bank()
                ptb = ptr.bitcast(BF16)
                for mt in range(2):
                    S.tr(ptb[:, mt * 128:(mt + 1) * 128], pn_[:, mt * 128:(mt + 1) * 128], ident)
                pt2 = pTt[c2 % 2]
                S.copy('act', pt2.rearrange("p a t -> p (a t)"), ptb[:, 0:256])
                po = bank()
                for mt in range(2):
                    S.mm(po[:, 0:128], vm[:, mt, h * 128:(h + 1) * 128], pt2[:, mt, :], start=(mt == 0), stop=(mt == 1))
                S.copy('dve', oT[:, h, ts_], po[:, 0:128])
                c2 += 1
        S.barrier()
        A.release(m_xa)
        Y2 = A.alloc([128, 16, T], F32, "Y2")

        def sink_Y2(cc, b, ps, m):
            S.copy('act' if (cc + b) % 2 else 'dve', Y2[:, cc, b * BW:(b + 1) * BW], ps)
        proj_fm(w_o, 0, 2048, 4, lambda kc, b: oT[:, kc, b * BW:(b + 1) * BW], TB, sink_Y2)

        def dst_R2(fc, res):
            S.tt('dve', R[:, fc, :], R[:, fc, :], res, ALU.add)
        post_norm_residual(Y2, 4, lambda fc: Y2[:, fc, :], dst_R2)
        tap("x2", R.rearrange("p a t -> p (a t)"))
        stop_at("xattn")
        S.barrier()
        A.release(seg_mark)
        S.dma('sp', xsp[seg].rearrange("(fc p) t -> p fc t", p=128), R, f'spill{seg}')
        if seg == NSEG - 1:
            sq2 = A.alloc([128, 16, 2], BF16, "sq2")
            S.act(sq2, R[:, :, T - 2:T], AF.Square)
            ps = bank()
            for fc in range(KC):
                S.mm(ps[:, 0:2], ones_bf, sq2[:, fc, :], start=(fc == 0), stop=(fc == KC - 1))
            rs2 = A.alloc([128, 2], F32, "rs2")
            S.act(rs2, ps[:, 0:2], AF.Sqrt, bias=eps_t[:, 0:1], scale=1.0 / D)
            S.recip(rs2, rs2)
            h3h = A.alloc([128, 16, 2], F32, "h3h")
            S.tt('dve', h3h, R[:, :, T - 2:T],
                 gcols[:, 80:96].rearrange("p (a o) -> p a o", o=1).to_broadcast([128, 16, 2]), ALU.mult)
            S.tt('dve', h3h, h3h, rs2.rearrange("p (o t) -> p o t", o=1).to_broadcast([128, 16, 2]), ALU.mult)
            S.dma('sp', hin, h3h.rearrange("p a t -> p (a t)"), 'hin')
        S.barrier()


    def seg_C(seg):
        t0 = seg * T
        A.release(ph_mark)
        A.limit = A.top_off
        hT3 = A.alloc([128, 16, T], BF16, "hT3")
        m_ = A.mark()
        rs3 = rms_stats(lambda fc: xload(xsp[seg][fc * 128:(fc + 1) * 128, :])[:, 0:T], BW, T)
        for fc in range(KC):
            xb = xload(xsp[seg][fc * 128:(fc + 1) * 128, :])
            S.stt(hT3[:, fc, :], xb[:, 0:T], gcols[:, 80 + fc:81 + fc], rs3, ALU.mult, ALU.mult)
        S.barrier()
        A.release(m_)
        ACC = R
        FB = ([('h', 2)] if seg == 0 else []) + TB

        def h3rhs(kc, b):
            if b == 'h':
                return h3halo[:, kc, :]
            return hT3[:, kc, b * BW:(b + 1) * BW]
        GF = 8
        actT = A.alloc([128, GF, T], BF16, "actT")
        ug = [A.alloc([128, T + 2], F32, f"ug{i}") for i in range(2)]
        uu = [A.alloc([128, T + 2], F32, f"uu{i}") for i in range(2)]
        cgs = [A.alloc([128, BW], F32, f"cg{i}") for i in range(2)]
        cus = [A.alloc([128, BW], F32, f"cu{i}") for i in range(2)]
        pls = [A.alloc([128, BW], F32, f"pl{i}") for i in range(2)]
        ci = [0]
        for g in range(DFF // 128 // GF):
            def sink_f(cc, b, pg_, pu_, g=g):
                f = g * GF + cc
                ug_, uu_ = ug[f % 2], uu[f % 2]
                if b == 'h':
                    S.copy('act', ug_[:, 0:2], pg_)
                    S.copy('act', uu_[:, 0:2], pu_)
                    return
                bs2 = slice(2 + b * BW, 2 + (b + 1) * BW)
                if b == 0 and seg > 0:
                    S.copy('pool', ug_[:, 0:2], uhalo[:, f, :])
                    S.copy('pool', uu_[:, 0:2], uhalo[:, 64 + f, :])
                S.copy('act', ug_[:, bs2], pg_)
                S.copy('act', uu_[:, bs2], pu_)
                k_ = ci[0] % 2
                ci[0] += 1
                cg, cu, pl = cgs[k_], cus[k_], pls[k_]
                for (src, dst, chn) in ((ug_, cg, f), (uu_, cu, 64 + f)):
                    S.ts('dve', dst, src[:, 2 + b * BW: 2 + (b + 1) * BW], convc[:, 256 + chn:257 + chn],
                         convc[:, 384 + chn:385 + chn], ALU.mult, ALU.add)
                    S.stt(dst, src[:, 1 + b * BW: 1 + (b + 1) * BW], convc[:, 128 + chn:129 + chn], dst, ALU.mult, ALU.add)
                    S.stt(dst, src[:, b * BW: (b + 1) * BW], convc[:, chn:chn + 1], dst, ALU.mult, ALU.add)
                S.tt('pool', pl, cg, cg, ALU.mult)
                S.ts('pool', pl, pl, 0.044715, 1.0, ALU.mult, ALU.add)
                S.tt('pool', pl, pl, cg, ALU.mult)
                S.act(pl, pl, AF.Sigmoid, scale=1.5957691216057308)
                S.tt('dve', cg, cg, pl, ALU.mult)
                S.tt('dve', actT[:, cc, b * BW:(b + 1) * BW], cg, cu, ALU.mult)
                if b == NB - 1:
                    S.copy('pool', uhalo[:, f, :], ug_[:, T:T + 2])
                    S.copy('pool', uhalo[:, 64 + f, :], uu_[:, T:T + 2])
            proj2(w_fu, g * GF * 128, KC, h3rhs,
                  w_fu, DFF + g * GF * 128, KC, h3rhs, GF * 128, FB, sink_f)

            def sink_acc(cc, b, ps, m, g=g):
                dst = ACC[:, cc, b * BW:(b + 1) * BW]
                if g == 0:
                    S.copy('act', dst, ps)
                else:
                    S.tt('dve', dst, dst, ps, ALU.add)
            proj_fm(w_fd[g * GF * 128:(g + 1) * GF * 128, :], 0, 2048, GF,
                    lambda kc, b: actT[:, kc, b * BW:(b + 1) * BW], TB, sink_acc)

        def dst_out(fc, res):
            S.tt('dve', ACC[:, fc, :], ACC[:, fc, :], res, ALU.add)
            S.dma('sp', outT[fc * 128:(fc + 1) * 128, t0:t0 + T], ACC[:, fc, :], 'out')
        post_norm_residual(ACC, 6, lambda fc: xload(xsp[seg][fc * 128:(fc + 1) * 128, :])[:, 0:T], dst_out)
        S.barrier()


    A.release(base_mark)
    SfA = A.alloc([128, 8, 128], F32, "SfA")
    SbA = [A.alloc([128, 8, 256], BF16, f"SbA{i}") for i in range(2)]
    scurA = [0] * 8
    S.memset('dve', SfA, 0.0)
    S.memset('dve', SbA[0], 0.0)
    S.memset('dve', SbA[1], 0.0)
    for hp in range(8):
        S.copy('dve', SfA[0:64, hp, 64:128], ident_f[0:64, 0:64])
        S.copy('dve', SfA[64:128, hp, 64:128], ident_f[64:128, 64:128])
        S.copy('dve', SbA[0][0:64, hp, 64:128], ident_f[0:64, 0:64])
        S.copy('dve', SbA[0][64:128, hp, 192:256], ident_f[64:128, 64:128])
    ph_mark = A.mark()
    for seg in range(NSEG):
        seg_AB(seg, 'A')
    A.release(ph_mark)
    S.dma('sp', summR.rearrange("(hp p) c -> p hp c", p=128), SfA, 'ex0')
    Cex = A.alloc([128, 4, 258], F32, "Cex")
    S.copy('dve', Cex[:, :, 0:257], Cst_f)
    S.copy('dve', Cex[:, :, 257], ebtot)
    S.dma('sp', summM.rearrange("(h p) c -> p h c", p=128), Cex, 'ex1')
    S.add('pool', lambda e: e.collective_compute("AllGather", ALU.bypass, replica_groups=[[0, 1, 2, 3], [4, 5, 6, 7]],
                                                 ins=[summR], outs=[gathR]),
          r=[summR], w=[gathR], dma=('cc', 'cc1'))
    S.add('pool', lambda e: e.collective_compute("AllGather", ALU.bypass, replica_groups=[[0, 1, 2, 3], [4, 5, 6, 7]],
                                                 ins=[summM], outs=[gathM]),
          r=[summM], w=[gathM], dma=('cc', 'cc1b'))
    ccd = A.alloc([128, 2], F32, "ccd")
    S.add('pool', lambda e: e.memset(U(ccd), 0.0), r=[gathR, gathM], w=[gathR, gathM, ccd])
    GR = A.alloc([128, 8, 128], F32, "GR")
    GM = A.alloc([128, 4, 258], F32, "GM")
    XA = A.alloc([128, 128], F32, "XA")
    XAT = A.alloc([128, 128], BF16, "XAT")
    Sp = A.alloc([128, 64], F32, "Sp")
    Cp = A.alloc([128, 257], F32, "Cp")
    S.memset('dve', Cst_f, 0.0)
    S.memset('dve', XA, 0.0)
    for r in range(3):
        S.dma('sp', GR, gathR[r * 1024:(r + 1) * 1024, :].rearrange("(hp p) c -> p hp c", p=128), 'gr')
        S.dma('sp', GM, gathM[r * 512:(r + 1) * 512, :].rearrange("(h p) c -> p h c", p=128), 'gm')
        for hp in range(8):
            S.copy('dve', XA[0:64, 0:64], GR[0:64, hp, 64:128])
            S.copy('dve', XA[64:128, 64:128], GR[64:128, hp, 64:128])
            ptr = bank()
            S.tr(ptr[:, 0:128], XA, ident_f)
            S.copy('act', XAT, ptr[:, 0:128])
            pf = bank()
            S.mm(pf[:, 0:128], XAT, Sst_b[scur[hp]][:, hp, :])
            for h in range(2):
                hs = slice(h * 64, (h + 1) * 64)
                S.tt('dve', Sp[hs, :], pf[hs, h * 64:(h + 1) * 64], GR[hs, hp, 0:64], ALU.add)
            S.tt('dve', Sp, Sp, Sst_f[:, hp, :], ALU.subtract)
            S.stt(Sst_f[:, hp, :], Sp, cmask[:, r:r + 1], Sst_f[:, hp, :], ALU.mult, ALU.add)
            S.copy('act', Sst_b[scur[hp]][0:64, hp, 0:64], Sst_f[0:64, hp, :])
            S.copy('act', Sst_b[scur[hp]][64:128, hp, 64:128], Sst_f[64:128, hp, :])
        for hh in range(4):
            S.stt(Cp, Cst_f[:, hh, :], GM[:, hh, 257:258], GM[:, hh, 0:257], ALU.mult, ALU.add)
            S.tt('dve', Cp, Cp, Cst_f[:, hh, :], ALU.subtract)
            S.stt(Cst_f[:, hh, :], Cp, cmask[:, r:r + 1], Cst_f[:, hh, :], ALU.mult, ALU.add)
    S.copy('act', Cst_b, Cst_f)
    S.barrier()

    A.release(base_mark)
    ph_mark = A.mark()
    for seg in range(NSEG):
        seg_AB(seg, 'B')
    A.release(ph_mark)
    S.add('pool', lambda e: e.collective_compute("AllGather", ALU.bypass, replica_groups=[[0, 1, 2, 3], [4, 5, 6, 7]],
                                                 ins=[hin], outs=[hout]),
          r=[hin], w=[hout], dma=('cc', 'cc2'))
    ccd2 = A.alloc([128, 2], F32, "ccd2")
    S.add('pool', lambda e: e.memset(U(ccd2), 0.0), r=[hout], w=[hout, ccd2])
    HG = A.alloc([128, 4, 32], F32, "HG")
    hacc = A.alloc([128, 32], F32, "hacc")
    S.dma('sp', HG, hout.rearrange("(r p) c -> p r c", p=128), 'hg')
    S.ts('dve', hacc, HG[:, 0, :], csel[:, 0:1], None, ALU.mult)
    for r in range(1, 4):
        S.stt(hacc, HG[:, r, :], csel[:, r:r + 1], hacc, ALU.mult, ALU.add)
    S.copy('dve', h3halo.rearrange("p a t -> p (a t)"), hacc)
    S.barrier()

    for seg in range(NSEG):
        seg_C(seg)
    S.finish()
    S.emit()
    S.stack.close()
    print("arena peak words", A.peak, "ops", {e: len(v) for e, v in S.ops.items()})
    return nc


_CACHE = {}


def _consts(T):
    r = np.arange(128)
    ident = np.eye(128, dtype=np.float32)
    ones = np.ones((128, 128), np.float32)
    m_su = (r[None, :] > r[:, None]).astype(np.float32)
    m_iu = (r[None, :] >= r[:, None]).astype(np.float32)
    m_sl = (r[None, :] < r[:, None]).astype(np.float32)
    blk = ((r[None, :] // 64) == (r[:, None] // 64)).astype(np.float32)
    cst = np.concatenate([ident, ones, m_su, m_iu, m_sl, blk], axis=1)
    mreset = np.ones((128, T), np.float32)
    mreset[:, ::128] = 0.0
    return np.ascontiguousarray(cst), mreset


def colmaj(v, nch):
    return np.ascontiguousarray(np.asarray(v, np.float32).reshape(nch, 128).T)


def prepare_shared(inp, T):
    f = lambda k: np.ascontiguousarray(np.asarray(inp[k], np.float32)[0])
    cst, mreset = _consts(T)
    gnames = ["mix_pre_norm", "mix_post_norm", "xattn_pre_norm", "mem_norm", "xattn_post_norm", "ffn_pre_norm", "ffn_post_norm"]
    gcols = np.concatenate([colmaj(f(n), 16) for n in gnames], axis=1)
    bif = np.ascontiguousarray(np.broadcast_to(np.concatenate([f("mlstm_b_i"), f("mlstm_b_f")])[None, :], (128, 8)))
    hncols = colmaj(f("mlstm_head_norm"), 8)
    mu = f("rwkv_mu")
    rvec = np.concatenate([colmaj(mu[0:1024], 8), colmaj(mu[1024:2048], 8), colmaj(mu[2048:3072], 8),
                           colmaj(f("rwkv_w0"), 8), colmaj(f("rwkv_a0"), 8), colmaj(f("rwkv_k_k"), 8),
                           colmaj(f("rwkv_k_a"), 8), colmaj(f("rwkv_r_k").reshape(-1), 8),
                           np.zeros((128, 8), np.float32)], axis=1)
    mulr = np.zeros((128, 4), np.float32)
    mulr[0:64, 0] = mu[3072:3136]
    mulr[64:128, 0] = mu[3136:3200]
    mulr[:, 1] = mu[3200:3328]
    mulr[0:32, 2] = mu[3328:3360]
    lngb = np.ascontiguousarray(np.broadcast_to(np.concatenate([f("rwkv_ln_g"), f("rwkv_ln_b")])[None, :], (128, 2048)))
    cw = f("ffn_conv_w")
    convc = np.concatenate([colmaj(cw[0], 128), colmaj(cw[1], 128), colmaj(cw[2], 128), colmaj(f("ffn_conv_b"), 128)], axis=1)
    return {
        "w_in": f("w_in"), "w_ba": f("w_branch_a"), "w_bb": f("w_branch_b"), "w_mo": f("w_mix_out"),
        "w_q": f("xattn_wq"), "w_kv": f("xattn_wkv"), "w_o": f("xattn_wo"), "w_fu": f("ffn_w_up"),
        "w_fd": f("ffn_w_down"), "rw_up": f("rwkv_w_up"), "ra_up": f("rwkv_a_up"), "rg_up": f("rwkv_g_up"),
        "gcols": gcols, "bif": bif, "hncols": hncols, "rvec": np.ascontiguousarray(rvec), "mulr": mulr,
        "lngb": lngb, "convc": np.ascontiguousarray(convc), "cst": cst, "mreset": mreset,
    }


T_SEG = 512
NSEG_CORE = 2
TOK_CORE = T_SEG * NSEG_CORE


def run(inputs, debug_taps=()):
    x = np.asarray(inputs["x"], np.float32)
    mem = np.asarray(inputs["mem"], np.float32)
    B, SQ, _ = x.shape
    G = SQ // TOK_CORE
    assert B * G == 8 and G == 4
    nc = build_program(NSEG_CORE, T_SEG, debug_taps)
    shared = prepare_shared(inputs, T_SEG)
    in_maps = []
    for c in range(8):
        b, g = c // G, c % G
        m = dict(shared)
        m["xT"] = np.ascontiguousarray(x[b, g * TOK_CORE:(g + 1) * TOK_CORE].T)
        m["memT"] = np.ascontiguousarray(mem[b].T)
        prev = x[b, g * TOK_CORE - 1] if g > 0 else np.zeros((D,), np.float32)
        m["xh"] = colmaj(prev, 16)
        cm = np.zeros((128, 4), np.float32)
        cm[:, :g] = 1.0
        cs = np.zeros((128, 4), np.float32)
        if g > 0:
            cs[:, g - 1] = 1.0
        m["cmask"] = cm
        m["csel"] = cs
        in_maps.append(m)
    res = run_bass_kernel_spmd(nc, in_maps, core_ids=list(range(8)))
    return res


def kernel(**inputs):
    x = np.asarray(inputs["x"], np.float32)
    B, SQ, _ = x.shape
    G = SQ // TOK_CORE
    res = run(inputs)
    out = np.empty((B, SQ, D), np.float32)
    for c in range(8):
        b, g = c // G, c % G
        out[b, g * TOK_CORE:(g + 1) * TOK_CORE] = res.results[c]["outT"].T
    return out
```
